# Optimizing a Trainium2 kernel written in Bass

```python
import math
import jax
import jax.numpy as jnp
from jax import lax
import numpy as np

D_MODEL = 1024
BATCH = 16
SEQ = 4096
DEPTH = 2

GRID_W = 64
CTX_LEN = 256
HEAD_DIM = 64
ATTN_SCALE = HEAD_DIM ** -0.5
ROPE_BASE = 10000.0
Q_BLOCK = 128
NORM_EPS = 1e-6

SSD_HEADS = 8
SSD_HEAD_DIM = 64
SSD_INNER = SSD_HEADS * SSD_HEAD_DIM
SSD_GROUPS = 2
SSD_STATE = 128
SSD_CONV = 4
SSD_CHUNK = 128
SSD_XBC = SSD_INNER + 2 * SSD_GROUPS * SSD_STATE

GQA_HEADS = 8
GQA_KV_HEADS = 2
GQA_REP = GQA_HEADS // GQA_KV_HEADS
GQA_WIDTH = GQA_HEADS * HEAD_DIM
GQA_KV_WIDTH = GQA_KV_HEADS * HEAD_DIM

DIFF_HEADS = 4
DIFF_WIDTH = DIFF_HEADS * 2 * HEAD_DIM

RG_WIDTH = 512
RG_BLOCKS = 8
RG_BLOCK = RG_WIDTH // RG_BLOCKS
RG_CONV = 4
RG_C = 8.0

N_BRANCHES = 4
BRANCH_WIDTH = 512
D_FF = 2816
FFN_CONV = 3

IN_SPLITS = (SSD_INNER, SSD_XBC, 2 * SSD_HEADS,
             GQA_WIDTH, GQA_KV_WIDTH, GQA_KV_WIDTH,
             DIFF_WIDTH, DIFF_WIDTH, DIFF_WIDTH,
             RG_WIDTH, RG_WIDTH)
IN_COLS = 4880

kernel_name = 'hybrid_prefix_dit_ssd_gqa_diff_rglru'


def rmsnorm(x, g):
    xf = x.astype(jnp.float32)
    y = xf * lax.rsqrt(jnp.mean(xf * xf, axis=-1, keepdims=True) + NORM_EPS)
    return (y * g.astype(jnp.float32)).astype(x.dtype)


def modulate(h, shift, scale):
    return h * (1.0 + scale) + shift


def dwconv(x, w, b, left):
    k, ch = w.shape
    y = lax.conv_general_dilated(x, w[:, None, :].astype(x.dtype), (1,), [(left, k - 1 - left)],
                                 dimension_numbers=('NWC', 'WIO', 'NWC'), feature_group_count=ch)
    return y + b.astype(x.dtype)


def split_in(u):
    return jnp.split(u, np.cumsum(IN_SPLITS)[:-1].tolist(), axis=-1)


def axial_rope(row, col):
    n_freq = HEAD_DIM // 4
    inv = ROPE_BASE ** (-jnp.arange(n_freq, dtype=jnp.float32) / n_freq)
    ang = jnp.concatenate([row[:, None] * inv, col[:, None] * inv], axis=-1)
    return jnp.cos(ang), jnp.sin(ang)


def apply_rope(x, cos, sin):
    shape = (1, x.shape[1]) + (1,) * (x.ndim - 3) + (cos.shape[-1],)
    cs = cos.reshape(shape).astype(x.dtype)
    sn = sin.reshape(shape).astype(x.dtype)
    x1, x2 = jnp.split(x, 2, axis=-1)
    return jnp.concatenate([x1 * cs - x2 * sn, x1 * sn + x2 * cs], axis=-1)


def softmax32(s):
    return jax.nn.softmax(s.astype(jnp.float32), axis=-1)


def sweep_query_blocks(fn, q):
    bsz, n = q.shape[:2]
    qb = jnp.moveaxis(q.reshape((bsz, n // Q_BLOCK, Q_BLOCK) + q.shape[2:]), 1, 0)
    ob = jnp.moveaxis(lax.map(fn, qb), 0, 1)
    return ob.reshape((bsz, n) + ob.shape[3:])


def gqa_core(q, k, v):
    s = jnp.einsum('bqgrd,bkgd->bgrqk', q, k) * ATTN_SCALE
    p = softmax32(s).astype(v.dtype)
    return jnp.einsum('bgrqk,bkgd->bqgrd', p, v)


def diff_core(q, k, v, lam):
    s = jnp.einsum('bqhcd,bkhcd->bhcqk', q, k) * ATTN_SCALE
    p = softmax32(s)
    a = (p[:, :, 0] - lam * p[:, :, 1]).astype(v.dtype)
    return jnp.einsum('bhqk,bkhd->bqhd', a, v)


def segsum(a):
    t = a.shape[-1]
    cs = jnp.cumsum(a, axis=-1)
    diff = cs[..., :, None] - cs[..., None, :]
    return jnp.where(jnp.tril(jnp.ones((t, t), dtype=bool)), diff, -jnp.inf)


def ssd_chunked(x, a, bm, cm, h0):
    bsz, n, nh, hp = x.shape
    nc = n // SSD_CHUNK
    x = x.reshape(bsz, nc, SSD_CHUNK, nh, hp)
    bm = bm.reshape(bsz, nc, SSD_CHUNK, nh, -1)
    cm = cm.reshape(bsz, nc, SSD_CHUNK, nh, -1)
    a = a.reshape(bsz, nc, SSD_CHUNK, nh).transpose(0, 3, 1, 2)
    a_cs = jnp.cumsum(a, axis=-1)
    decay_in = jnp.exp(segsum(a))
    scores = jnp.einsum('bclhn,bcshn->bhcls', cm, bm) * decay_in
    y_diag = jnp.einsum('bhcls,bcshp->bclhp', scores, x)
    decay_states = jnp.exp(a_cs[..., -1:] - a_cs)
    states = jnp.einsum('bclhn,bhcl,bclhp->bchpn', bm, decay_states, x)
    states = jnp.concatenate([h0[:, None].astype(states.dtype), states], axis=1)
    chunk_decay = jnp.exp(segsum(jnp.pad(a_cs[..., -1], ((0, 0), (0, 0), (1, 0)))))
    new_states = jnp.einsum('bhzc,bchpn->bzhpn', chunk_decay, states)
    states, final = new_states[:, :-1], new_states[:, -1]
    y_off = jnp.einsum('bclhn,bchpn,bhcl->bclhp', cm, states, jnp.exp(a_cs))
    return (y_diag + y_off).reshape(bsz, n, nh, hp), final


def ssd_sequence(z, xbc, dt_raw, p, h0):
    bsz, n, _ = xbc.shape
    xbc = jax.nn.silu(dwconv(xbc, p['ssd_conv_w'], p['ssd_conv_b'], (SSD_CONV - 1) // 2))
    xs, bm, cm = jnp.split(xbc, [SSD_INNER, SSD_INNER + SSD_GROUPS * SSD_STATE], axis=-1)
    xs = xs.reshape(bsz, n, SSD_HEADS, SSD_HEAD_DIM)
    rep = SSD_HEADS // SSD_GROUPS
    bm = jnp.repeat(bm.reshape(bsz, n, SSD_GROUPS, SSD_STATE), rep, axis=2)
    cm = jnp.repeat(cm.reshape(bsz, n, SSD_GROUPS, SSD_STATE), rep, axis=2)
    dt = jax.nn.softplus(dt_raw.astype(jnp.float32).reshape(bsz, n, 2, SSD_HEADS)
                         + p['ssd_dt_bias'].astype(jnp.float32))
    da = dt * (-jnp.exp(p['ssd_a_log'].astype(jnp.float32)))
    xdt = xs[:, :, None] * dt[..., None].astype(xs.dtype)
    y = xs * p['ssd_d'][:, None].astype(xs.dtype)
    finals = []
    for d in range(2):
        init = jnp.zeros((bsz, SSD_HEADS, SSD_HEAD_DIM, SSD_STATE), jnp.float32) if h0 is None else h0[d]
        xd, ad, bd, cd = xdt[:, :, d], da[:, :, d], bm, cm
        if d == 1:
            xd, ad, bd, cd = xd[:, ::-1], ad[:, ::-1], bd[:, ::-1], cd[:, ::-1]
        yd, fd = ssd_chunked(xd, ad, bd, cd, init)
        if d == 1:
            yd = yd[:, ::-1]
        y = y + yd
        finals.append(fd)
    y = y.reshape(bsz, n, SSD_INNER).astype(z.dtype) * jax.nn.silu(z)
    y = rmsnorm(y.reshape(bsz, n, SSD_GROUPS, SSD_INNER // SSD_GROUPS),
                p['ssd_norm_g'].reshape(SSD_GROUPS, -1)).reshape(bsz, n, SSD_INNER)
    return y, finals[0], finals[1]


def linear_scan(a, b, h0):
    b = b.at[:, 0].add(a[:, 0] * h0)

    def combine(left, right):
        a_l, b_l = left
        a_r, b_r = right
        return a_l * a_r, a_r * b_l + b_r

    _, h = lax.associative_scan(combine, (a, b), axis=1)
    return h


def rglru_sequence(x_in, p, h0):
    bsz, n, _ = x_in.shape
    xr = dwconv(x_in, p['rg_conv_w'], p['rg_conv_b'], (RG_CONV - 1) // 2)
    xb = xr.reshape(bsz, n, RG_BLOCKS, RG_BLOCK)
    h_sum = 0.0
    finals = []
    for d in range(2):
        gate_a = jnp.einsum('blkc,kce->blke', xb, p['rg_wa'][d]).reshape(bsz, n, RG_WIDTH) + p['rg_ba'][d]
        gate_x = jnp.einsum('blkc,kce->blke', xb, p['rg_wx'][d]).reshape(bsz, n, RG_WIDTH) + p['rg_bx'][d]
        log_a = -RG_C * jax.nn.sigmoid(gate_a.astype(jnp.float32)) * jax.nn.softplus(-p['rg_lambda'][d].astype(jnp.float32))
        a = jnp.exp(log_a)
        b = jnp.sqrt(-jnp.expm1(2.0 * log_a)) * (jax.nn.sigmoid(gate_x.astype(jnp.float32)) * xr.astype(jnp.float32))
        init = jnp.zeros((bsz, RG_WIDTH), jnp.float32) if h0 is None else h0[d]
        if d == 1:
            a, b = a[:, ::-1], b[:, ::-1]
        h = linear_scan(a, b, init)
        finals.append(h[:, -1])
        if d == 1:
            h = h[:, ::-1]
        h_sum = h_sum + h
    return h_sum.astype(x_in.dtype), finals[0], finals[1]


def gqa_branch(q_l, k_l, v_l, q_c, k_c, v_c, p, cos, sin, with_ctx_out):
    bsz, n, _ = q_l.shape
    m = q_c.shape[1]

    def heads(t, nh):
        return t.reshape(t.shape[0], t.shape[1], nh, HEAD_DIM)

    kc = rmsnorm(heads(k_c, GQA_KV_HEADS), p['gqa_knorm_g'])
    vc = heads(v_c, GQA_KV_HEADS)
    ql = apply_rope(rmsnorm(heads(q_l, GQA_HEADS), p['gqa_qnorm_g']), cos, sin)
    kl = apply_rope(rmsnorm(heads(k_l, GQA_KV_HEADS), p['gqa_knorm_g']), cos, sin)
    k_all = jnp.concatenate([kc, kl], axis=1)
    v_all = jnp.concatenate([vc, heads(v_l, GQA_KV_HEADS)], axis=1)
    ql = ql.reshape(bsz, n, GQA_KV_HEADS, GQA_REP, HEAD_DIM)
    o_l = sweep_query_blocks(lambda qb: gqa_core(qb, k_all, v_all), ql).reshape(bsz, n, GQA_WIDTH)
    o_c = None
    if with_ctx_out:
        qc = rmsnorm(heads(q_c, GQA_HEADS), p['gqa_qnorm_g']).reshape(bsz, m, GQA_KV_HEADS, GQA_REP, HEAD_DIM)
        o_c = gqa_core(qc, kc, vc).reshape(bsz, m, GQA_WIDTH)
    return o_l, o_c


def diff_branch(q_l, k_l, v_l, q_c, k_c, v_c, p, cos, sin, lambda_init, with_ctx_out):
    bsz, n, _ = q_l.shape
    m = q_c.shape[1]
    lp = p['diff_lambda'].astype(jnp.float32)
    lam = jnp.exp(jnp.sum(lp[0] * lp[1])) - jnp.exp(jnp.sum(lp[2] * lp[3])) + lambda_init

    def qk_heads(t):
        return t.reshape(t.shape[0], t.shape[1], DIFF_HEADS, 2, HEAD_DIM)

    def v_heads(t):
        return t.reshape(t.shape[0], t.shape[1], DIFF_HEADS, 2 * HEAD_DIM)

    def finish(o):
        o = rmsnorm(o, p['diff_subln_g']) * (1.0 - lambda_init)
        return o.reshape(o.shape[0], o.shape[1], DIFF_WIDTH)

    kc = rmsnorm(qk_heads(k_c), p['diff_knorm_g'])
    vc = v_heads(v_c)
    ql = apply_rope(rmsnorm(qk_heads(q_l), p['diff_qnorm_g']), cos, sin)
    kl = apply_rope(rmsnorm(qk_heads(k_l), p['diff_knorm_g']), cos, sin)
    k_all = jnp.concatenate([kc, kl], axis=1)
    v_all = jnp.concatenate([vc, v_heads(v_l)], axis=1)
    o_l = finish(sweep_query_blocks(lambda qb: diff_core(qb, k_all, v_all, lam), ql))
    o_c = None
    if with_ctx_out:
        qc = rmsnorm(qk_heads(q_c), p['diff_qnorm_g'])
        o_c = finish(diff_core(qc, kc, vc, lam))
    return o_l, o_c


def merge_branches(h, outs, p):
    m = 0.0
    for k, o in enumerate(outs):
        gate = jax.nn.sigmoid(h @ p['w_gate'][k] + p['b_gate'][k])
        m = m + gate * (o @ p['w_br'][k])
    return m @ p['w_out']


def token_mixer(h_l, h_c, p, cos, sin, lambda_init, with_ctx_out):
    z_l, xbc_l, dt_l, gq_l, gk_l, gv_l, dq_l, dk_l, dv_l, rgg_l, rgx_l = split_in(h_l @ p['w_in'])
    z_c, xbc_c, dt_c, gq_c, gk_c, gv_c, dq_c, dk_c, dv_c, rgg_c, rgx_c = split_in(h_c @ p['w_in'])
    ssd_c, sf, sb = ssd_sequence(z_c, xbc_c, dt_c, p, None)
    ssd_l, _, _ = ssd_sequence(z_l, xbc_l, dt_l, p, (sf, sb))
    gqa_l, gqa_c = gqa_branch(gq_l, gk_l, gv_l, gq_c, gk_c, gv_c, p, cos, sin, with_ctx_out)
    diff_l, diff_c = diff_branch(dq_l, dk_l, dv_l, dq_c, dk_c, dv_c, p, cos, sin, lambda_init, with_ctx_out)
    rg_c, rf, rb = rglru_sequence(rgx_c, p, None)
    rg_l, _, _ = rglru_sequence(rgx_l, p, (rf, rb))
    y_l = merge_branches(h_l, (ssd_l, gqa_l, diff_l, rg_l * jax.nn.gelu(rgg_l)), p)
    y_c = None
    if with_ctx_out:
        y_c = merge_branches(h_c, (ssd_c, gqa_c, diff_c, rg_c * jax.nn.gelu(rgg_c)), p)
    return y_l, y_c


def conv_ffn(h, p):
    u = dwconv(h @ p['w_up'], p['ffn_conv_w'], p['ffn_conv_b'], (FFN_CONV - 1) // 2)
    g, v = jnp.split(u, 2, axis=-1)
    return (jax.nn.silu(g) * v) @ p['w_down']


def layer(x_l, x_c, c, c_ctx, p, cos, sin, lambda_init, with_ctx_out):
    mod_l = (jax.nn.silu(c) @ p['w_ada'] + p['b_ada'])[:, None, :]
    mod_c = (jax.nn.silu(c_ctx) @ p['w_ada'] + p['b_ada'])[None, None, :]
    sh1_l, sc1_l, g1_l, sh2_l, sc2_l, g2_l = jnp.split(mod_l, 6, axis=-1)
    sh1_c, sc1_c, g1_c, sh2_c, sc2_c, g2_c = jnp.split(mod_c, 6, axis=-1)
    h_l = modulate(rmsnorm(x_l, p['norm1_g']), sh1_l, sc1_l)
    h_c = modulate(rmsnorm(x_c, p['norm1_g']), sh1_c, sc1_c)
    y_l, y_c = token_mixer(h_l, h_c, p, cos, sin, lambda_init, with_ctx_out)
    x_l = x_l + g1_l * y_l
    x_l = x_l + g2_l * conv_ffn(modulate(rmsnorm(x_l, p['norm2_g']), sh2_l, sc2_l), p)
    if not with_ctx_out:
        return x_l, None
    x_c = x_c + g1_c * y_c
    x_c = x_c + g2_c * conv_ffn(modulate(rmsnorm(x_c, p['norm2_g']), sh2_c, sc2_c), p)
    return x_l, x_c


def setup_inputs(seed: int = 0) -> dict:
    key = jax.random.key(seed)
    ks = iter(jax.random.split(key, 48))
    f32 = jnp.float32
    L, D = DEPTH, D_MODEL

    def nrm(shape, scale):
        return jax.random.normal(next(ks), shape, f32) * scale

    def gain(shape):
        return 1.0 + nrm(shape, 0.05)

    x = nrm((BATCH, SEQ, D), 1.0)
    c = nrm((BATCH, D), 1.0)
    ctx = nrm((BATCH, CTX_LEN, D), 1.0)
    c_ctx = nrm((D,), 1.0)
    w_ada = nrm((L, D, 6 * D), 0.3 * D ** -0.5)
    b_ada = nrm((L, 6 * D), 0.02)
    norm1_g = gain((L, D))
    norm2_g = gain((L, D))
    w_in = nrm((L, D, IN_COLS), D ** -0.5)
    ssd_conv_w = nrm((L, SSD_CONV, SSD_XBC), SSD_CONV ** -0.5)
    ssd_conv_b = nrm((L, SSD_XBC), 0.02)
    dt0 = jnp.exp(jax.random.uniform(next(ks), (L, 2, SSD_HEADS), f32, math.log(1e-3), math.log(1e-1)))
    ssd_dt_bias = dt0 + jnp.log(-jnp.expm1(-dt0))
    ssd_a_log = jnp.log(jax.random.uniform(next(ks), (L, 2, SSD_HEADS), f32, 1.0, 16.0))
    ssd_d = gain((L, SSD_HEADS))
    ssd_norm_g = gain((L, SSD_INNER))
    gqa_qnorm_g = gain((L, HEAD_DIM))
    gqa_knorm_g = gain((L, HEAD_DIM))
    diff_qnorm_g = gain((L, HEAD_DIM))
    diff_knorm_g = gain((L, HEAD_DIM))
    diff_lambda = nrm((L, 4, HEAD_DIM), 0.1)
    diff_subln_g = gain((L, 2 * HEAD_DIM))
    rg_conv_w = nrm((L, RG_CONV, RG_WIDTH), RG_CONV ** -0.5)
    rg_conv_b = nrm((L, RG_WIDTH), 0.02)
    rg_wa = nrm((L, 2, RG_BLOCKS, RG_BLOCK, RG_BLOCK), RG_BLOCK ** -0.5)
    rg_ba = nrm((L, 2, RG_WIDTH), 0.02)
    rg_wx = nrm((L, 2, RG_BLOCKS, RG_BLOCK, RG_BLOCK), RG_BLOCK ** -0.5)
    rg_bx = nrm((L, 2, RG_WIDTH), 0.02)
    a0 = jax.random.uniform(next(ks), (L, 2, RG_WIDTH), f32, 0.9, 0.999)
    s0 = a0 ** (1.0 / RG_C)
    rg_lambda = jnp.log(s0) - jnp.log1p(-s0)
    w_gate = nrm((L, N_BRANCHES, D, D), D ** -0.5)
    b_gate = nrm((L, N_BRANCHES, D), 0.02)
    w_br = nrm((L, N_BRANCHES, BRANCH_WIDTH, D), BRANCH_WIDTH ** -0.5)
    w_out = nrm((L, D, D), D ** -0.5)
    w_up = nrm((L, D, 2 * D_FF), D ** -0.5)
    ffn_conv_w = nrm((L, FFN_CONV, 2 * D_FF), FFN_CONV ** -0.5)
    ffn_conv_b = nrm((L, 2 * D_FF), 0.02)
    w_down = nrm((L, D_FF, D), D_FF ** -0.5)
    return {'x': x, 'c': c, 'ctx': ctx, 'c_ctx': c_ctx,
            'w_ada': w_ada, 'b_ada': b_ada, 'norm1_g': norm1_g, 'norm2_g': norm2_g, 'w_in': w_in,
            'ssd_conv_w': ssd_conv_w, 'ssd_conv_b': ssd_conv_b, 'ssd_dt_bias': ssd_dt_bias,
            'ssd_a_log': ssd_a_log, 'ssd_d': ssd_d, 'ssd_norm_g': ssd_norm_g,
            'gqa_qnorm_g': gqa_qnorm_g, 'gqa_knorm_g': gqa_knorm_g,
            'diff_qnorm_g': diff_qnorm_g, 'diff_knorm_g': diff_knorm_g,
            'diff_lambda': diff_lambda, 'diff_subln_g': diff_subln_g,
            'rg_conv_w': rg_conv_w, 'rg_conv_b': rg_conv_b, 'rg_wa': rg_wa, 'rg_ba': rg_ba,
            'rg_wx': rg_wx, 'rg_bx': rg_bx, 'rg_lambda': rg_lambda,
            'w_gate': w_gate, 'b_gate': b_gate, 'w_br': w_br, 'w_out': w_out,
            'w_up': w_up, 'ffn_conv_w': ffn_conv_w, 'ffn_conv_b': ffn_conv_b, 'w_down': w_down}


def reference(x, c, ctx, c_ctx, w_ada, b_ada, norm1_g, norm2_g, w_in,
              ssd_conv_w, ssd_conv_b, ssd_dt_bias, ssd_a_log, ssd_d, ssd_norm_g,
              gqa_qnorm_g, gqa_knorm_g, diff_qnorm_g, diff_knorm_g, diff_lambda, diff_subln_g,
              rg_conv_w, rg_conv_b, rg_wa, rg_ba, rg_wx, rg_bx, rg_lambda,
              w_gate, b_gate, w_br, w_out, w_up, ffn_conv_w, ffn_conv_b, w_down):
    n_lat = x.shape[1]
    rows = n_lat // GRID_W
    row = jnp.repeat(jnp.arange(rows, dtype=jnp.float32), GRID_W)
    col = jnp.tile(jnp.arange(GRID_W, dtype=jnp.float32), rows)
    cos, sin = axial_rope(row, col)
    x_l, x_c = x, ctx
    for l in range(DEPTH):
        p = {'w_ada': w_ada[l], 'b_ada': b_ada[l], 'norm1_g': norm1_g[l], 'norm2_g': norm2_g[l],
             'w_in': w_in[l], 'ssd_conv_w': ssd_conv_w[l], 'ssd_conv_b': ssd_conv_b[l],
             'ssd_dt_bias': ssd_dt_bias[l], 'ssd_a_log': ssd_a_log[l], 'ssd_d': ssd_d[l],
             'ssd_norm_g': ssd_norm_g[l], 'gqa_qnorm_g': gqa_qnorm_g[l], 'gqa_knorm_g': gqa_knorm_g[l],
             'diff_qnorm_g': diff_qnorm_g[l], 'diff_knorm_g': diff_knorm_g[l],
             'diff_lambda': diff_lambda[l], 'diff_subln_g': diff_subln_g[l],
             'rg_conv_w': rg_conv_w[l], 'rg_conv_b': rg_conv_b[l], 'rg_wa': rg_wa[l], 'rg_ba': rg_ba[l],
             'rg_wx': rg_wx[l], 'rg_bx': rg_bx[l], 'rg_lambda': rg_lambda[l],
             'w_gate': w_gate[l], 'b_gate': b_gate[l], 'w_br': w_br[l], 'w_out': w_out[l],
             'w_up': w_up[l], 'ffn_conv_w': ffn_conv_w[l], 'ffn_conv_b': ffn_conv_b[l], 'w_down': w_down[l]}
        lambda_init = 0.8 - 0.6 * math.exp(-0.3 * l)
        x_l, x_c = layer(x_l, x_c, c, c_ctx, p, cos, sin, lambda_init, l < DEPTH - 1)
    return x_l
```

```python
import math
from contextlib import ExitStack
import numpy as np
import concourse.bass as bass
import concourse.mybir as mybir
from concourse.bass_utils import run_bass_kernel_spmd

F32 = mybir.dt.float32
BF16 = mybir.dt.bfloat16
AF = mybir.ActivationFunctionType
ALU = mybir.AluOpType
AX = mybir.AxisListType

SEM_EPOCH = 30000
N_DMA_SEMS = 56
EPS = 1e-6
D = 1024
INC = 4880
DFF = 2816
TZ, TDT, TGQ, TGK, TGV, TDQ, TDK, TDV, TOKC = 0, 512, 528, 1040, 1168, 1296, 1808, 2320, 2832
PF_BADA, PF_N1, PF_N2, PF_SCW, PF_SCB, PF_SUB, PF_RCW, PF_RCB, PF_RBA, PF_RBX, PF_RLM, PF_BG, PF_FCW, PF_FCB, NPF = \
    0, 48, 56, 64, 96, 104, 105, 121, 125, 133, 141, 149, 181, 313, 357
PB_DTB, PB_ALOG, PB_D, PB_SNG, PB_GQ, PB_GK, PB_DQ, PB_DK, PB_LAM, PB_LI, NPB = 0, 16, 32, 40, 552, 616, 680, 744, 808, 1064, 1065


class Rec:
    def __init__(self, target):
        self._t = target

    def __getattr__(self, name):
        f = getattr(self._t, name)
        return lambda *a, **k: (f, a, k)


class Prog:
    def __init__(self, nc):
        self.nc = nc
        self.eng = {'pe': nc.tensor, 'act': nc.scalar, 'dve': nc.vector, 'pool': nc.gpsimd, 'sp': nc.sync}
        self.ops = []
        self.pending_dma_w = set()
        self.nbar = 0
        self.dummy = None

    def op(self, engine, fn, reads=(), writes=(), dma=False):
        rec = fn()
        assert isinstance(rec, tuple) and len(rec) == 3
        self.ops.append((engine, rec, tuple(reads), tuple(writes), dma))
        if dma:
            self.pending_dma_w.update(writes)

    def barrier(self):
        b = self.nbar
        self.nbar += 1
        nc = self.nc
        pend = list(self.pending_dma_w)
        self.pending_dma_w = set()
        engs = ['pe', 'act', 'dve', 'pool', 'sp']
        for e in engs:
            rd = pend if e == 'sp' else []
            if e == 'sp' or self.dummy is None:
                self.op(e, (lambda e=e: (self.eng[e].nop, (), {})), reads=rd, writes=[('bar', b, e)])
            else:
                fn, r2, w2 = self.dummy[e]
                self.op(e, fn, reads=r2, writes=[('bar', b, e)] + w2)
        for e in engs:
            self.op(e, (lambda e=e: (self.eng[e].nop, (), {})), reads=[('bar', b, f) for f in engs if f != e],
                    writes=[('bar2', b, e)])

    def emit(self, sem_ctx):
        ops = self.ops
        n = len(ops)
        last_w = {}
        rd_eng = {}
        rd_dma = {}
        deps = [None] * n
        needs_inc = [False] * n
        for i, (e, fn, rd, wr, dma) in enumerate(ops):
            d = set()
            for k in rd:
                w = last_w.get(k)
                if w is not None:
                    d.add(w)
            for k in wr:
                w = last_w.get(k)
                if w is not None:
                    d.add(w)
                re_ = rd_eng.get(k)
                if re_:
                    d.update(re_.values())
                rdm = rd_dma.get(k)
                if rdm:
                    d.update(rdm)
            d.discard(i)
            dd = []
            for j in d:
                ej, _, _, _, dmaj = ops[j]
                if ej == e and (not dmaj) and (not dma) and e == 'pe':
                    continue
                dd.append(j)
                needs_inc[j] = True
            deps[i] = dd
            for k in rd:
                if dma:
                    rd_dma.setdefault(k, []).append(i)
                else:
                    rd_eng.setdefault(k, {})[e] = i
            for k in wr:
                last_w[k] = i
                rd_eng[k] = {}
                rd_dma[k] = []
        cnt = {e: 0 for e in self.eng}
        tl = [None] * n
        dma_cnt = [0] * N_DMA_SEMS
        dma_rr = 0
        for i, (e, fn, rd, wr, dma) in enumerate(ops):
            if dma:
                s = dma_rr % N_DMA_SEMS
                dma_rr += 1
                dma_cnt[s] += 16
                tl[i] = ('dma', s, dma_cnt[s])
            elif needs_inc[i]:
                cnt[e] += 1
                tl[i] = ('eng', e, cnt[e])
        sems = {}
        for e in self.eng:
            for ep in range(cnt[e] // SEM_EPOCH + 1):
                sems[(e, ep)] = sem_ctx(f"s_{e}_{ep}")
        dsems = [sem_ctx(f"s_dma_{s}") for s in range(N_DMA_SEMS)]
        seen = {e: {f: 0 for f in self.eng} for e in self.eng}
        seen_dma = {e: [0] * N_DMA_SEMS for e in self.eng}
        for i, (e, fn, rd, wr, dma) in enumerate(ops):
            eng = self.eng[e]
            need_eng = {}
            need_dma = {}
            for j in deps[i]:
                t = tl[j]
                if t[0] == 'eng':
                    _, f, c = t
                    if c > seen[e][f] and c > need_eng.get(f, 0):
                        need_eng[f] = c
                else:
                    _, s, v = t
                    if v > seen_dma[e][s] and v > need_dma.get(s, 0):
                        need_dma[s] = v
            for f, c in need_eng.items():
                ep = (c - 1) // SEM_EPOCH
                eng.wait_ge(sems[(f, ep)], c - ep * SEM_EPOCH)
                seen[e][f] = c
            for s, v in need_dma.items():
                eng.wait_ge(dsems[s], v)
                seen_dma[e][s] = v
            inst = fn[0](*fn[1], **fn[2])
            t = tl[i]
            if t is not None:
                if t[0] == 'dma':
                    inst.then_inc(dsems[t[1]], 16)
                else:
                    c = t[2]
                    inst.then_inc(sems[(e, (c - 1) // SEM_EPOCH)], 1)
        return dict(n_ops=n, counts=cnt)


class Arena:
    def __init__(self, ap_f32, ncols):
        self.ap = ap_f32
        self.n = ncols
        self.top = 0

    def mark(self):
        return self.top

    def release(self, m):
        self.top = m

    def alloc(self, shape, dt):
        nel = int(np.prod(shape))
        cols = (nel * (2 if dt == BF16 else 4) + 3) // 4
        cols = (cols + 7) // 8 * 8
        assert self.top + cols <= self.n, f"arena overflow {self.top}+{cols}>{self.n}"
        v = self.ap[:, self.top:self.top + cols]
        self.top += cols
        if dt == BF16:
            v = v.bitcast(BF16)
        v = v[:, 0:nel]
        if len(shape) == 2:
            v = v.rearrange("p (a b) -> p a b", a=shape[0])
        elif len(shape) == 3:
            v = v.rearrange("p (a b c) -> p a b c", a=shape[0], b=shape[1])
        elif len(shape) == 4:
            v = v.rearrange("p (a b c d) -> p a b c d", a=shape[0], b=shape[1], c=shape[2])
        return v


class StopBuild(Exception):
    pass


def build_program(NB, DEPTH, NCTX, NLAT, debug=False, stop=None):
    NTOK = NCTX + NLAT
    NSUB = NTOK // 128
    NSC = NCTX // 128
    NLS = NLAT // 128
    R = NB + 1
    nc = bass.Bass("TRN2", target_bir_lowering=False)
    dram = lambda name, shape, dt, kind: nc.dram_tensor(name, shape, dt, kind=kind).ap()
    skind = "ExternalOutput" if debug else "Internal"
    xin = dram("xin", [NB, D, NTOK], F32, "ExternalInput")
    cT = dram("cT", [128, 8, R], F32, "ExternalInput")
    ropec = dram("ropec", [128, NLS, 32], F32, "ExternalInput")
    ropes = dram("ropes", [128, NLS, 32], F32, "ExternalInput")
    cmask = dram("cmask", [128, 6, 128], F32, "ExternalInput")
    pfd = dram("pf", [DEPTH, 128, NPF], F32, "ExternalInput")
    pbd = dram("pb", [DEPTH, NPB], F32, "ExternalInput")
    rgwd = dram("rgw", [DEPTH, 128, 16, 128], F32, "ExternalInput")
    w_ada = dram("w_ada", [DEPTH, D, 6 * D], F32, "ExternalInput")
    w_in = dram("w_in", [DEPTH, D, INC], F32, "ExternalInput")
    w_gate = dram("w_gate", [DEPTH, 4, D, D], F32, "ExternalInput")
    w_br = dram("w_br", [DEPTH, 4, 512, D], F32, "ExternalInput")
    w_out = dram("w_out", [DEPTH, D, D], F32, "ExternalInput")
    w_up = dram("w_up", [DEPTH, D, 2 * DFF], F32, "ExternalInput")
    w_down = dram("w_down", [DEPTH, DFF, D], F32, "ExternalInput")
    outd = dram("out", [NB, D, NLAT], F32, "ExternalOutput")
    xs = [dram(f"xs{s}", [D, NTOK], F32, skind) for s in range(NB)]
    xm = dram("xm", [D, NTOK], F32, skind)
    u_tok = dram("u_tok", [NTOK, TOKC], BF16, skind)
    dt_tok = dram("dt_tok", [NTOK, 16], F32, skind)
    u_fm = dram("u_fm", [2048, NTOK], BF16, skind)
    brd = [dram(f"br{k}", [512, NTOK], BF16, skind) for k in range(4)]

    st = ExitStack()
    with st:
        ARENA_COLS = 52992
        arena_t = st.enter_context(nc.sbuf_tensor("arena", [128, ARENA_COLS], F32))
        psum_ts = [st.enter_context(nc.psum_tensor(f"psum{b}", [128, 512], F32)) for b in range(8)]
        A = Arena(arena_t, ARENA_COLS)
        PB = [psum_ts[b][:, :] for b in range(8)]
        PBH = [psum_ts[b][:, :].bitcast(BF16) for b in range(8)]
        P = Prog(nc)
        V, S_, G, T, SY = Rec(nc.vector), Rec(nc.scalar), Rec(nc.gpsimd), Rec(nc.tensor), Rec(nc.sync)

        def dve(fn, r, w):
            P.op('dve', fn, r, w)

        def act(fn, r, w):
            P.op('act', fn, r, w)

        def pool(fn, r, w):
            P.op('pool', fn, r, w)

        def pe(fn, r, w):
            P.op('pe', fn, r, w)

        def ld(out, in_, r, w):
            P.op('sp', lambda: SY.dma_start(out=out, in_=in_), r, w, dma=True)

        def stq(out, in_, r, w):
            P.op('sp', lambda: SY.dma_start(out=out, in_=in_), r, w, dma=True)

        def mm(out, lhsT, rhs, start, stop, r, w, skip=False):
            pe(lambda: T.matmul(out, lhsT=lhsT, rhs=rhs, start=start, stop=stop, skip_group_check=skip), r, w)

        cm32 = A.alloc([6, 128], F32)
        ld(cm32, cmask, [], ['cm32'])
        LI, LS, UI, US, ONES, IDN = range(6)
        ident = A.alloc([128], BF16)
        ones_bf = A.alloc([128], BF16)
        dve(lambda: V.tensor_copy(out=ident, in_=cm32[:, IDN, :]), ['cm32'], ['ident'])
        dve(lambda: V.tensor_copy(out=ones_bf, in_=cm32[:, ONES, :]), ['cm32'], ['ones_bf'])
        rc = A.alloc([NLS, 32], F32)
        rs = A.alloc([NLS, 32], F32)
        ld(rc, ropec, [], ['rc'])
        ld(rs, ropes, [], ['rs'])
        scT = A.alloc([8, R], F32)
        ld(scT, cT, [], ['scT'])
        act(lambda: S_.activation(out=scT, in_=scT, func=AF.Silu), ['scT'], ['scT'])
        pf = A.alloc([NPF], F32)
        pbc = A.alloc([NPB], F32)
        modt = A.alloc([48, R], F32)
        A1 = A.alloc([8, R], F32)
        A2 = A.alloc([8, R], F32)
        aneg = A.alloc([16], F32)
        rgcp = A.alloc([8], F32)
        lamt = A.alloc([8], F32)
        gqk = A.alloc([10, 64], F32)
        gdk = A.alloc([16, 64], F32)
        dum = A.alloc([8], F32)
        dve(lambda: V.memset(dum, 0.0), [], [('dum', 'act'), ('dum', 'dve'), ('dum', 'pool')])
        P.dummy = {
            'pe': (lambda: T.matmul(PB[7][0:1, 0:2], lhsT=ones_bf[0:1, 0:1], rhs=ones_bf[0:1, 0:2], start=True, stop=True),
                   ['ones_bf'], [('ps', 7)]),
            'act': (lambda: S_.copy(out=dum[:, 0:1], in_=dum[:, 1:2]), [], [('dum', 'act')]),
            'dve': (lambda: V.tensor_copy(out=dum[:, 2:3], in_=dum[:, 3:4]), [], [('dum', 'dve')]),
            'pool': (lambda: G.tensor_copy(out=dum[:, 4:5], in_=dum[:, 5:6]), [], [('dum', 'pool')]),
        }
        PERSIST = A.mark()

        pbank = [0]

        def chk(name):
            if stop == name:
                raise StopBuild()

        def nb(lo=0, hi=8):
            b = lo + (pbank[0] % (hi - lo))
            pbank[0] += 1
            return b

        def tiles_of(include_ctx=True, w=512):
            t = [(NCTX * i // (-(-NCTX // w)), NCTX // (-(-NCTX // w))) for i in range(-(-NCTX // w))] if include_ctx else []
            t += [(NCTX + w * i, w) for i in range(NLAT // w)]
            return t

        def load_weight(dst_views, src_views, stage, tag):
            for i, (dv, sv) in enumerate(zip(dst_views, src_views)):
                sg = stage[i % len(stage)]
                sk = (tag + '_stg', i % len(stage))
                shp = dv.shape
                sgv = sg
                if len(shp) == 2:
                    sgv = sg[:, 0:shp[1]]
                else:
                    sgv = sg[:, 0:shp[1] * shp[2]].rearrange("p (a b) -> p a b", a=shp[1])
                ld(sgv, sv, [], [sk])
                pool(lambda dv=dv, sgv=sgv: G.tensor_copy(out=dv, in_=sgv), [sk], [(tag, i)])

        def norm_mod(xt, xk, W, sq, rstd, h, hk, Am, Bm, r, sfx):
            act(lambda: S_.activation(out=sq[:, :, 0:W], in_=xt[:, :, 0:W], func=AF.Square), [xk], ['sq' + sfx])
            b = nb(6, 8)
            for kc in range(8):
                mm(PB[b][:, 0:W], ones_bf, sq[:, kc, 0:W], kc == 0, kc == 7, ['sq' + sfx, 'ones_bf'], [('ps', b)])
            act(lambda: S_.activation(out=rstd[:, 0:W], in_=PB[b][:, 0:W], func=AF.Sqrt, scale=1.0 / D, bias=EPS),
                [('ps', b)], ['rstd' + sfx])
            dve(lambda: V.reciprocal(out=rstd[:, 0:W], in_=rstd[:, 0:W]), ['rstd' + sfx], ['rstd' + sfx])
            for kc in range(8):
                tk = ('tmpn', kc % 2)
                tv = tmpn[kc % 2]
                dve(lambda kc=kc, tv=tv: V.tensor_tensor(out=tv[:, 0:W], in0=xt[:, kc, 0:W], in1=rstd[:, 0:W], op=ALU.mult),
                    [xk, 'rstd' + sfx], [tk])
                act(lambda kc=kc, tv=tv: S_.activation(out=h[:, kc, 0:W], in_=tv[:, 0:W], func=AF.Identity,
                                                       scale=Am[:, kc, r:r + 1], bias=Bm[:, kc, r:r + 1]),
                    [tk, 'mods'], [hk])

        tmpn = [None, None]

        def emit_layer(l):
            last = (l == DEPTH - 1)
            chk('C0')
            A.release(PERSIST)
            ld(pf, pfd[l], [], ['pf'])
            ld(pbc, pbd[l:l + 1, :].partition_broadcast(128), [], ['pbc'])
            chk('S0p')
            wst = [A.alloc([6 * D], F32) for _ in range(2)]
            b0 = 0
            dve(lambda: V.memset(PB[b0][:, 0:48 * R], 0.0), [], [('ps', b0)])
            for kc in range(8):
                w = wst[kc % 2]
                wk = ('wst', kc % 2)
                ld(w, w_ada[l, kc * 128:(kc + 1) * 128, :], [], [wk])
                for j in range(48):
                    mm(PB[b0][:, j * R:(j + 1) * R], w[:, j * 128:(j + 1) * 128], scT[:, kc, :], False, kc == 7,
                       [wk, 'scT'], [('ps', b0)], skip=True)
                chk('S0k%d' % kc)
            chk('S0w')
            act(lambda: S_.copy(out=modt.rearrange("p a b -> p (a b)"), in_=PB[b0][:, 0:48 * R]), [('ps', b0)], ['mods'])
            chk('S0c')
            dve(lambda: V.tensor_tensor(out=modt, in0=modt,
                                        in1=pf[:, PF_BADA:PF_BADA + 48, None].to_broadcast([128, 48, R]), op=ALU.add),
                ['mods', 'pf'], ['mods'])
            chk('S0m')
            for (Ax, sc0, ng) in ((A1, 8, PF_N1), (A2, 32, PF_N2)):
                dve(lambda Ax=Ax, sc0=sc0: V.tensor_scalar(out=Ax, in0=modt[:, sc0:sc0 + 8, :], scalar1=1.0, scalar2=None,
                                                           op0=ALU.add), ['mods'], ['mods'])
                dve(lambda Ax=Ax, ng=ng: V.tensor_tensor(out=Ax, in0=Ax, in1=pf[:, ng:ng + 8, None].to_broadcast([128, 8, R]),
                                                         op=ALU.mult), ['mods', 'pf'], ['mods'])
            B1 = modt[:, 0:8, :]
            G1 = modt[:, 16:24, :]
            B2 = modt[:, 24:32, :]
            G2 = modt[:, 40:48, :]
            chk('S0n')
            act(lambda: S_.activation(out=aneg, in_=pbc[:, PB_ALOG:PB_ALOG + 16], func=AF.Exp), ['pbc'], ['aneg'])
            dve(lambda: V.tensor_scalar(out=aneg, in0=aneg, scalar1=-1.0, scalar2=None, op0=ALU.mult), ['aneg'], ['aneg'])
            act(lambda: S_.activation(out=rgcp, in_=pf[:, PF_RLM:PF_RLM + 8], func=AF.Exp, scale=-1.0), ['pf'], ['rgcp'])
            act(lambda: S_.activation(out=rgcp, in_=rgcp, func=AF.Ln, bias=1.0), ['rgcp'], ['rgcp'])
            dve(lambda: V.tensor_scalar(out=rgcp, in0=rgcp, scalar1=-8.0, scalar2=None, op0=ALU.mult), ['rgcp'], ['rgcp'])
            lt = A.alloc([128], F32)
            dve(lambda: V.tensor_tensor(out=lt[:, 0:64], in0=pbc[:, PB_LAM:PB_LAM + 64], in1=pbc[:, PB_LAM + 64:PB_LAM + 128],
                                        op=ALU.mult), ['pbc'], ['lt'])
            dve(lambda: V.tensor_tensor(out=lt[:, 64:128], in0=pbc[:, PB_LAM + 128:PB_LAM + 192],
                                        in1=pbc[:, PB_LAM + 192:PB_LAM + 256], op=ALU.mult), ['pbc', 'lt'], ['lt'])
            dve(lambda: V.tensor_reduce(out=lamt[:, 3:5], in_=lt.rearrange("p (a b) -> p a b", a=2), axis=AX.X, op=ALU.add),
                ['lt'], ['lamt'])
            act(lambda: S_.activation(out=lamt[:, 3:5], in_=lamt[:, 3:5], func=AF.Exp), ['lamt'], ['lamt'])
            dve(lambda: V.tensor_tensor(out=lamt[:, 0:1], in0=lamt[:, 4:5], in1=lamt[:, 3:4], op=ALU.subtract), ['lamt'], ['lamt'])
            dve(lambda: V.tensor_tensor(out=lamt[:, 0:1], in0=lamt[:, 0:1], in1=pbc[:, PB_LI:PB_LI + 1], op=ALU.subtract),
                ['lamt', 'pbc'], ['lamt'])
            dve(lambda: V.tensor_scalar(out=lamt[:, 1:2], in0=pbc[:, PB_LI:PB_LI + 1], scalar1=-1.0, scalar2=1.0,
                                        op0=ALU.mult, op1=ALU.add), ['lamt', 'pbc'], ['lamt'])
            dve(lambda: V.tensor_tensor(out=lamt[:, 2:3], in0=lamt[:, 1:2], in1=pf[:, PF_SUB:PF_SUB + 1], op=ALU.mult),
                ['lamt', 'pf'], ['lamt'])
            chk('S0l')
            dve(lambda: V.tensor_copy(out=gqk[:, 0:8, :], in_=pbc[:, None, PB_GQ:PB_GQ + 64].to_broadcast([128, 8, 64])),
                ['pbc'], ['gqk'])
            dve(lambda: V.tensor_copy(out=gqk[:, 8:10, :], in_=pbc[:, None, PB_GK:PB_GK + 64].to_broadcast([128, 2, 64])),
                ['pbc', 'gqk'], ['gqk'])
            dve(lambda: V.tensor_copy(out=gdk[:, 0:8, :], in_=pbc[:, None, PB_DQ:PB_DQ + 64].to_broadcast([128, 8, 64])),
                ['pbc'], ['gdk'])
            dve(lambda: V.tensor_copy(out=gdk[:, 8:16, :], in_=pbc[:, None, PB_DK:PB_DK + 64].to_broadcast([128, 8, 64])),
                ['pbc', 'gdk'], ['gdk'])
            P.barrier()
            chk('S0' + (kind if 'S0' in ('P3', 'P4') else ''))

            for s in range(NB):
                xsrc = xin[s] if l == 0 else xs[s]

                A.release(PERSIST)
                wb = A.alloc([8, INC], BF16)
                M1 = A.mark()
                stg = [A.alloc([INC], F32) for _ in range(2)]
                load_weight([wb[:, kc, :] for kc in range(8)], [w_in[l, kc * 128:(kc + 1) * 128, :] for kc in range(8)],
                            stg, 'wb')
                P.barrier()
                chk('P1w' + (kind if 'P1w' in ('P3', 'P4') else ''))
                A.release(M1)
                WB = [('wb', kc) for kc in range(8)]
                xt2 = [A.alloc([8, 512], F32) for _ in range(2)]
                sq = A.alloc([8, 512], BF16)
                rstd = A.alloc([512], F32)
                tmpn[0] = A.alloc([512], F32)
                tmpn[1] = A.alloc([512], F32)
                h2 = [A.alloc([8, 512], BF16) for _ in range(2)]
                ofm = [A.alloc([4, 512], BF16) for _ in range(2)]
                otok = [A.alloc([TOKC], BF16) for _ in range(2)]
                odt = [A.alloc([16], F32) for _ in range(2)]
                tokblocks = [(0, 0, 512)] + [(512 + 512 * i, 1536 + 512 * i, 512) for i in range(4)] + [(2560, 3584, 272)]
                fmcols = [512 + 128 * j for j in range(8)] + [3856 + 128 * j for j in range(8)]
                ev = [0]
                for ti, (t0, W) in enumerate(tiles_of()):
                    r = NB if t0 < NCTX else s
                    xt = xt2[ti % 2]
                    xk = ('xt', ti % 2)
                    h = h2[ti % 2]
                    hk = ('h', ti % 2)
                    ld(xt[:, :, 0:W], xsrc[:, t0:t0 + W].rearrange("(c p) t -> p c t", p=128), [('xs', s)], [xk])
                    chk('P1a')
                    norm_mod(xt, xk, W, sq, rstd, h, hk, A1, B1, r, '')
                    chk('P1b')
                    for j4 in range(4):
                        o = ofm[j4 % 2]
                        ok = ('ofm', j4 % 2)
                        for jj in range(4):
                            j = j4 * 4 + jj
                            c0 = fmcols[j]
                            b = nb(0, 6)
                            for kc in range(8):
                                mm(PB[b][:, 0:W], wb[:, kc, c0:c0 + 128], h[:, kc, 0:W], kc == 0, kc == 7,
                                   [WB[kc], hk], [('ps', b)])
                            ev[0] += 1
                            if ev[0] % 2:
                                act(lambda o=o, jj=jj, b=b: S_.copy(out=o[:, jj, 0:W], in_=PB[b][:, 0:W]), [('ps', b)], [ok])
                            else:
                                dve(lambda o=o, jj=jj, b=b: V.tensor_copy(out=o[:, jj, 0:W], in_=PB[b][:, 0:W]), [('ps', b)], [ok])
                        stq(u_fm[j4 * 512:(j4 + 1) * 512, t0:t0 + W].rearrange("(c p) t -> p c t", p=128), o[:, :, 0:W],
                            [ok], [('u_fm', j4)])
                    chk('P1d')
                    for si in range(W // 128):
                        tg = t0 + si * 128
                        ot = otok[si % 2]
                        otk = ('otok', si % 2)
                        od = odt[si % 2]
                        for (oc0, wc0, cw) in tokblocks:
                            b = nb(0, 6)
                            for kc in range(8):
                                mm(PB[b][:, 0:cw], h[:, kc, si * 128:(si + 1) * 128], wb[:, kc, wc0:wc0 + cw], kc == 0, kc == 7,
                                   [WB[kc], hk], [('ps', b)])
                            ev[0] += 1
                            if ev[0] % 2:
                                act(lambda ot=ot, b=b, oc0=oc0, cw=cw: S_.copy(out=ot[:, oc0:oc0 + cw], in_=PB[b][:, 0:cw]),
                                    [('ps', b)], [otk])
                            else:
                                dve(lambda ot=ot, b=b, oc0=oc0, cw=cw: V.tensor_copy(out=ot[:, oc0:oc0 + cw], in_=PB[b][:, 0:cw]),
                                    [('ps', b)], [otk])
                            if oc0 == 512:
                                dve(lambda od=od, b=b: V.tensor_copy(out=od, in_=PB[b][:, 0:16]), [('ps', b)], [otk])
                        stq(u_tok[tg:tg + 128, :], ot, [otk], [('u_tok', tg // 128)])
                        stq(dt_tok[tg:tg + 128, :], od, [otk], [('dt_tok', tg // 128)])
                P.barrier()
                chk('P1' + (kind if 'P1' in ('P3', 'P4') else ''))

                A.release(PERSIST)
                BCfm = A.alloc([4, NTOK], BF16)
                xs_tok = A.alloc([NSUB, 512], BF16)
                B_tok = A.alloc([NSUB, 256], BF16)
                dtr = A.alloc([NSUB, 16], F32)
                dtv = A.alloc([NSUB, 16], F32)
                av = A.alloc([NSUB, 16], F32)
                ev_ = A.alloc([NSUB, 16], F32)
                wg = A.alloc([NSUB, 16], F32)
                eA = A.alloc([NSUB, 16], F32)
                gS = A.alloc([512], F32)
                dtb = A.alloc([16], F32)
                M2 = A.mark()
                rawp = [A.alloc([NTOK + 6], BF16) for _ in range(2)]
                acc = A.alloc([NTOK], F32)
                xcv = [A.alloc([NTOK], BF16) for _ in range(2)]
                for i in range(2):
                    pool(lambda i=i: G.memset(rawp[i], 0.0), [], [('rawp', i)])
                SEG = [(0, NCTX, 1), (NCTX, NLAT, 4 + NCTX)]
                for j in range(8):
                    rp = rawp[j % 2]
                    rk = ('rawp', j % 2)
                    for (g0, gl, c0) in SEG:
                        ld(rp[:, c0:c0 + gl], u_fm[j * 128:(j + 1) * 128, g0:g0 + gl], [('u_fm', j // 4)], [rk])
                    for (g0, gl, c0) in SEG:
                        for k in range(4):
                            src = rp[:, c0 + k - 1:c0 + k - 1 + gl]
                            wk_ = pf[:, PF_SCW + j * 4 + k:PF_SCW + j * 4 + k + 1]
                            if k == 0:
                                dve(lambda src=src, wk_=wk_, g0=g0, gl=gl, j=j: V.tensor_scalar(
                                    out=acc[:, g0:g0 + gl], in0=src, scalar1=wk_, scalar2=pf[:, PF_SCB + j:PF_SCB + j + 1],
                                    op0=ALU.mult, op1=ALU.add), [rk, 'pf'], ['acc'])
                            else:
                                dve(lambda src=src, wk_=wk_, g0=g0, gl=gl: V.scalar_tensor_tensor(
                                    out=acc[:, g0:g0 + gl], in0=src, scalar=wk_, in1=acc[:, g0:g0 + gl],
                                    op0=ALU.mult, op1=ALU.add), [rk, 'pf', 'acc'], ['acc'])
                    if j < 6:
                        xc = xcv[j % 2]
                        xck = ('xcv', j % 2)
                    else:
                        xc = BCfm[:, j - 4, :]
                        xck = ('BCfm', j - 4)
                    act(lambda xc=xc: S_.activation(out=xc, in_=acc, func=AF.Silu), ['acc'], [xck])
                    if j in (4, 5):
                        pool(lambda xc=xc, j=j: G.tensor_copy(out=BCfm[:, j - 4, :], in_=xc), [xck], [('BCfm', j - 4)])
                    if j < 6:
                        for s0 in range(0, NSUB, 8):
                            ns = min(8, NSUB - s0)
                            b = nb(0, 6)
                            for q in range(ns):
                                pe(lambda q=q, s0=s0, b=b, xc=xc: T.transpose(out=PBH[b][:, q * 128:(q + 1) * 128],
                                                                             in_=xc[:, (s0 + q) * 128:(s0 + q + 1) * 128],
                                                                             identity=ident), [xck, 'ident'], [('ps', b)])
                            if j < 4:
                                dst = xs_tok[:, s0:s0 + ns, j * 128:(j + 1) * 128]
                                dk_ = 'xs_tok'
                            else:
                                dst = B_tok[:, s0:s0 + ns, (j - 4) * 128:(j - 3) * 128]
                                dk_ = 'B_tok'
                            dve(lambda dst=dst, b=b, ns=ns: V.tensor_copy(
                                out=dst, in_=PBH[b][:, 0:ns * 128].rearrange("p (a b) -> p a b", a=ns)), [('ps', b)], [dk_])
                ld(dtr, dt_tok.rearrange("(s p) j -> p s j", p=128), [('dt_tok', i) for i in range(NSUB)], ['dtr'])
                dve(lambda: V.tensor_copy(out=dtb, in_=pbc[:, PB_DTB:PB_DTB + 16]), ['pbc'], ['dtb'])
                dve(lambda: V.tensor_copy(out=gS, in_=pbc[:, PB_SNG:PB_SNG + 512]), ['pbc'], ['gS'])
                dve(lambda: V.tensor_tensor(out=dtr, in0=dtr, in1=dtb[:, None, :].to_broadcast([128, NSUB, 16]), op=ALU.add),
                    ['dtr', 'dtb'], ['dtr'])
                act(lambda: S_.activation(out=dtv, in_=dtr, func=AF.Exp), ['dtr'], ['dtv'])
                act(lambda: S_.activation(out=dtv, in_=dtv, func=AF.Ln, bias=1.0), ['dtv'], ['dtv'])
                dve(lambda: V.tensor_tensor(out=av, in0=dtv, in1=aneg[:, None, :].to_broadcast([128, NSUB, 16]), op=ALU.mult),
                    ['dtv', 'aneg'], ['av'])
                HALF = (NSUB + 1) // 2
                for c0 in range(0, NSUB, HALF):
                    ncn = min(HALF, NSUB - c0)
                    b1, b2, b3 = 0, 1, 2
                    for c in range(c0, c0 + ncn):
                        o = (c - c0) * 16
                        mm(PB[b1][:, o:o + 8], cm32[:, LI, :], av[:, c, 0:8], True, True, ['cm32', 'av'], [('ps', b1)])
                        mm(PB[b1][:, o + 8:o + 16], cm32[:, UI, :], av[:, c, 8:16], True, True, ['cm32', 'av'], [('ps', b1)])
                        mm(PB[b2][:, o:o + 8], cm32[:, US, :], av[:, c, 0:8], True, True, ['cm32', 'av'], [('ps', b2)])
                        mm(PB[b2][:, o + 8:o + 16], cm32[:, LS, :], av[:, c, 8:16], True, True, ['cm32', 'av'], [('ps', b2)])
                        mm(PB[b3][:, o:o + 16], cm32[:, ONES, :], av[:, c, :], True, True, ['cm32', 'av'], [('ps', b3)])
                    for (bb, dst, dk_) in ((b1, ev_, 'ev'), (b2, wg, 'wg'), (b3, eA, 'eA')):
                        act(lambda bb=bb, dst=dst, c0=c0, ncn=ncn: S_.activation(
                            out=dst[:, c0:c0 + ncn, :], in_=PB[bb][:, 0:ncn * 16].rearrange("p (a b) -> p a b", b=16),
                            func=AF.Exp), [('ps', bb)], [dk_])
                dve(lambda: V.tensor_tensor(out=wg, in0=wg, in1=dtv, op=ALU.mult), ['wg', 'dtv'], ['wg'])
                P.barrier()
                chk('P2a' + (kind if 'P2a' in ('P3', 'P4') else ''))
                A.release(M2)
                Sb_all = A.alloc([NSUB, 512], BF16)
                Sst = [A.alloc([512], F32) for _ in range(2)]
                Sf_bf = A.alloc([512], BF16)
                xw = [A.alloc([512], BF16) for _ in range(2)]
                CBm = A.alloc([2, 2, 128], F32)
                aM = A.alloc([16, 128], F32)
                E_sb = A.alloc([16, 128], F32)
                Wt = A.alloc([16, 128], BF16)
                zt = [A.alloc([512], BF16) for _ in range(2)]
                t1 = A.alloc([512], F32)
                t2 = A.alloc([512], F32)
                t3 = A.alloc([512], F32)
                yg = A.alloc([512], F32)
                ssq = A.alloc([4], F32)
                yn = A.alloc([512], BF16)
                ost = [A.alloc([4, 128], BF16) for _ in range(2)]
                junk = A.alloc([256], F32)
                for d in range(2):
                    dve(lambda d=d: V.memset(Sst[d], 0.0), [], [('Sst', d)])
                bw_order = list(range(NSC - 1, -1, -1)) + list(range(NSUB - 1, NSC - 1, -1))

                def state_update(c, d, xwk):
                    x_ = xw[xwk % 2]
                    k_ = ('xw', xwk % 2)
                    dve(lambda: V.tensor_tensor(out=x_.rearrange("p (h e) -> p h e", h=8),
                                                in0=xs_tok[:, c, :].rearrange("p (h e) -> p h e", h=8),
                                                in1=wg[:, c, d * 8:d * 8 + 8, None].to_broadcast([128, 8, 64]), op=ALU.mult),
                        ['xs_tok', 'wg'], [k_])
                    b = nb(4, 6)
                    for g in range(2):
                        mm(PB[b][:, g * 256:(g + 1) * 256], B_tok[:, c, g * 128:(g + 1) * 128], x_[:, g * 256:(g + 1) * 256],
                           True, True, ['B_tok', k_], [('ps', b)])
                    dve(lambda: V.tensor_tensor(out=Sst[d].rearrange("p (h e) -> p h e", h=8),
                                                in0=Sst[d].rearrange("p (h e) -> p h e", h=8),
                                                in1=eA[:, c, d * 8:d * 8 + 8, None].to_broadcast([128, 8, 64]), op=ALU.mult),
                        [('Sst', d), 'eA'], [('Sst', d)])
                    dve(lambda: V.tensor_tensor(out=Sst[d], in0=Sst[d], in1=PB[b], op=ALU.add), [('Sst', d), ('ps', b)],
                        [('Sst', d)])

                for i, c in enumerate(bw_order):
                    act(lambda c=c: S_.copy(out=Sb_all[:, c, :], in_=Sst[1]), [('Sst', 1)], [('Sb_all', c)])
                    if i < len(bw_order) - 1:
                        state_update(c, 1, i)
                for c in range(NSUB):
                    cs_ = slice(c * 128, (c + 1) * 128)
                    z_ = zt[c % 2]
                    zk = ('zt', c % 2)
                    ld(z_, u_tok[c * 128:(c + 1) * 128, TZ:TZ + 512], [('u_tok', c)], [zk])
                    act(lambda: S_.copy(out=Sf_bf, in_=Sst[0]), [('Sst', 0)], ['Sf_bf'])
                    bcb = 6
                    for g in range(2):
                        mm(PB[bcb][:, g * 128:(g + 1) * 128], BCfm[:, g, cs_], BCfm[:, 2 + g, cs_], True, True,
                           [('BCfm', g), ('BCfm', 2 + g)], [('ps', bcb)])
                    for d, mk in ((0, LI), (1, UI)):
                        dve(lambda d=d, mk=mk: V.tensor_tensor(
                            out=CBm[:, d, :, :], in0=PB[bcb][:, 0:256].rearrange("p (g l) -> p g l", g=2),
                            in1=cm32[:, mk, None, :].to_broadcast([128, 2, 128]), op=ALU.mult),
                            [('ps', bcb), 'cm32'], ['CBm'])
                    for d, mk in ((0, US), (1, LS)):
                        pool(lambda d=d, mk=mk, c=c: G.tensor_tensor(
                            out=aM[:, d * 8:d * 8 + 8, :], in0=cm32[:, mk, None, :].to_broadcast([128, 8, 128]),
                            in1=av[:, c, d * 8:d * 8 + 8, None].to_broadcast([128, 8, 128]), op=ALU.mult),
                            ['cm32', 'av'], [('aM', d)])
                    for q4 in range(4):
                        b = q4
                        for jj in range(4):
                            j = q4 * 4 + jj
                            d = j // 8
                            mm(PB[b][:, jj * 128:(jj + 1) * 128], aM[:, j, :], cm32[:, LI if d == 0 else UI, :], True, True,
                               [('aM', d), 'cm32'], [('ps', b)])
                        act(lambda q4=q4, b=b: S_.activation(out=E_sb[:, q4 * 4:q4 * 4 + 4, :],
                                                              in_=PB[b].rearrange("p (a l) -> p a l", a=4), func=AF.Exp),
                            [('ps', b)], [('E_sb', q4)])
                    dve(lambda: V.tensor_tensor(
                        out=E_sb.rearrange("p (a h) l -> p a h l", h=4), in0=E_sb.rearrange("p (a h) l -> p a h l", h=4),
                        in1=CBm.rearrange("p d g l -> p (d g) l")[:, :, None, :].to_broadcast([128, 4, 4, 128]), op=ALU.mult),
                        [('E_sb', q) for q in range(4)] + ['CBm'], [('E_sb', q) for q in range(4)])
                    dve(lambda c=c: V.tensor_tensor(out=Wt, in0=E_sb, in1=dtv[:, c, :, None].to_broadcast([128, 16, 128]),
                                                    op=ALU.mult), [('E_sb', q) for q in range(4)] + ['dtv'], ['Wt'])
                    by = 7
                    for hh in range(8):
                        mm(PB[by][:, hh * 64:(hh + 1) * 64], Wt[:, hh, :], xs_tok[:, c, hh * 64:(hh + 1) * 64], True, False,
                           ['Wt', 'xs_tok'], [('ps', by)])
                        mm(PB[by][:, hh * 64:(hh + 1) * 64], Wt[:, 8 + hh, :], xs_tok[:, c, hh * 64:(hh + 1) * 64], False, True,
                           ['Wt', 'xs_tok'], [('ps', by)])
                    bof, bob = 4, 5
                    for g in range(2):
                        mm(PB[bof][:, g * 256:(g + 1) * 256], BCfm[:, 2 + g, cs_], Sf_bf[:, g * 256:(g + 1) * 256], True, True,
                           [('BCfm', 2 + g), 'Sf_bf'], [('ps', bof)])
                        mm(PB[bob][:, g * 256:(g + 1) * 256], BCfm[:, 2 + g, cs_], Sb_all[:, c, g * 256:(g + 1) * 256], True, True,
                           [('BCfm', 2 + g), ('Sb_all', c)], [('ps', bob)])
                    dve(lambda c=c: V.tensor_tensor(out=t1.rearrange("p (h e) -> p h e", h=8),
                                                    in0=PB[bof].rearrange("p (h e) -> p h e", h=8),
                                                    in1=ev_[:, c, 0:8, None].to_broadcast([128, 8, 64]), op=ALU.mult),
                        [('ps', bof), 'ev'], ['t1'])
                    dve(lambda c=c: V.tensor_tensor(out=t2.rearrange("p (h e) -> p h e", h=8),
                                                    in0=PB[bob].rearrange("p (h e) -> p h e", h=8),
                                                    in1=ev_[:, c, 8:16, None].to_broadcast([128, 8, 64]), op=ALU.mult),
                        [('ps', bob), 'ev'], ['t2'])
                    pool(lambda c=c: G.tensor_tensor(out=t3.rearrange("p (h e) -> p h e", h=8),
                                                     in0=xs_tok[:, c, :].rearrange("p (h e) -> p h e", h=8),
                                                     in1=pbc[:, PB_D:PB_D + 8, None].to_broadcast([128, 8, 64]), op=ALU.mult),
                         ['xs_tok', 'pbc'], ['t3'])
                    pool(lambda: G.tensor_tensor(out=t1, in0=t1, in1=t2, op=ALU.add), ['t1', 't2'], ['t1'])
                    pool(lambda: G.tensor_tensor(out=t1, in0=t1, in1=t3, op=ALU.add), ['t1', 't3'], ['t1'])
                    dve(lambda: V.tensor_tensor(out=t1, in0=t1, in1=PB[by], op=ALU.add), ['t1', ('ps', by)], ['t1'])
                    if c < NSUB - 1:
                        state_update(c, 0, c)
                    act(lambda z_=z_: S_.activation(out=t2, in_=z_, func=AF.Silu), [zk, 't2'], ['t2'])
                    dve(lambda: V.tensor_tensor(out=yg, in0=t1, in1=t2, op=ALU.mult), ['t1', 't2'], ['yg'])
                    for g in range(2):
                        act(lambda g=g: S_.activation(out=junk, in_=yg[:, g * 256:(g + 1) * 256], func=AF.Square,
                                                      accum_out=ssq[:, g:g + 1]), ['yg', 'junk'], ['ssq', 'junk'])
                    act(lambda: S_.activation(out=ssq[:, 2:4], in_=ssq[:, 0:2], func=AF.Sqrt, scale=1.0 / 256, bias=EPS),
                        ['ssq'], ['ssq'])
                    dve(lambda: V.reciprocal(out=ssq[:, 2:4], in_=ssq[:, 2:4]), ['ssq'], ['ssq'])
                    for g in range(2):
                        dve(lambda g=g: V.scalar_tensor_tensor(out=yn[:, g * 256:(g + 1) * 256], in0=yg[:, g * 256:(g + 1) * 256],
                                                               scalar=ssq[:, 2 + g:3 + g], in1=gS[:, g * 256:(g + 1) * 256],
                                                               op0=ALU.mult, op1=ALU.mult), ['yg', 'ssq', 'gS'], ['yn'])
                    bt = 6
                    for k in range(4):
                        pe(lambda k=k: T.transpose(out=PBH[bt][:, 512 + k * 128:512 + (k + 1) * 128],
                                                   in_=yn[:, k * 128:(k + 1) * 128], identity=ident),
                           ['yn', 'ident'], [('ps', bt)])
                    o_ = ost[c % 2]
                    okk = ('ost', c % 2)
                    act(lambda o_=o_: S_.copy(out=o_, in_=PBH[bt][:, 512:1024].rearrange("p (k t) -> p k t", k=4)),
                        [('ps', bt)], [okk])
                    stq(brd[0][:, cs_].rearrange("(k p) t -> p k t", p=128), o_, [okk], [('br0', c)])
                P.barrier()
                chk('P2' + (kind if 'P2' in ('P3', 'P4') else ''))

                for kind in ('gqa', 'diff'):
                    A.release(PERSIST)
                    NH = 10 if kind == 'gqa' else 16
                    NQ = 8
                    NT = 6 if kind == 'gqa' else 8
                    c_in = TGQ if kind == 'gqa' else TDQ
                    w_in_cols = 640 if kind == 'gqa' else 1024
                    gq_, gk_n = (gqk, 'gqk') if kind == 'gqa' else (gdk, 'gdk')
                    qkT = A.alloc([NT, NTOK], BF16)
                    if kind == 'gqa':
                        vaug = A.alloc([NSUB, 2, 128], BF16)
                        pool(lambda: G.memset(vaug, 1.0), [], ['vaug'])
                        vtmp = [A.alloc([128], BF16) for _ in range(2)]
                    else:
                        vd = A.alloc([NSUB, 512], BF16)
                        ld(vd, u_tok[:, TDV:TDV + 512].rearrange("(s p) c -> p s c", p=128),
                           [('u_tok', i) for i in range(NSUB)], ['vd'])
                    qin = [A.alloc([NH, 64], BF16) for _ in range(2)]
                    sqf = A.alloc([NH, 64], F32)
                    qn = A.alloc([NH, 64], F32)
                    ssn = A.alloc([2, NH], F32)
                    ta = A.alloc([NH, 32], F32)
                    tb = A.alloc([NH, 32], F32)
                    tc_ = A.alloc([NH, 32], F32)
                    td = A.alloc([NH, 32], F32)
                    qr = [A.alloc([NH + 2, 64], BF16) for _ in range(2)]
                    for c in range(NSUB):
                        qi = qin[c % 2]
                        qik = ('qin', c % 2)
                        q_ = qr[c % 2]
                        qrk = ('qr', c % 2)
                        ld(qi, u_tok[c * 128:(c + 1) * 128, c_in:c_in + w_in_cols].rearrange("p (h e) -> p h e", e=64),
                           [('u_tok', c)], [qik])
                        if kind == 'gqa':
                            vt = vtmp[c % 2]
                            vk = ('vtmp', c % 2)
                            ld(vt, u_tok[c * 128:(c + 1) * 128, TGV:TGV + 128], [('u_tok', c)], [vk])
                            pool(lambda vt=vt, c=c: G.tensor_copy(out=vaug[:, c, :, 0:64], in_=vt.rearrange("p (g e) -> p g e", g=2)),
                                 [vk, 'vaug'], ['vaug'])
                        act(lambda qi=qi: S_.activation(out=sqf, in_=qi, func=AF.Square), [qik], ['sqf'])
                        dve(lambda: V.tensor_reduce(out=ssn[:, 0, :], in_=sqf, axis=AX.X, op=ALU.add), ['sqf'], ['ssn'])
                        act(lambda: S_.activation(out=ssn[:, 1, :], in_=ssn[:, 0, :], func=AF.Sqrt, scale=1.0 / 64, bias=EPS),
                            ['ssn'], ['ssn'])
                        dve(lambda: V.reciprocal(out=ssn[:, 1, :], in_=ssn[:, 1, :]), ['ssn'], ['ssn'])
                        dve(lambda qi=qi: V.tensor_tensor(out=qn, in0=qi, in1=ssn[:, 1, :, None].to_broadcast([128, NH, 64]),
                                                          op=ALU.mult), [qik, 'ssn'], ['qn'])
                        if c >= NSC:
                            cl = c - NSC
                            pool(lambda: G.tensor_tensor(out=qn, in0=qn, in1=gq_, op=ALU.mult), ['qn', gk_n], ['qn'])
                            csb = rc[:, cl, None, :].to_broadcast([128, NH, 32])
                            snb = rs[:, cl, None, :].to_broadcast([128, NH, 32])
                            dve(lambda csb=csb: V.tensor_tensor(out=ta, in0=qn[:, :, 0:32], in1=csb, op=ALU.mult), ['qn', 'rc'], ['ta'])
                            pool(lambda snb=snb: G.tensor_tensor(out=tb, in0=qn[:, :, 32:64], in1=snb, op=ALU.mult), ['qn', 'rs'], ['tb'])
                            pool(lambda snb=snb: G.tensor_tensor(out=tc_, in0=qn[:, :, 0:32], in1=snb, op=ALU.mult), ['qn', 'rs'], ['tc'])
                            dve(lambda csb=csb: V.tensor_tensor(out=td, in0=qn[:, :, 32:64], in1=csb, op=ALU.mult), ['qn', 'rc'], ['td'])
                            dve(lambda q_=q_: V.tensor_tensor(out=q_[:, 0:NH, 0:32], in0=ta, in1=tb, op=ALU.subtract),
                                ['ta', 'tb'], [qrk])
                            pool(lambda q_=q_: G.tensor_tensor(out=q_[:, 0:NH, 32:64], in0=tc_, in1=td, op=ALU.add),
                                 ['tc', 'td', qrk], [qrk])
                        else:
                            pool(lambda q_=q_: G.tensor_tensor(out=q_[:, 0:NH, :], in0=qn, in1=gq_, op=ALU.mult), ['qn', gk_n], [qrk])
                        bt = nb(0, 6)
                        if kind == 'gqa':
                            pool(lambda q_=q_: G.tensor_copy(out=q_[:, 10:12, :], in_=q_[:, 9:10, :].to_broadcast([128, 2, 64])),
                                 [qrk], [qrk])
                            pool(lambda q_=q_: G.tensor_copy(out=q_[:, 9:10, :], in_=q_[:, 8:9, :]), [qrk], [qrk])
                        for k in range(NT):
                            pe(lambda k=k, q_=q_, bt=bt: T.transpose(out=PBH[bt][:, k * 128:(k + 1) * 128],
                                                                     in_=q_[:, 2 * k:2 * k + 2, :], identity=ident),
                               [qrk, 'ident'], [('ps', bt)])
                        act(lambda bt=bt, c=c: S_.copy(out=qkT[:, :, c * 128:(c + 1) * 128],
                                                       in_=PBH[bt][:, 0:NT * 128].rearrange("p (k t) -> p k t", k=NT)),
                            [('ps', bt)], [('qkT', c)])
                    QKT = [('qkT', c) for c in range(NSUB)]
                    P.barrier()
                    chk('P3' + (kind if 'P3' in ('P3', 'P4') else ''))
                    MA = A.mark()
                    qblocks = ([] if last else [(0, NCTX, NSC)]) + [(NCTX + 512 * i, 512, NSUB) for i in range(NLAT // 512)]
                    pT = [A.alloc([512], BF16) for _ in range(4)]
                    oT = [A.alloc([4, 512], BF16) for _ in range(2)]
                    rsb = [A.alloc([512], F32) for _ in range(2)]
                    if kind == 'diff':
                        o1 = A.alloc([512], F32)
                        o2 = A.alloc([512], F32)
                        sqd = A.alloc([512], BF16)
                    ptc = [0]
                    for qi_, (q0, QW, nkt) in enumerate(qblocks):
                        o_ = oT[qi_ % 2]
                        ok_ = ('oT', qi_ % 2)
                        qs = slice(q0, q0 + QW)
                        if kind == 'gqa':
                            for hh in range(8):
                                hf, j, g = hh % 2, hh // 2, hh // 4
                                pp = slice(hf * 64, hf * 64 + 64)
                                bo = 4 + hh % 2
                                pend = None
                                for kt in range(nkt + 1):
                                    if kt < nkt:
                                        bs = kt % 2
                                        ks = slice(kt * 128, (kt + 1) * 128)
                                        mm(PB[bs][:, 0:QW], qkT[pp, 4 + g, ks], qkT[pp, j, qs], True, True, QKT, [('ps', bs)])
                                        p_ = pT[ptc[0] % 4]
                                        pk = ('pT', ptc[0] % 4)
                                        ptc[0] += 1
                                        act(lambda p_=p_, bs=bs: S_.activation(out=p_[:, 0:QW], in_=PB[bs][:, 0:QW], func=AF.Exp,
                                                                               scale=0.125), [('ps', bs)], [pk])
                                    if pend is not None:
                                        kt2, p2, pk2 = pend
                                        mm(PB[bo][:, 0:QW], vaug[:, kt2, g, :], p2[:, 0:QW], kt2 == 0, kt2 == nkt - 1,
                                           ['vaug', pk2], [('ps', bo)])
                                    pend = (kt, p_, pk) if kt < nkt else None
                                r_ = rsb[hh % 2]
                                rk_ = ('rsb', hh % 2)
                                dve(lambda r_=r_, bo=bo: V.reciprocal(out=r_[0:64, 0:QW], in_=PB[bo][64:128, 0:QW]), [('ps', bo)], [rk_])
                                dve(lambda r_=r_, bo=bo, o_=o_, pp=pp, j=j: V.tensor_tensor(
                                    out=o_[pp, j, 0:QW], in0=PB[bo][0:64, 0:QW], in1=r_[0:64, 0:QW], op=ALU.mult),
                                    [('ps', bo), rk_, ok_], [ok_])
                            stq(brd[1][:, qs].rearrange("(k p) t -> p k t", p=128), o_[:, :, 0:QW], [ok_],
                                [('br1', i) for i in range(q0 // 128, (q0 + QW) // 128)])
                        else:
                            for hh in range(4):
                                bo = [4, 5]
                                bsu = [6, 7]
                                pend = [None, None]
                                for kt in range(nkt + 1):
                                    for cc in range(2):
                                        pp = slice(cc * 64, cc * 64 + 64)
                                        if pend[cc] is not None:
                                            kt2, p2, pk2 = pend[cc]
                                            mm(PB[bo[cc]][:, 0:QW], vd[:, kt2, hh * 128:(hh + 1) * 128], p2[:, 0:QW],
                                               kt2 == 0, kt2 == nkt - 1, ['vd', pk2], [('ps', bo[cc])])
                                            mm(PB[bsu[cc]][:, 0:QW], ones_bf, p2[:, 0:QW], kt2 == 0, kt2 == nkt - 1,
                                               ['ones_bf', pk2], [('ps', bsu[cc])])
                                            pend[cc] = None
                                        if kt < nkt:
                                            bs = cc
                                            ks = slice(kt * 128, (kt + 1) * 128)
                                            mm(PB[bs][:, 0:QW], qkT[pp, 4 + hh, ks], qkT[pp, hh, qs], True, True, QKT, [('ps', bs)])
                                            p_ = pT[ptc[0] % 4]
                                            pk = ('pT', ptc[0] % 4)
                                            ptc[0] += 1
                                            act(lambda p_=p_, bs=bs: S_.activation(out=p_[:, 0:QW], in_=PB[bs][:, 0:QW],
                                                                                   func=AF.Exp, scale=0.125), [('ps', bs)], [pk])
                                            pend[cc] = (kt, p_, pk)
                                for cc, ox in ((0, o1), (1, o2)):
                                    r_ = rsb[cc]
                                    rk_ = ('rsb', cc)
                                    dve(lambda r_=r_, cc=cc: V.reciprocal(out=r_[:, 0:QW], in_=PB[bsu[cc]][:, 0:QW]),
                                        [('ps', bsu[cc])], [rk_])
                                    dve(lambda r_=r_, cc=cc, ox=ox: V.tensor_tensor(out=ox[:, 0:QW], in0=PB[bo[cc]][:, 0:QW],
                                                                                   in1=r_[:, 0:QW], op=ALU.mult),
                                        [('ps', bo[cc]), rk_], [('o12', cc)])
                                dve(lambda: V.scalar_tensor_tensor(out=o1[:, 0:QW], in0=o2[:, 0:QW], scalar=lamt[:, 0:1],
                                                                   in1=o1[:, 0:QW], op0=ALU.mult, op1=ALU.add),
                                    [('o12', 0), ('o12', 1), 'lamt'], [('o12', 0)])
                                act(lambda: S_.activation(out=sqd[:, 0:QW], in_=o1[:, 0:QW], func=AF.Square), [('o12', 0)], ['sqd'])
                                b3 = 3
                                mm(PB[b3][:, 0:QW], ones_bf, sqd[:, 0:QW], True, True, ['ones_bf', 'sqd'], [('ps', b3)])
                                act(lambda: S_.activation(out=o2[:, 0:QW], in_=PB[b3][:, 0:QW], func=AF.Sqrt, scale=1.0 / 128,
                                                          bias=EPS), [('ps', b3), ('o12', 1)], [('o12', 1)])
                                dve(lambda: V.reciprocal(out=o2[:, 0:QW], in_=o2[:, 0:QW]), [('o12', 1)], [('o12', 1)])
                                dve(lambda hh=hh, o_=o_: V.scalar_tensor_tensor(out=o_[:, hh, 0:QW], in0=o1[:, 0:QW],
                                                                                scalar=lamt[:, 2:3], in1=o2[:, 0:QW],
                                                                                op0=ALU.mult, op1=ALU.mult),
                                    [('o12', 0), ('o12', 1), 'lamt', ok_], [ok_])
                            stq(brd[2][:, qs].rearrange("(k p) t -> p k t", p=128), o_[:, :, 0:QW], [ok_],
                                [('br2', i) for i in range(q0 // 128, (q0 + QW) // 128)])
                    P.barrier()
                    chk('P4' + (kind if 'P4' in ('P3', 'P4') else ''))

                A.release(PERSIST)
                rgw32 = A.alloc([16, 128], F32)
                rgwb = A.alloc([16, 128], BF16)
                ld(rgw32, rgwd[l], [], ['rgw32'])
                pool(lambda: G.tensor_copy(out=rgwb, in_=rgw32), ['rgw32'], ['rgwb'])
                rawx = A.alloc([NTOK + 6], BF16)
                pool(lambda: G.memset(rawx, 0.0), [], ['rawx'])
                xr = A.alloc([NTOK], F32)
                xrb = A.alloc([NTOK], BF16)
                a_all = A.alloc([NTOK], F32)
                b_all = A.alloc([NTOK], F32)
                hsum = A.alloc([NTOK], F32)
                hb = A.alloc([NTOK], F32)
                rgg = A.alloc([NTOK], BF16)
                rgo = A.alloc([NTOK], BF16)
                tg1 = A.alloc([512], F32)
                tg2 = A.alloc([512], F32)
                tg3 = A.alloc([512], F32)
                for j in range(4):
                    for (g0, gl, c0) in SEG:
                        ld(rawx[:, c0:c0 + gl], u_fm[(12 + j) * 128:(13 + j) * 128, g0:g0 + gl], [('u_fm', 3)], ['rawx'])
                    ld(rgg, u_fm[(8 + j) * 128:(9 + j) * 128, :], [('u_fm', 2)], ['rgg'])
                    for (g0, gl, c0) in SEG:
                        for k in range(4):
                            src = rawx[:, c0 + k - 1:c0 + k - 1 + gl]
                            wk_ = pf[:, PF_RCW + j * 4 + k:PF_RCW + j * 4 + k + 1]
                            if k == 0:
                                dve(lambda src=src, wk_=wk_, g0=g0, gl=gl, j=j: V.tensor_scalar(
                                    out=xr[:, g0:g0 + gl], in0=src, scalar1=wk_, scalar2=pf[:, PF_RCB + j:PF_RCB + j + 1],
                                    op0=ALU.mult, op1=ALU.add), ['rawx', 'pf'], ['xr'])
                            else:
                                dve(lambda src=src, wk_=wk_, g0=g0, gl=gl: V.scalar_tensor_tensor(
                                    out=xr[:, g0:g0 + gl], in0=src, scalar=wk_, in1=xr[:, g0:g0 + gl],
                                    op0=ALU.mult, op1=ALU.add), ['rawx', 'pf', 'xr'], ['xr'])
                    act(lambda: S_.copy(out=xrb, in_=xr), ['xr'], ['xrb'])
                    for d in range(2):
                        for (t0, W) in tiles_of():
                            ts_ = slice(t0, t0 + W)
                            ba_, bx_ = nb(0, 4), nb(4, 8)
                            mm(PB[ba_][:, 0:W], rgwb[:, (0 * 2 + d) * 4 + j, :], xrb[:, ts_], True, True, ['rgwb', 'xrb'], [('ps', ba_)])
                            mm(PB[bx_][:, 0:W], rgwb[:, (1 * 2 + d) * 4 + j, :], xrb[:, ts_], True, True, ['rgwb', 'xrb'], [('ps', bx_)])
                            cba = pf[:, PF_RBA + d * 4 + j:PF_RBA + d * 4 + j + 1]
                            cbx = pf[:, PF_RBX + d * 4 + j:PF_RBX + d * 4 + j + 1]
                            ccp = rgcp[:, d * 4 + j:d * 4 + j + 1]
                            act(lambda W=W, ba_=ba_, cba=cba: S_.activation(out=tg1[:, 0:W], in_=PB[ba_][:, 0:W], func=AF.Sigmoid,
                                                                            bias=cba), [('ps', ba_), 'pf'], ['tg1'])
                            act(lambda W=W, ts_=ts_, ccp=ccp: S_.activation(out=a_all[:, ts_], in_=tg1[:, 0:W], func=AF.Exp,
                                                                            scale=ccp), ['tg1', 'rgcp'], ['a_all'])
                            act(lambda W=W, bx_=bx_, cbx=cbx: S_.activation(out=tg2[:, 0:W], in_=PB[bx_][:, 0:W], func=AF.Sigmoid,
                                                                            bias=cbx), [('ps', bx_), 'pf'], ['tg2'])
                            dve(lambda W=W, ts_=ts_: V.tensor_tensor(out=tg2[:, 0:W], in0=tg2[:, 0:W], in1=xr[:, ts_], op=ALU.mult),
                                ['tg2', 'xr'], ['tg2'])
                            pool(lambda W=W, ts_=ts_: G.tensor_tensor(out=tg3[:, 0:W], in0=a_all[:, ts_], in1=a_all[:, ts_],
                                                                      op=ALU.mult), ['a_all', 'tg3'], ['tg3'])
                            act(lambda W=W: S_.activation(out=tg3[:, 0:W], in_=tg3[:, 0:W], func=AF.Sqrt, scale=-1.0, bias=1.0),
                                ['tg3'], ['tg3'])
                            dve(lambda W=W, ts_=ts_: V.tensor_tensor(out=b_all[:, ts_], in0=tg2[:, 0:W], in1=tg3[:, 0:W], op=ALU.mult),
                                ['tg2', 'tg3'], ['b_all'])
                        if d == 0:
                            dve(lambda: V.tensor_tensor_scan(out=hsum, data0=a_all, data1=b_all, initial=0.0, op0=ALU.mult,
                                                             op1=ALU.add), ['a_all', 'b_all'], ['hsum'])
                        else:
                            dve(lambda: V.tensor_tensor_scan(out=hb[:, 0:NCTX][:, ::-1], data0=a_all[:, 0:NCTX][:, ::-1],
                                                             data1=b_all[:, 0:NCTX][:, ::-1], initial=0.0, op0=ALU.mult,
                                                             op1=ALU.add), ['a_all', 'b_all'], ['hb'])
                            dve(lambda: V.tensor_tensor_scan(out=hb[:, NCTX:NTOK][:, ::-1], data0=a_all[:, NCTX:NTOK][:, ::-1],
                                                             data1=b_all[:, NCTX:NTOK][:, ::-1], initial=hb[:, 0:1],
                                                             op0=ALU.mult, op1=ALU.add), ['a_all', 'b_all', 'hb'], ['hb'])
                            pool(lambda: G.tensor_tensor(out=hsum, in0=hsum, in1=hb, op=ALU.add), ['hsum', 'hb'], ['hsum'])
                    pool(lambda: G.tensor_tensor(out=a_all, in0=rgg, in1=rgg, op=ALU.mult), ['rgg', 'a_all'], ['a_all'])
                    dve(lambda: V.tensor_scalar(out=a_all, in0=a_all, scalar1=0.044715, scalar2=1.0, op0=ALU.mult, op1=ALU.add),
                        ['a_all'], ['a_all'])
                    dve(lambda: V.tensor_tensor(out=a_all, in0=a_all, in1=rgg, op=ALU.mult), ['a_all', 'rgg'], ['a_all'])
                    act(lambda: S_.activation(out=b_all, in_=a_all, func=AF.Sigmoid, scale=1.5957691216057308),
                        ['a_all', 'b_all'], ['b_all'])
                    pool(lambda: G.tensor_tensor(out=b_all, in0=b_all, in1=rgg, op=ALU.mult), ['b_all', 'rgg'], ['b_all'])
                    dve(lambda: V.tensor_tensor(out=rgo, in0=b_all, in1=hsum, op=ALU.mult), ['b_all', 'hsum'], ['rgo'])
                    stq(brd[3][j * 128:(j + 1) * 128, :], rgo, ['rgo'], [('br3', j)])
                P.barrier()
                chk('P6' + (kind if 'P6' in ('P3', 'P4') else ''))

                A.release(PERSIST)
                wgt_ = A.alloc([4, 8, D], BF16)
                wbr_ = A.alloc([4, 4, D], BF16)
                wo_ = A.alloc([8, D], BF16)
                M7 = A.mark()
                stg7 = [A.alloc([4 * D], F32) for _ in range(2)]
                dsts, srcs = [], []
                for k in range(4):
                    for hf in range(2):
                        dsts.append(wgt_[:, k, hf * 4:(hf + 1) * 4, :])
                        srcs.append(w_gate[l, k, hf * 512:(hf + 1) * 512, :].rearrange("(c p) n -> p c n", p=128))
                for k in range(4):
                    dsts.append(wbr_[:, k, :, :])
                    srcs.append(w_br[l, k].rearrange("(c p) n -> p c n", p=128))
                for hf in range(2):
                    dsts.append(wo_[:, hf * 4:(hf + 1) * 4, :])
                    srcs.append(w_out[l, hf * 512:(hf + 1) * 512, :].rearrange("(c p) n -> p c n", p=128))
                load_weight(dsts, srcs, stg7, 'w7')
                W7 = [('w7', i) for i in range(len(dsts))]
                P.barrier()
                chk('P7w' + (kind if 'P7w' in ('P3', 'P4') else ''))
                A.release(M7)
                xt7 = [A.alloc([8, 256], F32) for _ in range(2)]
                sq = A.alloc([8, 256], BF16)
                rstd = A.alloc([256], F32)
                tmpn[0] = A.alloc([256], F32)
                tmpn[1] = A.alloc([256], F32)
                h7 = A.alloc([8, 256], BF16)
                ob = [A.alloc([4, 256], BF16) for _ in range(2)]
                macc = A.alloc([8, 256], F32)
                mbf = A.alloc([8, 256], BF16)
                sg2 = [A.alloc([256], F32) for _ in range(2)]
                tm2 = [A.alloc([256], F32) for _ in range(2)]
                cnt7 = [0]
                for ti, (t0, W) in enumerate(tiles_of(not last, 256)):
                    r = NB if t0 < NCTX else s
                    xt = xt7[ti % 2]
                    xk = ('xt', ti % 2)
                    ts_ = slice(t0, t0 + W)
                    ld(xt[:, :, 0:W], xsrc[:, ts_].rearrange("(c p) t -> p c t", p=128), [('xs', s)], [xk])
                    norm_mod(xt, xk, W, sq, rstd, h7, 'h7', A1, B1, r, '')
                    for k in range(4):
                        o_ = ob[k % 2]
                        obk = ('ob', k % 2)
                        ld(o_[:, :, 0:W], brd[k][:, ts_].rearrange("(c p) t -> p c t", p=128),
                           [(f'br{k}', i) for i in range(NSUB if k != 3 else 4)], [obk])
                        for oc in range(8):
                            ocs = slice(oc * 128, (oc + 1) * 128)
                            bg_, bb_ = nb(0, 4), nb(4, 8)
                            for kc in range(8):
                                mm(PB[bg_][:, 0:W], wgt_[:, k, kc, ocs], h7[:, kc, 0:W], kc == 0, kc == 7, W7 + ['h7'], [('ps', bg_)])
                            for kc in range(4):
                                mm(PB[bb_][:, 0:W], wbr_[:, k, kc, ocs], o_[:, kc, 0:W], kc == 0, kc == 3, W7 + [obk], [('ps', bb_)])
                            cnt7[0] += 1
                            sg = sg2[cnt7[0] % 2]
                            sgk = ('sg', cnt7[0] % 2)
                            act(lambda sg=sg, bg_=bg_, k=k, oc=oc: S_.activation(
                                out=sg[:, 0:W], in_=PB[bg_][:, 0:W], func=AF.Sigmoid,
                                bias=pf[:, PF_BG + k * 8 + oc:PF_BG + k * 8 + oc + 1]), [('ps', bg_), 'pf'], [sgk])
                            if k == 0:
                                dve(lambda sg=sg, bb_=bb_, oc=oc: V.tensor_tensor(out=macc[:, oc, 0:W], in0=PB[bb_][:, 0:W],
                                                                                  in1=sg[:, 0:W], op=ALU.mult),
                                    [('ps', bb_), sgk], [('macc', oc)])
                            else:
                                tm = tm2[cnt7[0] % 2]
                                tmk = ('tm', cnt7[0] % 2)
                                dve(lambda sg=sg, bb_=bb_, tm=tm: V.tensor_tensor(out=tm[:, 0:W], in0=PB[bb_][:, 0:W],
                                                                                  in1=sg[:, 0:W], op=ALU.mult),
                                    [('ps', bb_), sgk], [tmk])
                                pool(lambda tm=tm, oc=oc: G.tensor_tensor(out=macc[:, oc, 0:W], in0=macc[:, oc, 0:W],
                                                                          in1=tm[:, 0:W], op=ALU.add), [tmk, ('macc', oc)],
                                     [('macc', oc)])
                    act(lambda: S_.copy(out=mbf[:, :, 0:W], in_=macc[:, :, 0:W]), [('macc', oc) for oc in range(8)], ['mbf'])
                    for oc in range(8):
                        ocs = slice(oc * 128, (oc + 1) * 128)
                        b = nb(0, 8)
                        for kc in range(8):
                            mm(PB[b][:, 0:W], wo_[:, kc, ocs], mbf[:, kc, 0:W], kc == 0, kc == 7, W7 + ['mbf'], [('ps', b)])
                        dve(lambda oc=oc, b=b, xt=xt: V.scalar_tensor_tensor(out=xt[:, oc, 0:W], in0=PB[b][:, 0:W],
                                                                             scalar=G1[:, oc, r:r + 1], in1=xt[:, oc, 0:W],
                                                                             op0=ALU.mult, op1=ALU.add),
                            [('ps', b), 'mods', xk], [xk])
                    stq(xm[:, ts_].rearrange("(c p) t -> p c t", p=128), xt[:, :, 0:W], [xk], ['xm'])
                P.barrier()
                chk('P7' + (kind if 'P7' in ('P3', 'P4') else ''))

                A.release(PERSIST)
                wup = A.alloc([8, 2 * DFF], BF16)
                wdn = A.alloc([22, D], BF16)
                M8 = A.mark()
                stg8 = [A.alloc([4 * D], F32) for _ in range(2)]
                dsts, srcs = [], []
                for kc in range(8):
                    for q in range(2):
                        dsts.append(wup[:, kc, q * DFF:(q + 1) * DFF])
                        srcs.append(w_up[l, kc * 128:(kc + 1) * 128, q * DFF:(q + 1) * DFF])
                for q in range(11):
                    dsts.append(wdn[:, 2 * q:2 * q + 2, :])
                    srcs.append(w_down[l, q * 256:(q + 1) * 256, :].rearrange("(c p) n -> p c n", p=128))
                load_weight(dsts, srcs, stg8, 'w8')
                W8 = [('w8', i) for i in range(len(dsts))]
                P.barrier()
                chk('P8w' + (kind if 'P8w' in ('P3', 'P4') else ''))
                A.release(M8)
                xt8 = A.alloc([8, 256], F32)
                sq = A.alloc([8, 256], BF16)
                rstd = A.alloc([256], F32)
                tmpn[0] = A.alloc([256], F32)
                tmpn[1] = A.alloc([256], F32)
                h8 = A.alloc([8, 256], BF16)
                actb = A.alloc([22, 256], BF16)
                cg2 = [A.alloc([256], F32) for _ in range(2)]
                cv2 = [A.alloc([256], F32) for _ in range(2)]
                dve(lambda: V.memset(xt8, 1.0), [], ['xt8'])
                ftiles = []
                for (sg0, sg1) in ([] if last else [(0, NCTX)]) + [(NCTX, NTOK)]:
                    ln = sg1 - sg0
                    nlt = max(1, -(-ln // 254))
                    base = ln // nlt
                    o0 = sg0
                    for i in range(nlt):
                        wo = base + (1 if i < ln - base * nlt else 0)
                        ftiles.append((o0, wo, sg0, sg1))
                        o0 += wo
                dst_x = outd[s] if last else xs[s]
                for (o0, Wo, sg0, sg1) in ftiles:
                    Ww = Wo + 2
                    lo = max(o0 - 1, sg0)
                    hi = min(o0 + Wo + 1, sg1)
                    cl = lo - (o0 - 1)
                    ld(xt8[:, :, cl:cl + hi - lo], xm[:, lo:hi].rearrange("(c p) t -> p c t", p=128), ['xm'], ['xt8'])
                    r = NB if o0 < NCTX else s
                    norm_mod(xt8, 'xt8', Ww, sq, rstd, h8, 'h8', A2, B2, r, '')
                    if cl > 0:
                        pool(lambda: G.memset(h8[:, :, 0:1], 0.0), ['h8'], ['h8'])
                    if hi < o0 + Wo + 1:
                        pool(lambda Ww=Ww: G.memset(h8[:, :, Ww - 1:Ww], 0.0), ['h8'], ['h8'])
                    for i in range(22):
                        res = []
                        for half, buf in ((0, cg2), (1, cv2)):
                            ch = half * 22 + i
                            c0 = ch * 128
                            b = nb(0, 8)
                            for kc in range(8):
                                mm(PB[b][:, 0:Ww], wup[:, kc, c0:c0 + 128], h8[:, kc, 0:Ww], kc == 0, kc == 7, W8 + ['h8'], [('ps', b)])
                            cb_ = buf[i % 2]
                            cbk = ('c%d' % half, i % 2)
                            fw = lambda k, ch=ch: pf[:, PF_FCW + ch * 3 + k:PF_FCW + ch * 3 + k + 1]
                            act(lambda cb_=cb_, b=b, ch=ch, fw=fw: S_.activation(
                                out=cb_[:, 0:Wo], in_=PB[b][:, 1:1 + Wo], func=AF.Identity, scale=fw(1),
                                bias=pf[:, PF_FCB + ch:PF_FCB + ch + 1]), [('ps', b), 'pf'], [cbk])
                            dve(lambda cb_=cb_, b=b, fw=fw: V.scalar_tensor_tensor(out=cb_[:, 0:Wo], in0=PB[b][:, 0:Wo], scalar=fw(0),
                                                                                   in1=cb_[:, 0:Wo], op0=ALU.mult, op1=ALU.add),
                                [('ps', b), 'pf', cbk], [cbk])
                            dve(lambda cb_=cb_, b=b, fw=fw: V.scalar_tensor_tensor(out=cb_[:, 0:Wo], in0=PB[b][:, 2:2 + Wo],
                                                                                   scalar=fw(2), in1=cb_[:, 0:Wo], op0=ALU.mult,
                                                                                   op1=ALU.add), [('ps', b), 'pf', cbk], [cbk])
                            res.append((cb_, cbk))
                        (cgb, cgk), (cvb, cvk) = res
                        act(lambda cgb=cgb: S_.activation(out=cgb[:, 0:Wo], in_=cgb[:, 0:Wo], func=AF.Silu), [cgk], [cgk])
                        pool(lambda cgb=cgb, cvb=cvb, i=i: G.tensor_tensor(out=actb[:, i, 0:Wo], in0=cgb[:, 0:Wo], in1=cvb[:, 0:Wo],
                                                                           op=ALU.mult), [cgk, cvk], [('actb', i)])
                    AK = [('actb', i) for i in range(22)]
                    for oc in range(8):
                        ocs = slice(oc * 128, (oc + 1) * 128)
                        b = nb(0, 8)
                        for kc in range(22):
                            mm(PB[b][:, 0:Wo], wdn[:, kc, ocs], actb[:, kc, 0:Wo], kc == 0, kc == 21, W8 + AK, [('ps', b)])
                        dve(lambda oc=oc, b=b: V.scalar_tensor_tensor(out=xt8[:, oc, 1:1 + Wo], in0=PB[b][:, 0:Wo],
                                                                      scalar=G2[:, oc, r:r + 1], in1=xt8[:, oc, 1:1 + Wo],
                                                                      op0=ALU.mult, op1=ALU.add), [('ps', b), 'mods', 'xt8'], ['xt8'])
                    od0 = o0 - NCTX if last else o0
                    stq(dst_x[:, od0:od0 + Wo].rearrange("(c p) t -> p c t", p=128), xt8[:, :, 1:1 + Wo], ['xt8'],
                        [('xs', s), 'out'])
                P.barrier()
                chk('P8' + (kind if 'P8' in ('P3', 'P4') else ''))
        try:
            for l_ in range(DEPTH):
                emit_layer(l_)
        except StopBuild:
            P.barrier()
        P.op('sp', lambda: SY.nop(), reads=['out'])
        info = P.emit(lambda name: st.enter_context(nc.semaphore(name)))
    return nc, info


def _consts(NLAT):
    m = np.arange(128)[:, None]
    l_ = np.arange(128)[None, :]
    cm = np.stack([(m <= l_), (m < l_), (m >= l_), (m > l_), np.ones((128, 128), bool), (m == l_)], 1).astype(np.float32)
    t = np.arange(NLAT)
    row = (t // 64).astype(np.float32)
    col = (t % 64).astype(np.float32)
    inv = (10000.0 ** (-np.arange(16, dtype=np.float32) / 16)).astype(np.float32)
    ang = np.concatenate([row[:, None] * inv, col[:, None] * inv], -1).astype(np.float32)
    cs = np.cos(ang).astype(np.float32).reshape(NLAT // 128, 128, 32).transpose(1, 0, 2)
    sn = np.sin(ang).astype(np.float32).reshape(NLAT // 128, 128, 32).transpose(1, 0, 2)
    return np.ascontiguousarray(cm), np.ascontiguousarray(cs), np.ascontiguousarray(sn)


def _fm(v, n):
    return np.ascontiguousarray(np.asarray(v, np.float32).reshape(n, 128).T)


def _pack_params(inp, DEPTH):
    pf = np.zeros((DEPTH, 128, NPF), np.float32)
    pb = np.zeros((DEPTH, NPB), np.float32)
    rgw = np.zeros((DEPTH, 128, 16, 128), np.float32)
    for l in range(DEPTH):
        pf[l, :, PF_BADA:PF_BADA + 48] = _fm(inp['b_ada'][l], 48)
        pf[l, :, PF_N1:PF_N1 + 8] = _fm(inp['norm1_g'][l], 8)
        pf[l, :, PF_N2:PF_N2 + 8] = _fm(inp['norm2_g'][l], 8)
        scw = np.asarray(inp['ssd_conv_w'][l])
        pf[l, :, PF_SCW:PF_SCW + 32] = scw.reshape(4, 8, 128).transpose(2, 1, 0).reshape(128, 32)
        pf[l, :, PF_SCB:PF_SCB + 8] = _fm(inp['ssd_conv_b'][l], 8)
        pf[l, :, PF_SUB] = np.asarray(inp['diff_subln_g'][l])
        rcw = np.asarray(inp['rg_conv_w'][l])
        pf[l, :, PF_RCW:PF_RCW + 16] = rcw.reshape(4, 4, 128).transpose(2, 1, 0).reshape(128, 16)
        pf[l, :, PF_RCB:PF_RCB + 4] = _fm(inp['rg_conv_b'][l], 4)
        for nm, off in (('rg_ba', PF_RBA), ('rg_bx', PF_RBX), ('rg_lambda', PF_RLM)):
            v = np.asarray(inp[nm][l])
            pf[l, :, off:off + 8] = v.reshape(2, 4, 128).transpose(2, 0, 1).reshape(128, 8)
        bg = np.asarray(inp['b_gate'][l])
        pf[l, :, PF_BG:PF_BG + 32] = bg.reshape(4, 8, 128).transpose(2, 0, 1).reshape(128, 32)
        fcw = np.asarray(inp['ffn_conv_w'][l])
        pf[l, :, PF_FCW:PF_FCW + 132] = fcw.reshape(3, 44, 128).transpose(2, 1, 0).reshape(128, 132)
        pf[l, :, PF_FCB:PF_FCB + 44] = _fm(inp['ffn_conv_b'][l], 44)
        pb[l, PB_DTB:PB_DTB + 16] = np.asarray(inp['ssd_dt_bias'][l]).reshape(16)
        pb[l, PB_ALOG:PB_ALOG + 16] = np.asarray(inp['ssd_a_log'][l]).reshape(16)
        pb[l, PB_D:PB_D + 8] = np.asarray(inp['ssd_d'][l])
        pb[l, PB_SNG:PB_SNG + 512] = np.asarray(inp['ssd_norm_g'][l])
        pb[l, PB_GQ:PB_GQ + 64] = np.asarray(inp['gqa_qnorm_g'][l])
        pb[l, PB_GK:PB_GK + 64] = np.asarray(inp['gqa_knorm_g'][l])
        pb[l, PB_DQ:PB_DQ + 64] = np.asarray(inp['diff_qnorm_g'][l])
        pb[l, PB_DK:PB_DK + 64] = np.asarray(inp['diff_knorm_g'][l])
        pb[l, PB_LAM:PB_LAM + 256] = np.asarray(inp['diff_lambda'][l]).reshape(256)
        pb[l, PB_LI] = 0.8 - 0.6 * math.exp(-0.3 * l)
        for ax, nm in enumerate(('rg_wa', 'rg_wx')):
            w = np.asarray(inp[nm][l])
            for d in range(2):
                for j in range(4):
                    for q in range(2):
                        rgw[l, q * 64:(q + 1) * 64, (ax * 2 + d) * 4 + j, q * 64:(q + 1) * 64] = w[d, 2 * j + q]
    return pf, pb, rgw


_CACHE = {}
STOP = None


def run(inputs, NB, DEPTH, NCTX, NLAT, n_cores, debug=False):
    x = np.asarray(inputs['x'], np.float32)
    ctx = np.asarray(inputs['ctx'], np.float32)
    c = np.asarray(inputs['c'], np.float32)
    c_ctx = np.asarray(inputs['c_ctx'], np.float32)
    key = (NB, DEPTH, NCTX, NLAT, debug)
    if key not in _CACHE:
        _CACHE[key] = build_program(NB, DEPTH, NCTX, NLAT, debug, stop=STOP)
    nc, info = _CACHE[key]
    cm, cs, sn = _consts(NLAT)
    pf, pb, rgw = _pack_params(inputs, DEPTH)
    wnames = ['w_ada', 'w_in', 'w_gate', 'w_br', 'w_out', 'w_up', 'w_down']
    shared = {k: np.ascontiguousarray(np.asarray(inputs[k], np.float32)) for k in wnames}
    shared.update(dict(ropec=cs, ropes=sn, cmask=cm, pf=pf, pb=pb, rgw=rgw))
    in_maps = []
    for core in range(n_cores):
        bs = range(core * NB, (core + 1) * NB)
        xin = np.stack([np.concatenate([ctx[b].T, x[b].T], axis=1) for b in bs], 0)
        cc = np.stack([c[b] for b in bs] + [c_ctx], 0)
        cTm = np.ascontiguousarray(cc.reshape(NB + 1, 8, 128).transpose(2, 1, 0))
        m = dict(shared)
        m['xin'] = np.ascontiguousarray(xin)
        m['cT'] = cTm
        in_maps.append(m)
    res = run_bass_kernel_spmd(nc, in_maps, core_ids=list(range(n_cores)))
    outs = []
    for core in range(n_cores):
        o = res.results[core]['out']
        for i in range(NB):
            outs.append(np.ascontiguousarray(o[i].T))
    return np.stack(outs, 0).astype(np.float32), res


def kernel(**inputs):
    out, _ = run(inputs, NB=2, DEPTH=2, NCTX=256, NLAT=4096, n_cores=8)
    return out
```

```python
import math
from contextlib import ExitStack
import numpy as np
import concourse.bass as bass
import concourse.mybir as mybir
from concourse.bass_utils import run_bass_kernel_spmd

F32 = mybir.dt.float32
BF16 = mybir.dt.bfloat16
AF = mybir.ActivationFunctionType
ALU = mybir.AluOpType
AX = mybir.AxisListType

SEM_EPOCH = 30000
N_DMA_SEMS = 56
EPS = 1e-6
D = 1024
INC = 4880
DFF = 2816
TZ, TDT, TGQ, TGK, TGV, TDQ, TDK, TDV, TOKC = 0, 512, 528, 1040, 1168, 1296, 1808, 2320, 2832
PF_BADA, PF_N1, PF_N2, PF_SCW, PF_SCB, PF_SUB, PF_RCW, PF_RCB, PF_RBA, PF_RBX, PF_RLM, PF_BG, PF_FCW, PF_FCB, NPF = \
    0, 48, 56, 64, 96, 104, 105, 121, 125, 133, 141, 149, 181, 313, 357
PB_DTB, PB_ALOG, PB_D, PB_SNG, PB_GQ, PB_GK, PB_DQ, PB_DK, PB_LAM, PB_LI, NPB = 0, 16, 32, 40, 552, 616, 680, 744, 808, 1064, 1065


class Rec:
    def __init__(self, target):
        self._t = target

    def __getattr__(self, name):
        f = getattr(self._t, name)
        return lambda *a, **k: (f, a, k)


class Prog:
    def __init__(self, nc):
        self.nc = nc
        self.eng = {'pe': nc.tensor, 'act': nc.scalar, 'dve': nc.vector, 'pool': nc.gpsimd, 'sp': nc.sync}
        self.ops = []
        self.pending_dma_w = set()
        self.nbar = 0
        self.dummy = None

    def op(self, engine, fn, reads=(), writes=(), dma=False):
        rec = fn()
        assert isinstance(rec, tuple) and len(rec) == 3
        self.ops.append((engine, rec, tuple(reads), tuple(writes), dma))
        if dma:
            self.pending_dma_w.update(writes)

    def barrier(self):
        b = self.nbar
        self.nbar += 1
        nc = self.nc
        pend = list(self.pending_dma_w)
        self.pending_dma_w = set()
        engs = ['pe', 'act', 'dve', 'pool', 'sp']
        for e in engs:
            rd = pend if e == 'sp' else []
            if e == 'sp' or self.dummy is None:
                self.op(e, (lambda e=e: (self.eng[e].nop, (), {})), reads=rd, writes=[('bar', b, e)])
            else:
                fn, r2, w2 = self.dummy[e]
                self.op(e, fn, reads=r2, writes=[('bar', b, e)] + w2)
        for e in engs:
            self.op(e, (lambda e=e: (self.eng[e].nop, (), {})), reads=[('bar', b, f) for f in engs if f != e],
                    writes=[('bar2', b, e)])

    def emit(self, sem_ctx):
        ops = self.ops
        n = len(ops)
        last_w = {}
        rd_eng = {}
        rd_dma = {}
        deps = [None] * n
        needs_inc = [False] * n
        for i, (e, fn, rd, wr, dma) in enumerate(ops):
            d = set()
            for k in rd:
                w = last_w.get(k)
                if w is not None:
                    d.add(w)
            for k in wr:
                w = last_w.get(k)
                if w is not None:
                    d.add(w)
                re_ = rd_eng.get(k)
                if re_:
                    d.update(re_.values())
                rdm = rd_dma.get(k)
                if rdm:
                    d.update(rdm)
            d.discard(i)
            dd = []
            for j in d:
                ej, _, _, _, dmaj = ops[j]
                if ej == e and (not dmaj) and (not dma) and e == 'pe':
                    continue
                dd.append(j)
                needs_inc[j] = True
            deps[i] = dd
            for k in rd:
                if dma:
                    rd_dma.setdefault(k, []).append(i)
                else:
                    rd_eng.setdefault(k, {})[e] = i
            for k in wr:
                last_w[k] = i
                rd_eng[k] = {}
                rd_dma[k] = []
        cnt = {e: 0 for e in self.eng}
        tl = [None] * n
        dma_cnt = [0] * N_DMA_SEMS
        dma_rr = 0
        for i, (e, fn, rd, wr, dma) in enumerate(ops):
            if dma:
                s = dma_rr % N_DMA_SEMS
                dma_rr += 1
                dma_cnt[s] += 16
                tl[i] = ('dma', s, dma_cnt[s])
            elif needs_inc[i]:
                cnt[e] += 1
                tl[i] = ('eng', e, cnt[e])
        sems = {}
        for e in self.eng:
            for ep in range(cnt[e] // SEM_EPOCH + 1):
                sems[(e, ep)] = sem_ctx(f"s_{e}_{ep}")
        dsems = [sem_ctx(f"s_dma_{s}") for s in range(N_DMA_SEMS)]
        seen = {e: {f: 0 for f in self.eng} for e in self.eng}
        seen_dma = {e: [0] * N_DMA_SEMS for e in self.eng}
        for i, (e, fn, rd, wr, dma) in enumerate(ops):
            eng = self.eng[e]
            need_eng = {}
            need_dma = {}
            for j in deps[i]:
                t = tl[j]
                if t[0] == 'eng':
                    _, f, c = t
                    if c > seen[e][f] and c > need_eng.get(f, 0):
                        need_eng[f] = c
                else:
                    _, s, v = t
                    if v > seen_dma[e][s] and v > need_dma.get(s, 0):
                        need_dma[s] = v
            for f, c in need_eng.items():
                ep = (c - 1) // SEM_EPOCH
                eng.wait_ge(sems[(f, ep)], c - ep * SEM_EPOCH)
                seen[e][f] = c
            for s, v in need_dma.items():
                eng.wait_ge(dsems[s], v)
                seen_dma[e][s] = v
            inst = fn[0](*fn[1], **fn[2])
            t = tl[i]
            if t is not None:
                if t[0] == 'dma':
                    inst.then_inc(dsems[t[1]], 16)
                else:
                    c = t[2]
                    inst.then_inc(sems[(e, (c - 1) // SEM_EPOCH)], 1)
        return dict(n_ops=n, counts=cnt)


class Arena:
    def __init__(self, ap_f32, ncols):
        self.ap = ap_f32
        self.n = ncols
        self.top = 0

    def mark(self):
        return self.top

    def release(self, m):
        self.top = m

    def alloc(self, shape, dt):
        nel = int(np.prod(shape))
        cols = (nel * (2 if dt == BF16 else 4) + 3) // 4
        cols = (cols + 7) // 8 * 8
        assert self.top + cols <= self.n, f"arena overflow {self.top}+{cols}>{self.n}"
        v = self.ap[:, self.top:self.top + cols]
        self.top += cols
        if dt == BF16:
            v = v.bitcast(BF16)
        v = v[:, 0:nel]
        if len(shape) == 2:
            v = v.rearrange("p (a b) -> p a b", a=shape[0])
        elif len(shape) == 3:
            v = v.rearrange("p (a b c) -> p a b c", a=shape[0], b=shape[1])
        elif len(shape) == 4:
            v = v.rearrange("p (a b c d) -> p a b c d", a=shape[0], b=shape[1], c=shape[2])
        return v


class StopBuild(Exception):
    pass


def build_program(NB, DEPTH, NCTX, NLAT, debug=False, stop=None):
    NTOK = NCTX + NLAT
    NSUB = NTOK // 128
    NSC = NCTX // 128
    NLS = NLAT // 128
    R = NB + 1
    nc = bass.Bass("TRN2", target_bir_lowering=False)
    dram = lambda name, shape, dt, kind: nc.dram_tensor(name, shape, dt, kind=kind).ap()
    skind = "ExternalOutput" if debug else "Internal"
    xin = dram("xin", [NB, D, NTOK], F32, "ExternalInput")
    cT = dram("cT", [128, 8, R], F32, "ExternalInput")
    ropec = dram("ropec", [128, NLS, 32], F32, "ExternalInput")
    ropes = dram("ropes", [128, NLS, 32], F32, "ExternalInput")
    cmask = dram("cmask", [128, 6, 128], F32, "ExternalInput")
    pfd = dram("pf", [DEPTH, 128, NPF], F32, "ExternalInput")
    pbd = dram("pb", [DEPTH, NPB], F32, "ExternalInput")
    rgwd = dram("rgw", [DEPTH, 128, 16, 128], F32, "ExternalInput")
    w_ada = dram("w_ada", [DEPTH, D, 6 * D], F32, "ExternalInput")
    w_in = dram("w_in", [DEPTH, D, INC], F32, "ExternalInput")
    w_gate = dram("w_gate", [DEPTH, 4, D, D], F32, "ExternalInput")
    w_br = dram("w_br", [DEPTH, 4, 512, D], F32, "ExternalInput")
    w_out = dram("w_out", [DEPTH, D, D], F32, "ExternalInput")
    w_up = dram("w_up", [DEPTH, D, 2 * DFF], F32, "ExternalInput")
    w_down = dram("w_down", [DEPTH, DFF, D], F32, "ExternalInput")
    outd = dram("out", [NB, D, NLAT], F32, "ExternalOutput")
    xs = [dram(f"xs{s}", [D, NTOK], F32, skind) for s in range(NB)]
    xm = dram("xm", [D, NTOK], F32, skind)
    u_tok = dram("u_tok", [NTOK, TOKC], BF16, skind)
    dt_tok = dram("dt_tok", [NTOK, 16], F32, skind)
    u_fm = dram("u_fm", [2048, NTOK], BF16, skind)
    brd = [dram(f"br{k}", [512, NTOK], BF16, skind) for k in range(4)]

    st = ExitStack()
    with st:
        ARENA_COLS = 52992
        arena_t = st.enter_context(nc.sbuf_tensor("arena", [128, ARENA_COLS], F32))
        psum_ts = [st.enter_context(nc.psum_tensor(f"psum{b}", [128, 512], F32)) for b in range(8)]
        A = Arena(arena_t, ARENA_COLS)
        PB = [psum_ts[b][:, :] for b in range(8)]
        PBH = [psum_ts[b][:, :].bitcast(BF16) for b in range(8)]
        P = Prog(nc)
        V, S_, G, T, SY = Rec(nc.vector), Rec(nc.scalar), Rec(nc.gpsimd), Rec(nc.tensor), Rec(nc.sync)

        def dve(fn, r, w):
            P.op('dve', fn, r, w)

        def act(fn, r, w):
            P.op('act', fn, r, w)

        def pool(fn, r, w):
            P.op('pool', fn, r, w)

        def pe(fn, r, w):
            P.op('pe', fn, r, w)

        def ld(out, in_, r, w):
            P.op('sp', lambda: SY.dma_start(out=out, in_=in_), r, w, dma=True)

        def stq(out, in_, r, w):
            P.op('sp', lambda: SY.dma_start(out=out, in_=in_), r, w, dma=True)

        def mm(out, lhsT, rhs, start, stop, r, w, skip=False):
            pe(lambda: T.matmul(out, lhsT=lhsT, rhs=rhs, start=start, stop=stop, skip_group_check=skip), r, w)

        cm32 = A.alloc([6, 128], F32)
        ld(cm32, cmask, [], ['cm32'])
        LI, LS, UI, US, ONES, IDN = range(6)
        ident = A.alloc([128], BF16)
        ones_bf = A.alloc([128], BF16)
        dve(lambda: V.tensor_copy(out=ident, in_=cm32[:, IDN, :]), ['cm32'], ['ident'])
        dve(lambda: V.tensor_copy(out=ones_bf, in_=cm32[:, ONES, :]), ['cm32'], ['ones_bf'])
        rc = A.alloc([NLS, 32], F32)
        rs = A.alloc([NLS, 32], F32)
        ld(rc, ropec, [], ['rc'])
        ld(rs, ropes, [], ['rs'])
        scT = A.alloc([8, R], F32)
        ld(scT, cT, [], ['scT'])
        act(lambda: S_.activation(out=scT, in_=scT, func=AF.Silu), ['scT'], ['scT'])
        pf = A.alloc([NPF], F32)
        pbc = A.alloc([NPB], F32)
        modt = A.alloc([48, R], F32)
        A1 = A.alloc([8, R], F32)
        A2 = A.alloc([8, R], F32)
        aneg = A.alloc([16], F32)
        rgcp = A.alloc([8], F32)
        lamt = A.alloc([8], F32)
        gqk = A.alloc([10, 64], F32)
        gdk = A.alloc([16, 64], F32)
        dum = A.alloc([8], F32)
        dve(lambda: V.memset(dum, 0.0), [], [('dum', 'act'), ('dum', 'dve'), ('dum', 'pool')])
        P.dummy = {
            'pe': (lambda: T.matmul(PB[7][0:1, 0:2], lhsT=ones_bf[0:1, 0:1], rhs=ones_bf[0:1, 0:2], start=True, stop=True),
                   ['ones_bf'], [('ps', 7)]),
            'act': (lambda: S_.copy(out=dum[:, 0:1], in_=dum[:, 1:2]), [], [('dum', 'act')]),
            'dve': (lambda: V.tensor_copy(out=dum[:, 2:3], in_=dum[:, 3:4]), [], [('dum', 'dve')]),
            'pool': (lambda: G.tensor_copy(out=dum[:, 4:5], in_=dum[:, 5:6]), [], [('dum', 'pool')]),
        }
        PERSIST = A.mark()

        pbank = [0]

        def chk(name):
            if stop == name:
                raise StopBuild()

        def nb(lo=0, hi=8):
            b = lo + (pbank[0] % (hi - lo))
            pbank[0] += 1
            return b

        def tiles_of(include_ctx=True, w=512):
            t = [(NCTX * i // (-(-NCTX // w)), NCTX // (-(-NCTX // w))) for i in range(-(-NCTX // w))] if include_ctx else []
            t += [(NCTX + w * i, w) for i in range(NLAT // w)]
            return t

        def load_weight(dst_views, src_views, stage, tag):
            for i, (dv, sv) in enumerate(zip(dst_views, src_views)):
                sg = stage[i % len(stage)]
                sk = (tag + '_stg', i % len(stage))
                shp = dv.shape
                sgv = sg
                if len(shp) == 2:
                    sgv = sg[:, 0:shp[1]]
                else:
                    sgv = sg[:, 0:shp[1] * shp[2]].rearrange("p (a b) -> p a b", a=shp[1])
                ld(sgv, sv, [], [sk])
                pool(lambda dv=dv, sgv=sgv: G.tensor_copy(out=dv, in_=sgv), [sk], [(tag, i)])

        def norm_mod(xt, xk, W, sq, rstd, h, hk, Am, Bm, r, sfx):
            act(lambda: S_.activation(out=sq[:, :, 0:W], in_=xt[:, :, 0:W], func=AF.Square), [xk], ['sq' + sfx])
            b = nb(6, 8)
            for kc in range(8):
                mm(PB[b][:, 0:W], ones_bf, sq[:, kc, 0:W], kc == 0, kc == 7, ['sq' + sfx, 'ones_bf'], [('ps', b)])
            act(lambda: S_.activation(out=rstd[:, 0:W], in_=PB[b][:, 0:W], func=AF.Sqrt, scale=1.0 / D, bias=EPS),
                [('ps', b)], ['rstd' + sfx])
            dve(lambda: V.reciprocal(out=rstd[:, 0:W], in_=rstd[:, 0:W]), ['rstd' + sfx], ['rstd' + sfx])
            for kc in range(8):
                tk = ('tmpn', kc % 2)
                tv = tmpn[kc % 2]
                dve(lambda kc=kc, tv=tv: V.tensor_tensor(out=tv[:, 0:W], in0=xt[:, kc, 0:W], in1=rstd[:, 0:W], op=ALU.mult),
                    [xk, 'rstd' + sfx], [tk])
                act(lambda kc=kc, tv=tv: S_.activation(out=h[:, kc, 0:W], in_=tv[:, 0:W], func=AF.Identity,
                                                       scale=Am[:, kc, r:r + 1], bias=Bm[:, kc, r:r + 1]),
                    [tk, 'mods'], [hk])

        tmpn = [None, None]

        def emit_layer(l):
            last = (l == DEPTH - 1)
            chk('C0')
            A.release(PERSIST)
            ld(pf, pfd[l], [], ['pf'])
            ld(pbc, pbd[l:l + 1, :].partition_broadcast(128), [], ['pbc'])
            chk('S0p')
            wst = [A.alloc([6 * D], F32) for _ in range(2)]
            b0 = 0
            dve(lambda: V.memset(PB[b0][:, 0:48 * R], 0.0), [], [('ps', b0)])
            for kc in range(8):
                w = wst[kc % 2]
                wk = ('wst', kc % 2)
                ld(w, w_ada[l, kc * 128:(kc + 1) * 128, :], [], [wk])
                for j in range(48):
                    mm(PB[b0][:, j * R:(j + 1) * R], w[:, j * 128:(j + 1) * 128], scT[:, kc, :], False, kc == 7,
                       [wk, 'scT'], [('ps', b0)], skip=True)
                chk('S0k%d' % kc)
            chk('S0w')
            act(lambda: S_.copy(out=modt.rearrange("p a b -> p (a b)"), in_=PB[b0][:, 0:48 * R]), [('ps', b0)], ['mods'])
            chk('S0c')
            dve(lambda: V.tensor_tensor(out=modt, in0=modt,
                                        in1=pf[:, PF_BADA:PF_BADA + 48, None].to_broadcast([128, 48, R]), op=ALU.add),
                ['mods', 'pf'], ['mods'])
            chk('S0m')
            for (Ax, sc0, ng) in ((A1, 8, PF_N1), (A2, 32, PF_N2)):
                dve(lambda Ax=Ax, sc0=sc0: V.tensor_scalar(out=Ax, in0=modt[:, sc0:sc0 + 8, :], scalar1=1.0, scalar2=None,
                                                           op0=ALU.add), ['mods'], ['mods'])
                dve(lambda Ax=Ax, ng=ng: V.tensor_tensor(out=Ax, in0=Ax, in1=pf[:, ng:ng + 8, None].to_broadcast([128, 8, R]),
                                                         op=ALU.mult), ['mods', 'pf'], ['mods'])
            B1 = modt[:, 0:8, :]
            G1 = modt[:, 16:24, :]
            B2 = modt[:, 24:32, :]
            G2 = modt[:, 40:48, :]
            chk('S0n')
            act(lambda: S_.activation(out=aneg, in_=pbc[:, PB_ALOG:PB_ALOG + 16], func=AF.Exp), ['pbc'], ['aneg'])
            dve(lambda: V.tensor_scalar(out=aneg, in0=aneg, scalar1=-1.0, scalar2=None, op0=ALU.mult), ['aneg'], ['aneg'])
            act(lambda: S_.activation(out=rgcp, in_=pf[:, PF_RLM:PF_RLM + 8], func=AF.Exp, scale=-1.0), ['pf'], ['rgcp'])
            act(lambda: S_.activation(out=rgcp, in_=rgcp, func=AF.Ln, bias=1.0), ['rgcp'], ['rgcp'])
            dve(lambda: V.tensor_scalar(out=rgcp, in0=rgcp, scalar1=-8.0, scalar2=None, op0=ALU.mult), ['rgcp'], ['rgcp'])
            lt = A.alloc([128], F32)
            dve(lambda: V.tensor_tensor(out=lt[:, 0:64], in0=pbc[:, PB_LAM:PB_LAM + 64], in1=pbc[:, PB_LAM + 64:PB_LAM + 128],
                                        op=ALU.mult), ['pbc'], ['lt'])
            dve(lambda: V.tensor_tensor(out=lt[:, 64:128], in0=pbc[:, PB_LAM + 128:PB_LAM + 192],
                                        in1=pbc[:, PB_LAM + 192:PB_LAM + 256], op=ALU.mult), ['pbc', 'lt'], ['lt'])
            dve(lambda: V.tensor_reduce(out=lamt[:, 3:5], in_=lt.rearrange("p (a b) -> p a b", a=2), axis=AX.X, op=ALU.add),
                ['lt'], ['lamt'])
            act(lambda: S_.activation(out=lamt[:, 3:5], in_=lamt[:, 3:5], func=AF.Exp), ['lamt'], ['lamt'])
            dve(lambda: V.tensor_tensor(out=lamt[:, 0:1], in0=lamt[:, 4:5], in1=lamt[:, 3:4], op=ALU.subtract), ['lamt'], ['lamt'])
            dve(lambda: V.tensor_tensor(out=lamt[:, 0:1], in0=lamt[:, 0:1], in1=pbc[:, PB_LI:PB_LI + 1], op=ALU.subtract),
                ['lamt', 'pbc'], ['lamt'])
            dve(lambda: V.tensor_scalar(out=lamt[:, 1:2], in0=pbc[:, PB_LI:PB_LI + 1], scalar1=-1.0, scalar2=1.0,
                                        op0=ALU.mult, op1=ALU.add), ['lamt', 'pbc'], ['lamt'])
            dve(lambda: V.tensor_tensor(out=lamt[:, 2:3], in0=lamt[:, 1:2], in1=pf[:, PF_SUB:PF_SUB + 1], op=ALU.mult),
                ['lamt', 'pf'], ['lamt'])
            chk('S0l')
            dve(lambda: V.tensor_copy(out=gqk[:, 0:8, :], in_=pbc[:, None, PB_GQ:PB_GQ + 64].to_broadcast([128, 8, 64])),
                ['pbc'], ['gqk'])
            dve(lambda: V.tensor_copy(out=gqk[:, 8:10, :], in_=pbc[:, None, PB_GK:PB_GK + 64].to_broadcast([128, 2, 64])),
                ['pbc', 'gqk'], ['gqk'])
            dve(lambda: V.tensor_copy(out=gdk[:, 0:8, :], in_=pbc[:, None, PB_DQ:PB_DQ + 64].to_broadcast([128, 8, 64])),
                ['pbc'], ['gdk'])
            dve(lambda: V.tensor_copy(out=gdk[:, 8:16, :], in_=pbc[:, None, PB_DK:PB_DK + 64].to_broadcast([128, 8, 64])),
                ['pbc', 'gdk'], ['gdk'])
            P.barrier()
            chk('S0' + (kind if 'S0' in ('P3', 'P4') else ''))

            for s in range(NB):
                xsrc = xin[s] if l == 0 else xs[s]

                A.release(PERSIST)
                wb = A.alloc([8, INC], BF16)
                M1 = A.mark()
                stg = [A.alloc([INC], F32) for _ in range(2)]
                load_weight([wb[:, kc, :] for kc in range(8)], [w_in[l, kc * 128:(kc + 1) * 128, :] for kc in range(8)],
                            stg, 'wb')
                P.barrier()
                chk('P1w' + (kind if 'P1w' in ('P3', 'P4') else ''))
                A.release(M1)
                WB = [('wb', kc) for kc in range(8)]
                xt2 = [A.alloc([8, 512], F32) for _ in range(2)]
                sq = A.alloc([8, 512], BF16)
                rstd = A.alloc([512], F32)
                tmpn[0] = A.alloc([512], F32)
                tmpn[1] = A.alloc([512], F32)
                h2 = [A.alloc([8, 512], BF16) for _ in range(2)]
                ofm = [A.alloc([4, 512], BF16) for _ in range(2)]
                otok = [A.alloc([TOKC], BF16) for _ in range(2)]
                odt = [A.alloc([16], F32) for _ in range(2)]
                tokblocks = [(0, 0, 512)] + [(512 + 512 * i, 1536 + 512 * i, 512) for i in range(4)] + [(2560, 3584, 272)]
                fmcols = [512 + 128 * j for j in range(8)] + [3856 + 128 * j for j in range(8)]
                ev = [0]
                for ti, (t0, W) in enumerate(tiles_of()):
                    r = NB if t0 < NCTX else s
                    xt = xt2[ti % 2]
                    xk = ('xt', ti % 2)
                    h = h2[ti % 2]
                    hk = ('h', ti % 2)
                    ld(xt[:, :, 0:W], xsrc[:, t0:t0 + W].rearrange("(c p) t -> p c t", p=128), [('xs', s)], [xk])
                    chk('P1a')
                    norm_mod(xt, xk, W, sq, rstd, h, hk, A1, B1, r, '')
                    chk('P1b')
                    for j4 in range(4):
                        o = ofm[j4 % 2]
                        ok = ('ofm', j4 % 2)
                        for jj in range(4):
                            j = j4 * 4 + jj
                            c0 = fmcols[j]
                            b = nb(0, 6)
                            for kc in range(8):
                                mm(PB[b][:, 0:W], wb[:, kc, c0:c0 + 128], h[:, kc, 0:W], kc == 0, kc == 7,
                                   [WB[kc], hk], [('ps', b)])
                            ev[0] += 1
                            if ev[0] % 2:
                                act(lambda o=o, jj=jj, b=b: S_.copy(out=o[:, jj, 0:W], in_=PB[b][:, 0:W]), [('ps', b)], [ok])
                            else:
                                dve(lambda o=o, jj=jj, b=b: V.tensor_copy(out=o[:, jj, 0:W], in_=PB[b][:, 0:W]), [('ps', b)], [ok])
                        stq(u_fm[j4 * 512:(j4 + 1) * 512, t0:t0 + W].rearrange("(c p) t -> p c t", p=128), o[:, :, 0:W],
                            [ok], [('u_fm', j4)])
                    chk('P1d')
                    for si in range(W // 128):
                        tg = t0 + si * 128
                        ot = otok[si % 2]
                        otk = ('otok', si % 2)
                        od = odt[si % 2]
                        for (oc0, wc0, cw) in tokblocks:
                            b = nb(0, 6)
                            for kc in range(8):
                                mm(PB[b][:, 0:cw], h[:, kc, si * 128:(si + 1) * 128], wb[:, kc, wc0:wc0 + cw], kc == 0, kc == 7,
                                   [WB[kc], hk], [('ps', b)])
                            ev[0] += 1
                            if ev[0] % 2:
                                act(lambda ot=ot, b=b, oc0=oc0, cw=cw: S_.copy(out=ot[:, oc0:oc0 + cw], in_=PB[b][:, 0:cw]),
                                    [('ps', b)], [otk])
                            else:
                                dve(lambda ot=ot, b=b, oc0=oc0, cw=cw: V.tensor_copy(out=ot[:, oc0:oc0 + cw], in_=PB[b][:, 0:cw]),
                                    [('ps', b)], [otk])
                            if oc0 == 512:
                                dve(lambda od=od, b=b: V.tensor_copy(out=od, in_=PB[b][:, 0:16]), [('ps', b)], [otk])
                        stq(u_tok[tg:tg + 128, :], ot, [otk], [('u_tok', tg // 128)])
                        stq(dt_tok[tg:tg + 128, :], od, [otk], [('dt_tok', tg // 128)])
                P.barrier()
                chk('P1' + (kind if 'P1' in ('P3', 'P4') else ''))

                A.release(PERSIST)
                BCfm = A.alloc([4, NTOK], BF16)
                xs_tok = A.alloc([NSUB, 512], BF16)
                B_tok = A.alloc([NSUB, 256], BF16)
                dtr = A.alloc([NSUB, 16], F32)
                dtv = A.alloc([NSUB, 16], F32)
                av = A.alloc([NSUB, 16], F32)
                ev_ = A.alloc([NSUB, 16], F32)
                wg = A.alloc([NSUB, 16], F32)
                eA = A.alloc([NSUB, 16], F32)
                gS = A.alloc([512], F32)
                dtb = A.alloc([16], F32)
                M2 = A.mark()
                rawp = [A.alloc([NTOK + 6], BF16) for _ in range(2)]
                acc = A.alloc([NTOK], F32)
                xcv = [A.alloc([NTOK], BF16) for _ in range(2)]
                for i in range(2):
                    pool(lambda i=i: G.memset(rawp[i], 0.0), [], [('rawp', i)])
                SEG = [(0, NCTX, 1), (NCTX, NLAT, 4 + NCTX)]
                for j in range(8):
                    rp = rawp[j % 2]
                    rk = ('rawp', j % 2)
                    for (g0, gl, c0) in SEG:
                        ld(rp[:, c0:c0 + gl], u_fm[j * 128:(j + 1) * 128, g0:g0 + gl], [('u_fm', j // 4)], [rk])
                    for (g0, gl, c0) in SEG:
                        for k in range(4):
                            src = rp[:, c0 + k - 1:c0 + k - 1 + gl]
                            wk_ = pf[:, PF_SCW + j * 4 + k:PF_SCW + j * 4 + k + 1]
                            if k == 0:
                                dve(lambda src=src, wk_=wk_, g0=g0, gl=gl, j=j: V.tensor_scalar(
                                    out=acc[:, g0:g0 + gl], in0=src, scalar1=wk_, scalar2=pf[:, PF_SCB + j:PF_SCB + j + 1],
                                    op0=ALU.mult, op1=ALU.add), [rk, 'pf'], ['acc'])
                            else:
                                dve(lambda src=src, wk_=wk_, g0=g0, gl=gl: V.scalar_tensor_tensor(
                                    out=acc[:, g0:g0 + gl], in0=src, scalar=wk_, in1=acc[:, g0:g0 + gl],
                                    op0=ALU.mult, op1=ALU.add), [rk, 'pf', 'acc'], ['acc'])
                    if j < 6:
                        xc = xcv[j % 2]
                        xck = ('xcv', j % 2)
                    else:
                        xc = BCfm[:, j - 4, :]
                        xck = ('BCfm', j - 4)
                    act(lambda xc=xc: S_.activation(out=xc, in_=acc, func=AF.Silu), ['acc'], [xck])
                    if j in (4, 5):
                        pool(lambda xc=xc, j=j: G.tensor_copy(out=BCfm[:, j - 4, :], in_=xc), [xck], [('BCfm', j - 4)])
                    if j < 6:
                        for s0 in range(0, NSUB, 8):
                            ns = min(8, NSUB - s0)
                            b = nb(0, 6)
                            for q in range(ns):
                                pe(lambda q=q, s0=s0, b=b, xc=xc: T.transpose(out=PBH[b][:, q * 128:(q + 1) * 128],
                                                                             in_=xc[:, (s0 + q) * 128:(s0 + q + 1) * 128],
                                                                             identity=ident), [xck, 'ident'], [('ps', b)])
                            if j < 4:
                                dst = xs_tok[:, s0:s0 + ns, j * 128:(j + 1) * 128]
                                dk_ = 'xs_tok'
                            else:
                                dst = B_tok[:, s0:s0 + ns, (j - 4) * 128:(j - 3) * 128]
                                dk_ = 'B_tok'
                            dve(lambda dst=dst, b=b, ns=ns: V.tensor_copy(
                                out=dst, in_=PBH[b][:, 0:ns * 128].rearrange("p (a b) -> p a b", a=ns)), [('ps', b)], [dk_])
                ld(dtr, dt_tok.rearrange("(s p) j -> p s j", p=128), [('dt_tok', i) for i in range(NSUB)], ['dtr'])
                dve(lambda: V.tensor_copy(out=dtb, in_=pbc[:, PB_DTB:PB_DTB + 16]), ['pbc'], ['dtb'])
                dve(lambda: V.tensor_copy(out=gS, in_=pbc[:, PB_SNG:PB_SNG + 512]), ['pbc'], ['gS'])
                dve(lambda: V.tensor_tensor(out=dtr, in0=dtr, in1=dtb[:, None, :].to_broadcast([128, NSUB, 16]), op=ALU.add),
                    ['dtr', 'dtb'], ['dtr'])
                act(lambda: S_.activation(out=dtv, in_=dtr, func=AF.Exp), ['dtr'], ['dtv'])
                act(lambda: S_.activation(out=dtv, in_=dtv, func=AF.Ln, bias=1.0), ['dtv'], ['dtv'])
                dve(lambda: V.tensor_tensor(out=av, in0=dtv, in1=aneg[:, None, :].to_broadcast([128, NSUB, 16]), op=ALU.mult),
                    ['dtv', 'aneg'], ['av'])
                HALF = (NSUB + 1) // 2
                for c0 in range(0, NSUB, HALF):
                    ncn = min(HALF, NSUB - c0)
                    b1, b2, b3 = 0, 1, 2
                    for c in range(c0, c0 + ncn):
                        o = (c - c0) * 16
                        mm(PB[b1][:, o:o + 8], cm32[:, LI, :], av[:, c, 0:8], True, True, ['cm32', 'av'], [('ps', b1)])
                        mm(PB[b1][:, o + 8:o + 16], cm32[:, UI, :], av[:, c, 8:16], True, True, ['cm32', 'av'], [('ps', b1)])
                        mm(PB[b2][:, o:o + 8], cm32[:, US, :], av[:, c, 0:8], True, True, ['cm32', 'av'], [('ps', b2)])
                        mm(PB[b2][:, o + 8:o + 16], cm32[:, LS, :], av[:, c, 8:16], True, True, ['cm32', 'av'], [('ps', b2)])
                        mm(PB[b3][:, o:o + 16], cm32[:, ONES, :], av[:, c, :], True, True, ['cm32', 'av'], [('ps', b3)])
                    for (bb, dst, dk_) in ((b1, ev_, 'ev'), (b2, wg, 'wg'), (b3, eA, 'eA')):
                        act(lambda bb=bb, dst=dst, c0=c0, ncn=ncn: S_.activation(
                            out=dst[:, c0:c0 + ncn, :], in_=PB[bb][:, 0:ncn * 16].rearrange("p (a b) -> p a b", b=16),
                            func=AF.Exp), [('ps', bb)], [dk_])
                dve(lambda: V.tensor_tensor(out=wg, in0=wg, in1=dtv, op=ALU.mult), ['wg', 'dtv'], ['wg'])
                P.barrier()
                chk('P2a' + (kind if 'P2a' in ('P3', 'P4') else ''))
                A.release(M2)
                Sb_all = A.alloc([NSUB, 512], BF16)
                Sst = [A.alloc([512], F32) for _ in range(2)]
                Sf_bf = A.alloc([512], BF16)
                xw = [A.alloc([512], BF16) for _ in range(2)]
                CBm = A.alloc([2, 2, 128], F32)
                aM = A.alloc([16, 128], F32)
                E_sb = A.alloc([16, 128], F32)
                Wt = A.alloc([16, 128], BF16)
                zt = [A.alloc([512], BF16) for _ in range(2)]
                t1 = A.alloc([512], F32)
                t2 = A.alloc([512], F32)
                t3 = A.alloc([512], F32)
                yg = A.alloc([512], F32)
                ssq = A.alloc([4], F32)
                yn = A.alloc([512], BF16)
                ost = [A.alloc([4, 128], BF16) for _ in range(2)]
                junk = A.alloc([256], F32)
                for d in range(2):
                    dve(lambda d=d: V.memset(Sst[d], 0.0), [], [('Sst', d)])
                bw_order = list(range(NSC - 1, -1, -1)) + list(range(NSUB - 1, NSC - 1, -1))

                def state_update(c, d, xwk):
                    x_ = xw[xwk % 2]
                    k_ = ('xw', xwk % 2)
                    dve(lambda: V.tensor_tensor(out=x_.rearrange("p (h e) -> p h e", h=8),
                                                in0=xs_tok[:, c, :].rearrange("p (h e) -> p h e", h=8),
                                                in1=wg[:, c, d * 8:d * 8 + 8, None].to_broadcast([128, 8, 64]), op=ALU.mult),
                        ['xs_tok', 'wg'], [k_])
                    b = nb(4, 6)
                    for g in range(2):
                        mm(PB[b][:, g * 256:(g + 1) * 256], B_tok[:, c, g * 128:(g + 1) * 128], x_[:, g * 256:(g + 1) * 256],
                           True, True, ['B_tok', k_], [('ps', b)])
                    dve(lambda: V.tensor_tensor(out=Sst[d].rearrange("p (h e) -> p h e", h=8),
                                                in0=Sst[d].rearrange("p (h e) -> p h e", h=8),
                                                in1=eA[:, c, d * 8:d * 8 + 8, None].to_broadcast([128, 8, 64]), op=ALU.mult),
                        [('Sst', d), 'eA'], [('Sst', d)])
                    dve(lambda: V.tensor_tensor(out=Sst[d], in0=Sst[d], in1=PB[b], op=ALU.add), [('Sst', d), ('ps', b)],
                        [('Sst', d)])

                for i, c in enumerate(bw_order):
                    act(lambda c=c: S_.copy(out=Sb_all[:, c, :], in_=Sst[1]), [('Sst', 1)], [('Sb_all', c)])
                    if i < len(bw_order) - 1:
                        state_update(c, 1, i)
                for c in range(NSUB):
                    cs_ = slice(c * 128, (c + 1) * 128)
                    z_ = zt[c % 2]
                    zk = ('zt', c % 2)
                    ld(z_, u_tok[c * 128:(c + 1) * 128, TZ:TZ + 512], [('u_tok', c)], [zk])
                    act(lambda: S_.copy(out=Sf_bf, in_=Sst[0]), [('Sst', 0)], ['Sf_bf'])
                    bcb = 6
                    for g in range(2):
                        mm(PB[bcb][:, g * 128:(g + 1) * 128], BCfm[:, g, cs_], BCfm[:, 2 + g, cs_], True, True,
                           [('BCfm', g), ('BCfm', 2 + g)], [('ps', bcb)])
                    for d, mk in ((0, LI), (1, UI)):
                        dve(lambda d=d, mk=mk: V.tensor_tensor(
                            out=CBm[:, d, :, :], in0=PB[bcb][:, 0:256].rearrange("p (g l) -> p g l", g=2),
                            in1=cm32[:, mk, None, :].to_broadcast([128, 2, 128]), op=ALU.mult),
                            [('ps', bcb), 'cm32'], ['CBm'])
                    for d, mk in ((0, US), (1, LS)):
                        pool(lambda d=d, mk=mk, c=c: G.tensor_tensor(
                            out=aM[:, d * 8:d * 8 + 8, :], in0=cm32[:, mk, None, :].to_broadcast([128, 8, 128]),
                            in1=av[:, c, d * 8:d * 8 + 8, None].to_broadcast([128, 8, 128]), op=ALU.mult),
                            ['cm32', 'av'], [('aM', d)])
                    for q4 in range(4):
                        b = q4
                        for jj in range(4):
                            j = q4 * 4 + jj
                            d = j // 8
                            mm(PB[b][:, jj * 128:(jj + 1) * 128], aM[:, j, :], cm32[:, LI if d == 0 else UI, :], True, True,
                               [('aM', d), 'cm32'], [('ps', b)])
                        act(lambda q4=q4, b=b: S_.activation(out=E_sb[:, q4 * 4:q4 * 4 + 4, :],
                                                              in_=PB[b].rearrange("p (a l) -> p a l", a=4), func=AF.Exp),
                            [('ps', b)], [('E_sb', q4)])
                    dve(lambda: V.tensor_tensor(
                        out=E_sb.rearrange("p (a h) l -> p a h l", h=4), in0=E_sb.rearrange("p (a h) l -> p a h l", h=4),
                        in1=CBm.rearrange("p d g l -> p (d g) l")[:, :, None, :].to_broadcast([128, 4, 4, 128]), op=ALU.mult),
                        [('E_sb', q) for q in range(4)] + ['CBm'], [('E_sb', q) for q in range(4)])
                    dve(lambda c=c: V.tensor_tensor(out=Wt, in0=E_sb, in1=dtv[:, c, :, None].to_broadcast([128, 16, 128]),
                                                    op=ALU.mult), [('E_sb', q) for q in range(4)] + ['dtv'], ['Wt'])
                    by = 7
                    for hh in range(8):
                        mm(PB[by][:, hh * 64:(hh + 1) * 64], Wt[:, hh, :], xs_tok[:, c, hh * 64:(hh + 1) * 64], True, False,
                           ['Wt', 'xs_tok'], [('ps', by)])
                        mm(PB[by][:, hh * 64:(hh + 1) * 64], Wt[:, 8 + hh, :], xs_tok[:, c, hh * 64:(hh + 1) * 64], False, True,
                           ['Wt', 'xs_tok'], [('ps', by)])
                    bof, bob = 4, 5
                    for g in range(2):
                        mm(PB[bof][:, g * 256:(g + 1) * 256], BCfm[:, 2 + g, cs_], Sf_bf[:, g * 256:(g + 1) * 256], True, True,
                           [('BCfm', 2 + g), 'Sf_bf'], [('ps', bof)])
                        mm(PB[bob][:, g * 256:(g + 1) * 256], BCfm[:, 2 + g, cs_], Sb_all[:, c, g * 256:(g + 1) * 256], True, True,
                           [('BCfm', 2 + g), ('Sb_all', c)], [('ps', bob)])
                    dve(lambda c=c: V.tensor_tensor(out=t1.rearrange("p (h e) -> p h e", h=8),
                                                    in0=PB[bof].rearrange("p (h e) -> p h e", h=8),
                                                    in1=ev_[:, c, 0:8, None].to_broadcast([128, 8, 64]), op=ALU.mult),
                        [('ps', bof), 'ev'], ['t1'])
                    dve(lambda c=c: V.tensor_tensor(out=t2.rearrange("p (h e) -> p h e", h=8),
                                                    in0=PB[bob].rearrange("p (h e) -> p h e", h=8),
                                                    in1=ev_[:, c, 8:16, None].to_broadcast([128, 8, 64]), op=ALU.mult),
                        [('ps', bob), 'ev'], ['t2'])
                    pool(lambda c=c: G.tensor_tensor(out=t3.rearrange("p (h e) -> p h e", h=8),
                                                     in0=xs_tok[:, c, :].rearrange("p (h e) -> p h e", h=8),
                                                     in1=pbc[:, PB_D:PB_D + 8, None].to_broadcast([128, 8, 64]), op=ALU.mult),
                         ['xs_tok', 'pbc'], ['t3'])
                    pool(lambda: G.tensor_tensor(out=t1, in0=t1, in1=t2, op=ALU.add), ['t1', 't2'], ['t1'])
                    pool(lambda: G.tensor_tensor(out=t1, in0=t1, in1=t3, op=ALU.add), ['t1', 't3'], ['t1'])
                    dve(lambda: V.tensor_tensor(out=t1, in0=t1, in1=PB[by], op=ALU.add), ['t1', ('ps', by)], ['t1'])
                    if c < NSUB - 1:
                        state_update(c, 0, c)
                    act(lambda z_=z_: S_.activation(out=t2, in_=z_, func=AF.Silu), [zk, 't2'], ['t2'])
                    dve(lambda: V.tensor_tensor(out=yg, in0=t1, in1=t2, op=ALU.mult), ['t1', 't2'], ['yg'])
                    for g in range(2):
                        act(lambda g=g: S_.activation(out=junk, in_=yg[:, g * 256:(g + 1) * 256], func=AF.Square,
                                                      accum_out=ssq[:, g:g + 1]), ['yg', 'junk'], ['ssq', 'junk'])
                    act(lambda: S_.activation(out=ssq[:, 2:4], in_=ssq[:, 0:2], func=AF.Sqrt, scale=1.0 / 256, bias=EPS),
                        ['ssq'], ['ssq'])
                    dve(lambda: V.reciprocal(out=ssq[:, 2:4], in_=ssq[:, 2:4]), ['ssq'], ['ssq'])
                    for g in range(2):
                        dve(lambda g=g: V.scalar_tensor_tensor(out=yn[:, g * 256:(g + 1) * 256], in0=yg[:, g * 256:(g + 1) * 256],
                                                               scalar=ssq[:, 2 + g:3 + g], in1=gS[:, g * 256:(g + 1) * 256],
                                                               op0=ALU.mult, op1=ALU.mult), ['yg', 'ssq', 'gS'], ['yn'])
                    bt = 6
                    for k in range(4):
                        pe(lambda k=k: T.transpose(out=PBH[bt][:, 512 + k * 128:512 + (k + 1) * 128],
                                                   in_=yn[:, k * 128:(k + 1) * 128], identity=ident),
                           ['yn', 'ident'], [('ps', bt)])
                    o_ = ost[c % 2]
                    okk = ('ost', c % 2)
                    act(lambda o_=o_: S_.copy(out=o_, in_=PBH[bt][:, 512:1024].rearrange("p (k t) -> p k t", k=4)),
                        [('ps', bt)], [okk])
                    stq(brd[0][:, cs_].rearrange("(k p) t -> p k t", p=128), o_, [okk], [('br0', c)])
                P.barrier()
                chk('P2' + (kind if 'P2' in ('P3', 'P4') else ''))

                for kind in ('gqa', 'diff'):
                    A.release(PERSIST)
                    NH = 10 if kind == 'gqa' else 16
                    NQ = 8
                    NT = 6 if kind == 'gqa' else 8
                    c_in = TGQ if kind == 'gqa' else TDQ
                    w_in_cols = 640 if kind == 'gqa' else 1024
                    gq_, gk_n = (gqk, 'gqk') if kind == 'gqa' else (gdk, 'gdk')
                    qkT = A.alloc([NT, NTOK], BF16)
                    if kind == 'gqa':
                        vaug = A.alloc([NSUB, 2, 128], BF16)
                        pool(lambda: G.memset(vaug, 1.0), [], ['vaug'])
                        vtmp = [A.alloc([128], BF16) for _ in range(2)]
                    else:
                        vd = A.alloc([NSUB, 512], BF16)
                        ld(vd, u_tok[:, TDV:TDV + 512].rearrange("(s p) c -> p s c", p=128),
                           [('u_tok', i) for i in range(NSUB)], ['vd'])
                    qin = [A.alloc([NH, 64], BF16) for _ in range(2)]
                    sqf = A.alloc([NH, 64], F32)
                    qn = A.alloc([NH, 64], F32)
                    ssn = A.alloc([2, NH], F32)
                    ta = A.alloc([NH, 32], F32)
                    tb = A.alloc([NH, 32], F32)
                    tc_ = A.alloc([NH, 32], F32)
                    td = A.alloc([NH, 32], F32)
                    qr = [A.alloc([NH + 2, 64], BF16) for _ in range(2)]
                    for c in range(NSUB):
                        qi = qin[c % 2]
                        qik = ('qin', c % 2)
                        q_ = qr[c % 2]
                        qrk = ('qr', c % 2)
                        ld(qi, u_tok[c * 128:(c + 1) * 128, c_in:c_in + w_in_cols].rearrange("p (h e) -> p h e", e=64),
                           [('u_tok', c)], [qik])
                        if kind == 'gqa':
                            vt = vtmp[c % 2]
                            vk = ('vtmp', c % 2)
                            ld(vt, u_tok[c * 128:(c + 1) * 128, TGV:TGV + 128], [('u_tok', c)], [vk])
                            pool(lambda vt=vt, c=c: G.tensor_copy(out=vaug[:, c, :, 0:64], in_=vt.rearrange("p (g e) -> p g e", g=2)),
                                 [vk, 'vaug'], ['vaug'])
                        act(lambda qi=qi: S_.activation(out=sqf, in_=qi, func=AF.Square), [qik], ['sqf'])
                        dve(lambda: V.tensor_reduce(out=ssn[:, 0, :], in_=sqf, axis=AX.X, op=ALU.add), ['sqf'], ['ssn'])
                        act(lambda: S_.activation(out=ssn[:, 1, :], in_=ssn[:, 0, :], func=AF.Sqrt, scale=1.0 / 64, bias=EPS),
                            ['ssn'], ['ssn'])
                        dve(lambda: V.reciprocal(out=ssn[:, 1, :], in_=ssn[:, 1, :]), ['ssn'], ['ssn'])
                        dve(lambda qi=qi: V.tensor_tensor(out=qn, in0=qi, in1=ssn[:, 1, :, None].to_broadcast([128, NH, 64]),
                                                          op=ALU.mult), [qik, 'ssn'], ['qn'])
                        if c >= NSC:
                            cl = c - NSC
                            pool(lambda: G.tensor_tensor(out=qn, in0=qn, in1=gq_, op=ALU.mult), ['qn', gk_n], ['qn'])
                            csb = rc[:, cl, None, :].to_broadcast([128, NH, 32])
                            snb = rs[:, cl, None, :].to_broadcast([128, NH, 32])
                            dve(lambda csb=csb: V.tensor_tensor(out=ta, in0=qn[:, :, 0:32], in1=csb, op=ALU.mult), ['qn', 'rc'], ['ta'])
                            pool(lambda snb=snb: G.tensor_tensor(out=tb, in0=qn[:, :, 32:64], in1=snb, op=ALU.mult), ['qn', 'rs'], ['tb'])
                            pool(lambda snb=snb: G.tensor_tensor(out=tc_, in0=qn[:, :, 0:32], in1=snb, op=ALU.mult), ['qn', 'rs'], ['tc'])
                            dve(lambda csb=csb: V.tensor_tensor(out=td, in0=qn[:, :, 32:64], in1=csb, op=ALU.mult), ['qn', 'rc'], ['td'])
                            dve(lambda q_=q_: V.tensor_tensor(out=q_[:, 0:NH, 0:32], in0=ta, in1=tb, op=ALU.subtract),
                                ['ta', 'tb'], [qrk])
                            pool(lambda q_=q_: G.tensor_tensor(out=q_[:, 0:NH, 32:64], in0=tc_, in1=td, op=ALU.add),
                                 ['tc', 'td', qrk], [qrk])
                        else:
                            pool(lambda q_=q_: G.tensor_tensor(out=q_[:, 0:NH, :], in0=qn, in1=gq_, op=ALU.mult), ['qn', gk_n], [qrk])
                        bt = nb(0, 6)
                        if kind == 'gqa':
                            pool(lambda q_=q_: G.tensor_copy(out=q_[:, 10:12, :], in_=q_[:, 9:10, :].to_broadcast([128, 2, 64])),
                                 [qrk], [qrk])
                            pool(lambda q_=q_: G.tensor_copy(out=q_[:, 9:10, :], in_=q_[:, 8:9, :]), [qrk], [qrk])
                        for k in range(NT):
                            pe(lambda k=k, q_=q_, bt=bt: T.transpose(out=PBH[bt][:, k * 128:(k + 1) * 128],
                                                                     in_=q_[:, 2 * k:2 * k + 2, :], identity=ident),
                               [qrk, 'ident'], [('ps', bt)])
                        act(lambda bt=bt, c=c: S_.copy(out=qkT[:, :, c * 128:(c + 1) * 128],
                                                       in_=PBH[bt][:, 0:NT * 128].rearrange("p (k t) -> p k t", k=NT)),
                            [('ps', bt)], [('qkT', c)])
                    QKT = [('qkT', c) for c in range(NSUB)]
                    P.barrier()
                    chk('P3' + (kind if 'P3' in ('P3', 'P4') else ''))
                    MA = A.mark()
                    qblocks = ([] if last else [(0, NCTX, NSC)]) + [(NCTX + 512 * i, 512, NSUB) for i in range(NLAT // 512)]
                    pT = [A.alloc([512], BF16) for _ in range(8)]
                    oT = [A.alloc([4, 512], BF16) for _ in range(2)]
                    rsb = [A.alloc([512], F32) for _ in range(2)]
                    if kind == 'diff':
                        o1 = A.alloc([512], F32)
                        o2 = A.alloc([512], F32)
                        sqd = A.alloc([512], BF16)
                    ptc = [0]
                    for qi_, (q0, QW, nkt) in enumerate(qblocks):
                        o_ = oT[qi_ % 2]
                        ok_ = ('oT', qi_ % 2)
                        qs = slice(q0, q0 + QW)
                        if kind == 'gqa':
                            for hh in range(8):
                                hf, j, g = hh % 2, hh // 2, hh // 4
                                pp = slice(hf * 64, hf * 64 + 64)
                                bo = 4 + hh % 2
                                pq = []
                                for kt in range(nkt + 2):
                                    if kt < nkt:
                                        bs = kt % 3
                                        ks = slice(kt * 128, (kt + 1) * 128)
                                        mm(PB[bs][:, 0:QW], qkT[pp, 4 + g, ks], qkT[pp, j, qs], True, True, QKT, [('ps', bs)])
                                        p_ = pT[ptc[0] % 8]
                                        pk = ('pT', ptc[0] % 8)
                                        ptc[0] += 1
                                        act(lambda p_=p_, bs=bs: S_.activation(out=p_[:, 0:QW], in_=PB[bs][:, 0:QW], func=AF.Exp,
                                                                               scale=0.125), [('ps', bs)], [pk])
                                        pq.append((kt, p_, pk))
                                    if kt >= 2:
                                        kt2, p2, pk2 = pq.pop(0)
                                        mm(PB[bo][:, 0:QW], vaug[:, kt2, g, :], p2[:, 0:QW], kt2 == 0, kt2 == nkt - 1,
                                           ['vaug', pk2], [('ps', bo)])
                                r_ = rsb[hh % 2]
                                rk_ = ('rsb', hh % 2)
                                dve(lambda r_=r_, bo=bo: V.reciprocal(out=r_[0:64, 0:QW], in_=PB[bo][64:128, 0:QW]), [('ps', bo)], [rk_])
                                dve(lambda r_=r_, bo=bo, o_=o_, pp=pp, j=j: V.tensor_tensor(
                                    out=o_[pp, j, 0:QW], in0=PB[bo][0:64, 0:QW], in1=r_[0:64, 0:QW], op=ALU.mult),
                                    [('ps', bo), rk_, ok_], [ok_])
                            stq(brd[1][:, qs].rearrange("(k p) t -> p k t", p=128), o_[:, :, 0:QW], [ok_],
                                [('br1', i) for i in range(q0 // 128, (q0 + QW) // 128)])
                        else:
                            for hh in range(4):
                                bo = [4, 5]
                                bsu = [6, 7]
                                pq = [[], []]
                                for kt in range(nkt + 2):
                                    for cc in range(2):
                                        pp = slice(cc * 64, cc * 64 + 64)
                                        if kt >= 2:
                                            kt2, p2, pk2 = pq[cc].pop(0)
                                            mm(PB[bo[cc]][:, 0:QW], vd[:, kt2, hh * 128:(hh + 1) * 128], p2[:, 0:QW],
                                               kt2 == 0, kt2 == nkt - 1, ['vd', pk2], [('ps', bo[cc])])
                                            mm(PB[bsu[cc]][:, 0:QW], ones_bf, p2[:, 0:QW], kt2 == 0, kt2 == nkt - 1,
                                               ['ones_bf', pk2], [('ps', bsu[cc])])
                                        if kt < nkt:
                                            bs = cc * 2 + kt % 2
                                            ks = slice(kt * 128, (kt + 1) * 128)
                                            mm(PB[bs][:, 0:QW], qkT[pp, 4 + hh, ks], qkT[pp, hh, qs], True, True, QKT, [('ps', bs)])
                                            p_ = pT[ptc[0] % 8]
                                            pk = ('pT', ptc[0] % 8)
                                            ptc[0] += 1
                                            act(lambda p_=p_, bs=bs: S_.activation(out=p_[:, 0:QW], in_=PB[bs][:, 0:QW],
                                                                                   func=AF.Exp, scale=0.125), [('ps', bs)], [pk])
                                            pq[cc].append((kt, p_, pk))
                                for cc, ox in ((0, o1), (1, o2)):
                                    r_ = rsb[cc]
                                    rk_ = ('rsb', cc)
                                    dve(lambda r_=r_, cc=cc: V.reciprocal(out=r_[:, 0:QW], in_=PB[bsu[cc]][:, 0:QW]),
                                        [('ps', bsu[cc])], [rk_])
                                    dve(lambda r_=r_, cc=cc, ox=ox: V.tensor_tensor(out=ox[:, 0:QW], in0=PB[bo[cc]][:, 0:QW],
                                                                                   in1=r_[:, 0:QW], op=ALU.mult),
                                        [('ps', bo[cc]), rk_], [('o12', cc)])
                                dve(lambda: V.scalar_tensor_tensor(out=o1[:, 0:QW], in0=o2[:, 0:QW], scalar=lamt[:, 0:1],
                                                                   in1=o1[:, 0:QW], op0=ALU.mult, op1=ALU.add),
                                    [('o12', 0), ('o12', 1), 'lamt'], [('o12', 0)])
                                act(lambda: S_.activation(out=sqd[:, 0:QW], in_=o1[:, 0:QW], func=AF.Square), [('o12', 0)], ['sqd'])
                                b3 = 3
                                mm(PB[b3][:, 0:QW], ones_bf, sqd[:, 0:QW], True, True, ['ones_bf', 'sqd'], [('ps', b3)])
                                act(lambda: S_.activation(out=o2[:, 0:QW], in_=PB[b3][:, 0:QW], func=AF.Sqrt, scale=1.0 / 128,
                                                          bias=EPS), [('ps', b3), ('o12', 1)], [('o12', 1)])
                                dve(lambda: V.reciprocal(out=o2[:, 0:QW], in_=o2[:, 0:QW]), [('o12', 1)], [('o12', 1)])
                                dve(lambda hh=hh, o_=o_: V.scalar_tensor_tensor(out=o_[:, hh, 0:QW], in0=o1[:, 0:QW],
                                                                                scalar=lamt[:, 2:3], in1=o2[:, 0:QW],
                                                                                op0=ALU.mult, op1=ALU.mult),
                                    [('o12', 0), ('o12', 1), 'lamt', ok_], [ok_])
                            stq(brd[2][:, qs].rearrange("(k p) t -> p k t", p=128), o_[:, :, 0:QW], [ok_],
                                [('br2', i) for i in range(q0 // 128, (q0 + QW) // 128)])
                    P.barrier()
                    chk('P4' + (kind if 'P4' in ('P3', 'P4') else ''))

                A.release(PERSIST)
                rgw32 = A.alloc([16, 128], F32)
                rgwb = A.alloc([16, 128], BF16)
                ld(rgw32, rgwd[l], [], ['rgw32'])
                pool(lambda: G.tensor_copy(out=rgwb, in_=rgw32), ['rgw32'], ['rgwb'])
                rawx = A.alloc([NTOK + 6], BF16)
                pool(lambda: G.memset(rawx, 0.0), [], ['rawx'])
                xr = A.alloc([NTOK], F32)
                xrb = A.alloc([NTOK], BF16)
                a_all = A.alloc([NTOK], F32)
                b_all = A.alloc([NTOK], F32)
                hsum = A.alloc([NTOK], F32)
                hb = A.alloc([NTOK], F32)
                rgg = A.alloc([NTOK], BF16)
                rgo = A.alloc([NTOK], BF16)
                tg1 = A.alloc([512], F32)
                tg2 = A.alloc([512], F32)
                tg3 = A.alloc([512], F32)
                for j in range(4):
                    for (g0, gl, c0) in SEG:
                        ld(rawx[:, c0:c0 + gl], u_fm[(12 + j) * 128:(13 + j) * 128, g0:g0 + gl], [('u_fm', 3)], ['rawx'])
                    ld(rgg, u_fm[(8 + j) * 128:(9 + j) * 128, :], [('u_fm', 2)], ['rgg'])
                    for (g0, gl, c0) in SEG:
                        for k in range(4):
                            src = rawx[:, c0 + k - 1:c0 + k - 1 + gl]
                            wk_ = pf[:, PF_RCW + j * 4 + k:PF_RCW + j * 4 + k + 1]
                            if k == 0:
                                dve(lambda src=src, wk_=wk_, g0=g0, gl=gl, j=j: V.tensor_scalar(
                                    out=xr[:, g0:g0 + gl], in0=src, scalar1=wk_, scalar2=pf[:, PF_RCB + j:PF_RCB + j + 1],
                                    op0=ALU.mult, op1=ALU.add), ['rawx', 'pf'], ['xr'])
                            else:
                                dve(lambda src=src, wk_=wk_, g0=g0, gl=gl: V.scalar_tensor_tensor(
                                    out=xr[:, g0:g0 + gl], in0=src, scalar=wk_, in1=xr[:, g0:g0 + gl],
                                    op0=ALU.mult, op1=ALU.add), ['rawx', 'pf', 'xr'], ['xr'])
                    act(lambda: S_.copy(out=xrb, in_=xr), ['xr'], ['xrb'])
                    for d in range(2):
                        for (t0, W) in tiles_of():
                            ts_ = slice(t0, t0 + W)
                            ba_, bx_ = nb(0, 4), nb(4, 8)
                            mm(PB[ba_][:, 0:W], rgwb[:, (0 * 2 + d) * 4 + j, :], xrb[:, ts_], True, True, ['rgwb', 'xrb'], [('ps', ba_)])
                            mm(PB[bx_][:, 0:W], rgwb[:, (1 * 2 + d) * 4 + j, :], xrb[:, ts_], True, True, ['rgwb', 'xrb'], [('ps', bx_)])
                            cba = pf[:, PF_RBA + d * 4 + j:PF_RBA + d * 4 + j + 1]
                            cbx = pf[:, PF_RBX + d * 4 + j:PF_RBX + d * 4 + j + 1]
                            ccp = rgcp[:, d * 4 + j:d * 4 + j + 1]
                            act(lambda W=W, ba_=ba_, cba=cba: S_.activation(out=tg1[:, 0:W], in_=PB[ba_][:, 0:W], func=AF.Sigmoid,
                                                                            bias=cba), [('ps', ba_), 'pf'], ['tg1'])
                            act(lambda W=W, ts_=ts_, ccp=ccp: S_.activation(out=a_all[:, ts_], in_=tg1[:, 0:W], func=AF.Exp,
                                                                            scale=ccp), ['tg1', 'rgcp'], ['a_all'])
                            act(lambda W=W, bx_=bx_, cbx=cbx: S_.activation(out=tg2[:, 0:W], in_=PB[bx_][:, 0:W], func=AF.Sigmoid,
                                                                            bias=cbx), [('ps', bx_), 'pf'], ['tg2'])
                            dve(lambda W=W, ts_=ts_: V.tensor_tensor(out=tg2[:, 0:W], in0=tg2[:, 0:W], in1=xr[:, ts_], op=ALU.mult),
                                ['tg2', 'xr'], ['tg2'])
                            pool(lambda W=W, ts_=ts_: G.tensor_tensor(out=tg3[:, 0:W], in0=a_all[:, ts_], in1=a_all[:, ts_],
                                                                      op=ALU.mult), ['a_all', 'tg3'], ['tg3'])
                            act(lambda W=W: S_.activation(out=tg3[:, 0:W], in_=tg3[:, 0:W], func=AF.Sqrt, scale=-1.0, bias=1.0),
                                ['tg3'], ['tg3'])
                            dve(lambda W=W, ts_=ts_: V.tensor_tensor(out=b_all[:, ts_], in0=tg2[:, 0:W], in1=tg3[:, 0:W], op=ALU.mult),
                                ['tg2', 'tg3'], ['b_all'])
                        if d == 0:
                            dve(lambda: V.tensor_tensor_scan(out=hsum, data0=a_all, data1=b_all, initial=0.0, op0=ALU.mult,
                                                             op1=ALU.add), ['a_all', 'b_all'], ['hsum'])
                        else:
                            dve(lambda: V.tensor_tensor_scan(out=hb[:, 0:NCTX][:, ::-1], data0=a_all[:, 0:NCTX][:, ::-1],
                                                             data1=b_all[:, 0:NCTX][:, ::-1], initial=0.0, op0=ALU.mult,
                                                             op1=ALU.add), ['a_all', 'b_all'], ['hb'])
                            dve(lambda: V.tensor_tensor_scan(out=hb[:, NCTX:NTOK][:, ::-1], data0=a_all[:, NCTX:NTOK][:, ::-1],
                                                             data1=b_all[:, NCTX:NTOK][:, ::-1], initial=hb[:, 0:1],
                                                             op0=ALU.mult, op1=ALU.add), ['a_all', 'b_all', 'hb'], ['hb'])
                            pool(lambda: G.tensor_tensor(out=hsum, in0=hsum, in1=hb, op=ALU.add), ['hsum', 'hb'], ['hsum'])
                    pool(lambda: G.tensor_tensor(out=a_all, in0=rgg, in1=rgg, op=ALU.mult), ['rgg', 'a_all'], ['a_all'])
                    dve(lambda: V.tensor_scalar(out=a_all, in0=a_all, scalar1=0.044715, scalar2=1.0, op0=ALU.mult, op1=ALU.add),
                        ['a_all'], ['a_all'])
                    dve(lambda: V.tensor_tensor(out=a_all, in0=a_all, in1=rgg, op=ALU.mult), ['a_all', 'rgg'], ['a_all'])
                    act(lambda: S_.activation(out=b_all, in_=a_all, func=AF.Sigmoid, scale=1.5957691216057308),
                        ['a_all', 'b_all'], ['b_all'])
                    pool(lambda: G.tensor_tensor(out=b_all, in0=b_all, in1=rgg, op=ALU.mult), ['b_all', 'rgg'], ['b_all'])
                    dve(lambda: V.tensor_tensor(out=rgo, in0=b_all, in1=hsum, op=ALU.mult), ['b_all', 'hsum'], ['rgo'])
                    stq(brd[3][j * 128:(j + 1) * 128, :], rgo, ['rgo'], [('br3', j)])
                P.barrier()
                chk('P6' + (kind if 'P6' in ('P3', 'P4') else ''))

                A.release(PERSIST)
                wgt_ = A.alloc([4, 8, D], BF16)
                wbr_ = A.alloc([4, 4, D], BF16)
                wo_ = A.alloc([8, D], BF16)
                M7 = A.mark()
                stg7 = [A.alloc([4 * D], F32) for _ in range(2)]
                dsts, srcs = [], []
                for k in range(4):
                    for hf in range(2):
                        dsts.append(wgt_[:, k, hf * 4:(hf + 1) * 4, :])
                        srcs.append(w_gate[l, k, hf * 512:(hf + 1) * 512, :].rearrange("(c p) n -> p c n", p=128))
                for k in range(4):
                    dsts.append(wbr_[:, k, :, :])
                    srcs.append(w_br[l, k].rearrange("(c p) n -> p c n", p=128))
                for hf in range(2):
                    dsts.append(wo_[:, hf * 4:(hf + 1) * 4, :])
                    srcs.append(w_out[l, hf * 512:(hf + 1) * 512, :].rearrange("(c p) n -> p c n", p=128))
                load_weight(dsts, srcs, stg7, 'w7')
                W7 = [('w7', i) for i in range(len(dsts))]
                P.barrier()
                chk('P7w' + (kind if 'P7w' in ('P3', 'P4') else ''))
                A.release(M7)
                xt7 = [A.alloc([8, 256], F32) for _ in range(2)]
                sq = A.alloc([8, 256], BF16)
                rstd = A.alloc([256], F32)
                tmpn[0] = A.alloc([256], F32)
                tmpn[1] = A.alloc([256], F32)
                h7 = A.alloc([8, 256], BF16)
                ob = [A.alloc([4, 256], BF16) for _ in range(2)]
                macc = A.alloc([8, 256], F32)
                mbf = A.alloc([8, 256], BF16)
                sg2 = [A.alloc([256], F32) for _ in range(2)]
                tm2 = [A.alloc([256], F32) for _ in range(2)]
                cnt7 = [0]
                for ti, (t0, W) in enumerate(tiles_of(not last, 256)):
                    r = NB if t0 < NCTX else s
                    xt = xt7[ti % 2]
                    xk = ('xt', ti % 2)
                    ts_ = slice(t0, t0 + W)
                    ld(xt[:, :, 0:W], xsrc[:, ts_].rearrange("(c p) t -> p c t", p=128), [('xs', s)], [xk])
                    norm_mod(xt, xk, W, sq, rstd, h7, 'h7', A1, B1, r, '')
                    for k in range(4):
                        o_ = ob[k % 2]
                        obk = ('ob', k % 2)
                        ld(o_[:, :, 0:W], brd[k][:, ts_].rearrange("(c p) t -> p c t", p=128),
                           [(f'br{k}', i) for i in range(NSUB if k != 3 else 4)], [obk])
                        for oc in range(8):
                            ocs = slice(oc * 128, (oc + 1) * 128)
                            bg_, bb_ = nb(0, 4), nb(4, 8)
                            for kc in range(8):
                                mm(PB[bg_][:, 0:W], wgt_[:, k, kc, ocs], h7[:, kc, 0:W], kc == 0, kc == 7, W7 + ['h7'], [('ps', bg_)])
                            for kc in range(4):
                                mm(PB[bb_][:, 0:W], wbr_[:, k, kc, ocs], o_[:, kc, 0:W], kc == 0, kc == 3, W7 + [obk], [('ps', bb_)])
                            cnt7[0] += 1
                            sg = sg2[cnt7[0] % 2]
                            sgk = ('sg', cnt7[0] % 2)
                            act(lambda sg=sg, bg_=bg_, k=k, oc=oc: S_.activation(
                                out=sg[:, 0:W], in_=PB[bg_][:, 0:W], func=AF.Sigmoid,
                                bias=pf[:, PF_BG + k * 8 + oc:PF_BG + k * 8 + oc + 1]), [('ps', bg_), 'pf'], [sgk])
                            if k == 0:
                                dve(lambda sg=sg, bb_=bb_, oc=oc: V.tensor_tensor(out=macc[:, oc, 0:W], in0=PB[bb_][:, 0:W],
                                                                                  in1=sg[:, 0:W], op=ALU.mult),
                                    [('ps', bb_), sgk], [('macc', oc)])
                            else:
                                tm = tm2[cnt7[0] % 2]
                                tmk = ('tm', cnt7[0] % 2)
                                dve(lambda sg=sg, bb_=bb_, tm=tm: V.tensor_tensor(out=tm[:, 0:W], in0=PB[bb_][:, 0:W],
                                                                                  in1=sg[:, 0:W], op=ALU.mult),
                                    [('ps', bb_), sgk], [tmk])
                                pool(lambda tm=tm, oc=oc: G.tensor_tensor(out=macc[:, oc, 0:W], in0=macc[:, oc, 0:W],
                                                                          in1=tm[:, 0:W], op=ALU.add), [tmk, ('macc', oc)],
                                     [('macc', oc)])
                    act(lambda: S_.copy(out=mbf[:, :, 0:W], in_=macc[:, :, 0:W]), [('macc', oc) for oc in range(8)], ['mbf'])
                    for oc in range(8):
                        ocs = slice(oc * 128, (oc + 1) * 128)
                        b = nb(0, 8)
                        for kc in range(8):
                            mm(PB[b][:, 0:W], wo_[:, kc, ocs], mbf[:, kc, 0:W], kc == 0, kc == 7, W7 + ['mbf'], [('ps', b)])
                        dve(lambda oc=oc, b=b, xt=xt: V.scalar_tensor_tensor(out=xt[:, oc, 0:W], in0=PB[b][:, 0:W],
                                                                             scalar=G1[:, oc, r:r + 1], in1=xt[:, oc, 0:W],
                                                                             op0=ALU.mult, op1=ALU.add),
                            [('ps', b), 'mods', xk], [xk])
                    stq(xm[:, ts_].rearrange("(c p) t -> p c t", p=128), xt[:, :, 0:W], [xk], ['xm'])
                P.barrier()
                chk('P7' + (kind if 'P7' in ('P3', 'P4') else ''))

                A.release(PERSIST)
                wup = A.alloc([8, 2 * DFF], BF16)
                wdn = A.alloc([22, D], BF16)
                M8 = A.mark()
                stg8 = [A.alloc([4 * D], F32) for _ in range(2)]
                dsts, srcs = [], []
                for kc in range(8):
                    for q in range(2):
                        dsts.append(wup[:, kc, q * DFF:(q + 1) * DFF])
                        srcs.append(w_up[l, kc * 128:(kc + 1) * 128, q * DFF:(q + 1) * DFF])
                for q in range(11):
                    dsts.append(wdn[:, 2 * q:2 * q + 2, :])
                    srcs.append(w_down[l, q * 256:(q + 1) * 256, :].rearrange("(c p) n -> p c n", p=128))
                load_weight(dsts, srcs, stg8, 'w8')
                W8 = [('w8', i) for i in range(len(dsts))]
                P.barrier()
                chk('P8w' + (kind if 'P8w' in ('P3', 'P4') else ''))
                A.release(M8)
                xt8 = A.alloc([8, 256], F32)
                sq = A.alloc([8, 256], BF16)
                rstd = A.alloc([256], F32)
                tmpn[0] = A.alloc([256], F32)
                tmpn[1] = A.alloc([256], F32)
                h8 = A.alloc([8, 256], BF16)
                actb = A.alloc([22, 256], BF16)
                cg2 = [A.alloc([256], F32) for _ in range(2)]
                cv2 = [A.alloc([256], F32) for _ in range(2)]
                dve(lambda: V.memset(xt8, 1.0), [], ['xt8'])
                ftiles = []
                for (sg0, sg1) in ([] if last else [(0, NCTX)]) + [(NCTX, NTOK)]:
                    ln = sg1 - sg0
                    nlt = max(1, -(-ln // 254))
                    base = ln // nlt
                    o0 = sg0
                    for i in range(nlt):
                        wo = base + (1 if i < ln - base * nlt else 0)
                        ftiles.append((o0, wo, sg0, sg1))
                        o0 += wo
                dst_x = outd[s] if last else xs[s]
                for (o0, Wo, sg0, sg1) in ftiles:
                    Ww = Wo + 2
                    lo = max(o0 - 1, sg0)
                    hi = min(o0 + Wo + 1, sg1)
                    cl = lo - (o0 - 1)
                    ld(xt8[:, :, cl:cl + hi - lo], xm[:, lo:hi].rearrange("(c p) t -> p c t", p=128), ['xm'], ['xt8'])
                    r = NB if o0 < NCTX else s
                    norm_mod(xt8, 'xt8', Ww, sq, rstd, h8, 'h8', A2, B2, r, '')
                    if cl > 0:
                        pool(lambda: G.memset(h8[:, :, 0:1], 0.0), ['h8'], ['h8'])
                    if hi < o0 + Wo + 1:
                        pool(lambda Ww=Ww: G.memset(h8[:, :, Ww - 1:Ww], 0.0), ['h8'], ['h8'])
                    for i in range(22):
                        res = []
                        for half, buf in ((0, cg2), (1, cv2)):
                            ch = half * 22 + i
                            c0 = ch * 128
                            b = nb(0, 8)
                            for kc in range(8):
                                mm(PB[b][:, 0:Ww], wup[:, kc, c0:c0 + 128], h8[:, kc, 0:Ww], kc == 0, kc == 7, W8 + ['h8'], [('ps', b)])
                            cb_ = buf[i % 2]
                            cbk = ('c%d' % half, i % 2)
                            fw = lambda k, ch=ch: pf[:, PF_FCW + ch * 3 + k:PF_FCW + ch * 3 + k + 1]
                            act(lambda cb_=cb_, b=b, ch=ch, fw=fw: S_.activation(
                                out=cb_[:, 0:Wo], in_=PB[b][:, 1:1 + Wo], func=AF.Identity, scale=fw(1),
                                bias=pf[:, PF_FCB + ch:PF_FCB + ch + 1]), [('ps', b), 'pf'], [cbk])
                            dve(lambda cb_=cb_, b=b, fw=fw: V.scalar_tensor_tensor(out=cb_[:, 0:Wo], in0=PB[b][:, 0:Wo], scalar=fw(0),
                                                                                   in1=cb_[:, 0:Wo], op0=ALU.mult, op1=ALU.add),
                                [('ps', b), 'pf', cbk], [cbk])
                            dve(lambda cb_=cb_, b=b, fw=fw: V.scalar_tensor_tensor(out=cb_[:, 0:Wo], in0=PB[b][:, 2:2 + Wo],
                                                                                   scalar=fw(2), in1=cb_[:, 0:Wo], op0=ALU.mult,
                                                                                   op1=ALU.add), [('ps', b), 'pf', cbk], [cbk])
                            res.append((cb_, cbk))
                        (cgb, cgk), (cvb, cvk) = res
                        act(lambda cgb=cgb: S_.activation(out=cgb[:, 0:Wo], in_=cgb[:, 0:Wo], func=AF.Silu), [cgk], [cgk])
                        pool(lambda cgb=cgb, cvb=cvb, i=i: G.tensor_tensor(out=actb[:, i, 0:Wo], in0=cgb[:, 0:Wo], in1=cvb[:, 0:Wo],
                                                                           op=ALU.mult), [cgk, cvk], [('actb', i)])
                    AK = [('actb', i) for i in range(22)]
                    for oc in range(8):
                        ocs = slice(oc * 128, (oc + 1) * 128)
                        b = nb(0, 8)
                        for kc in range(22):
                            mm(PB[b][:, 0:Wo], wdn[:, kc, ocs], actb[:, kc, 0:Wo], kc == 0, kc == 21, W8 + AK, [('ps', b)])
                        dve(lambda oc=oc, b=b: V.scalar_tensor_tensor(out=xt8[:, oc, 1:1 + Wo], in0=PB[b][:, 0:Wo],
                                                                      scalar=G2[:, oc, r:r + 1], in1=xt8[:, oc, 1:1 + Wo],
                                                                      op0=ALU.mult, op1=ALU.add), [('ps', b), 'mods', 'xt8'], ['xt8'])
                    od0 = o0 - NCTX if last else o0
                    stq(dst_x[:, od0:od0 + Wo].rearrange("(c p) t -> p c t", p=128), xt8[:, :, 1:1 + Wo], ['xt8'],
                        [('xs', s), 'out'])
                P.barrier()
                chk('P8' + (kind if 'P8' in ('P3', 'P4') else ''))
        try:
            for l_ in range(DEPTH):
                emit_layer(l_)
        except StopBuild:
            P.barrier()
        P.op('sp', lambda: SY.nop(), reads=['out'])
        info = P.emit(lambda name: st.enter_context(nc.semaphore(name)))
    return nc, info


def _consts(NLAT):
    m = np.arange(128)[:, None]
    l_ = np.arange(128)[None, :]
    cm = np.stack([(m <= l_), (m < l_), (m >= l_), (m > l_), np.ones((128, 128), bool), (m == l_)], 1).astype(np.float32)
    t = np.arange(NLAT)
    row = (t // 64).astype(np.float32)
    col = (t % 64).astype(np.float32)
    inv = (10000.0 ** (-np.arange(16, dtype=np.float32) / 16)).astype(np.float32)
    ang = np.concatenate([row[:, None] * inv, col[:, None] * inv], -1).astype(np.float32)
    cs = np.cos(ang).astype(np.float32).reshape(NLAT // 128, 128, 32).transpose(1, 0, 2)
    sn = np.sin(ang).astype(np.float32).reshape(NLAT // 128, 128, 32).transpose(1, 0, 2)
    return np.ascontiguousarray(cm), np.ascontiguousarray(cs), np.ascontiguousarray(sn)


def _fm(v, n):
    return np.ascontiguousarray(np.asarray(v, np.float32).reshape(n, 128).T)


def _pack_params(inp, DEPTH):
    pf = np.zeros((DEPTH, 128, NPF), np.float32)
    pb = np.zeros((DEPTH, NPB), np.float32)
    rgw = np.zeros((DEPTH, 128, 16, 128), np.float32)
    for l in range(DEPTH):
        pf[l, :, PF_BADA:PF_BADA + 48] = _fm(inp['b_ada'][l], 48)
        pf[l, :, PF_N1:PF_N1 + 8] = _fm(inp['norm1_g'][l], 8)
        pf[l, :, PF_N2:PF_N2 + 8] = _fm(inp['norm2_g'][l], 8)
        scw = np.asarray(inp['ssd_conv_w'][l])
        pf[l, :, PF_SCW:PF_SCW + 32] = scw.reshape(4, 8, 128).transpose(2, 1, 0).reshape(128, 32)
        pf[l, :, PF_SCB:PF_SCB + 8] = _fm(inp['ssd_conv_b'][l], 8)
        pf[l, :, PF_SUB] = np.asarray(inp['diff_subln_g'][l])
        rcw = np.asarray(inp['rg_conv_w'][l])
        pf[l, :, PF_RCW:PF_RCW + 16] = rcw.reshape(4, 4, 128).transpose(2, 1, 0).reshape(128, 16)
        pf[l, :, PF_RCB:PF_RCB + 4] = _fm(inp['rg_conv_b'][l], 4)
        for nm, off in (('rg_ba', PF_RBA), ('rg_bx', PF_RBX), ('rg_lambda', PF_RLM)):
            v = np.asarray(inp[nm][l])
            pf[l, :, off:off + 8] = v.reshape(2, 4, 128).transpose(2, 0, 1).reshape(128, 8)
        bg = np.asarray(inp['b_gate'][l])
        pf[l, :, PF_BG:PF_BG + 32] = bg.reshape(4, 8, 128).transpose(2, 0, 1).reshape(128, 32)
        fcw = np.asarray(inp['ffn_conv_w'][l])
        pf[l, :, PF_FCW:PF_FCW + 132] = fcw.reshape(3, 44, 128).transpose(2, 1, 0).reshape(128, 132)
        pf[l, :, PF_FCB:PF_FCB + 44] = _fm(inp['ffn_conv_b'][l], 44)
        pb[l, PB_DTB:PB_DTB + 16] = np.asarray(inp['ssd_dt_bias'][l]).reshape(16)
        pb[l, PB_ALOG:PB_ALOG + 16] = np.asarray(inp['ssd_a_log'][l]).reshape(16)
        pb[l, PB_D:PB_D + 8] = np.asarray(inp['ssd_d'][l])
        pb[l, PB_SNG:PB_SNG + 512] = np.asarray(inp['ssd_norm_g'][l])
        pb[l, PB_GQ:PB_GQ + 64] = np.asarray(inp['gqa_qnorm_g'][l])
        pb[l, PB_GK:PB_GK + 64] = np.asarray(inp['gqa_knorm_g'][l])
        pb[l, PB_DQ:PB_DQ + 64] = np.asarray(inp['diff_qnorm_g'][l])
        pb[l, PB_DK:PB_DK + 64] = np.asarray(inp['diff_knorm_g'][l])
        pb[l, PB_LAM:PB_LAM + 256] = np.asarray(inp['diff_lambda'][l]).reshape(256)
        pb[l, PB_LI] = 0.8 - 0.6 * math.exp(-0.3 * l)
        for ax, nm in enumerate(('rg_wa', 'rg_wx')):
            w = np.asarray(inp[nm][l])
            for d in range(2):
                for j in range(4):
                    for q in range(2):
                        rgw[l, q * 64:(q + 1) * 64, (ax * 2 + d) * 4 + j, q * 64:(q + 1) * 64] = w[d, 2 * j + q]
    return pf, pb, rgw


_CACHE = {}
STOP = None


def run(inputs, NB, DEPTH, NCTX, NLAT, n_cores, debug=False):
    x = np.asarray(inputs['x'], np.float32)
    ctx = np.asarray(inputs['ctx'], np.float32)
    c = np.asarray(inputs['c'], np.float32)
    c_ctx = np.asarray(inputs['c_ctx'], np.float32)
    key = (NB, DEPTH, NCTX, NLAT, debug)
    if key not in _CACHE:
        _CACHE[key] = build_program(NB, DEPTH, NCTX, NLAT, debug, stop=STOP)
    nc, info = _CACHE[key]
    cm, cs, sn = _consts(NLAT)
    pf, pb, rgw = _pack_params(inputs, DEPTH)
    wnames = ['w_ada', 'w_in', 'w_gate', 'w_br', 'w_out', 'w_up', 'w_down']
    shared = {k: np.ascontiguousarray(np.asarray(inputs[k], np.float32)) for k in wnames}
    shared.update(dict(ropec=cs, ropes=sn, cmask=cm, pf=pf, pb=pb, rgw=rgw))
    in_maps = []
    for core in range(n_cores):
        bs = range(core * NB, (core + 1) * NB)
        xin = np.stack([np.concatenate([ctx[b].T, x[b].T], axis=1) for b in bs], 0)
        cc = np.stack([c[b] for b in bs] + [c_ctx], 0)
        cTm = np.ascontiguousarray(cc.reshape(NB + 1, 8, 128).transpose(2, 1, 0))
        m = dict(shared)
        m['xin'] = np.ascontiguousarray(xin)
        m['cT'] = cTm
        in_maps.append(m)
    res = run_bass_kernel_spmd(nc, in_maps, core_ids=list(range(n_cores)))
    outs = []
    for core in range(n_cores):
        o = res.results[core]['out']
        for i in range(NB):
            outs.append(np.ascontiguousarray(o[i].T))
    return np.stack(outs, 0).astype(np.float32), res


def kernel(**inputs):
    out, _ = run(inputs, NB=2, DEPTH=2, NCTX=256, NLAT=4096, n_cores=8)
    return out
```

```python
import math
from contextlib import ExitStack
import numpy as np
import concourse.bass as bass
import concourse.mybir as mybir
from concourse.bass_utils import run_bass_kernel_spmd

F32 = mybir.dt.float32
BF16 = mybir.dt.bfloat16
AF = mybir.ActivationFunctionType
ALU = mybir.AluOpType
AX = mybir.AxisListType

SEM_EPOCH = 30000
N_DMA_SEMS = 56
EPS = 1e-6
D = 1024
INC = 4880
DFF = 2816
TZ, TDT, TGQ, TGK, TGV, TDQ, TDK, TDV, TOKC = 0, 512, 528, 1040, 1168, 1296, 1808, 2320, 2832
PF_BADA, PF_N1, PF_N2, PF_SCW, PF_SCB, PF_SUB, PF_RCW, PF_RCB, PF_RBA, PF_RBX, PF_RLM, PF_BG, PF_FCW, PF_FCB, NPF = \
    0, 48, 56, 64, 96, 104, 105, 121, 125, 133, 141, 149, 181, 313, 357
PB_DTB, PB_ALOG, PB_D, PB_SNG, PB_GQ, PB_GK, PB_DQ, PB_DK, PB_LAM, PB_LI, NPB = 0, 16, 32, 40, 552, 616, 680, 744, 808, 1064, 1065


class Rec:
    def __init__(self, target):
        self._t = target

    def __getattr__(self, name):
        f = getattr(self._t, name)
        return lambda *a, **k: (f, a, k)


class Prog:
    def __init__(self, nc):
        self.nc = nc
        self.eng = {'pe': nc.tensor, 'act': nc.scalar, 'dve': nc.vector, 'pool': nc.gpsimd, 'sp': nc.sync}
        self.ops = []
        self.pending_dma_w = set()
        self.nbar = 0
        self.dummy = None

    def op(self, engine, fn, reads=(), writes=(), dma=False):
        rec = fn()
        assert isinstance(rec, tuple) and len(rec) == 3
        self.ops.append((engine, rec, tuple(reads), tuple(writes), dma))
        if dma:
            self.pending_dma_w.update(writes)

    def barrier(self):
        b = self.nbar
        self.nbar += 1
        nc = self.nc
        pend = list(self.pending_dma_w)
        self.pending_dma_w = set()
        engs = ['pe', 'act', 'dve', 'pool', 'sp']
        for e in engs:
            rd = pend if e == 'sp' else []
            if e == 'sp' or self.dummy is None:
                self.op(e, (lambda e=e: (self.eng[e].nop, (), {})), reads=rd, writes=[('bar', b, e)])
            else:
                fn, r2, w2 = self.dummy[e]
                self.op(e, fn, reads=r2, writes=[('bar', b, e)] + w2)
        for e in engs:
            self.op(e, (lambda e=e: (self.eng[e].nop, (), {})), reads=[('bar', b, f) for f in engs if f != e],
                    writes=[('bar2', b, e)])

    def emit(self, sem_ctx):
        ops = self.ops
        n = len(ops)
        last_w = {}
        rd_eng = {}
        rd_dma = {}
        deps = [None] * n
        needs_inc = [False] * n
        for i, (e, fn, rd, wr, dma) in enumerate(ops):
            d = set()
            for k in rd:
                w = last_w.get(k)
                if w is not None:
                    d.add(w)
            for k in wr:
                w = last_w.get(k)
                if w is not None:
                    d.add(w)
                re_ = rd_eng.get(k)
                if re_:
                    d.update(re_.values())
                rdm = rd_dma.get(k)
                if rdm:
                    d.update(rdm)
            d.discard(i)
            dd = []
            for j in d:
                ej, _, _, _, dmaj = ops[j]
                if ej == e and (not dmaj) and (not dma) and e == 'pe':
                    continue
                dd.append(j)
                needs_inc[j] = True
            deps[i] = dd
            for k in rd:
                if dma:
                    rd_dma.setdefault(k, []).append(i)
                else:
                    rd_eng.setdefault(k, {})[e] = i
            for k in wr:
                last_w[k] = i
                rd_eng[k] = {}
                rd_dma[k] = []
        cnt = {e: 0 for e in self.eng}
        tl = [None] * n
        dma_cnt = [0] * N_DMA_SEMS
        dma_rr = 0
        for i, (e, fn, rd, wr, dma) in enumerate(ops):
            if dma:
                s = dma_rr % N_DMA_SEMS
                dma_rr += 1
                dma_cnt[s] += 16
                tl[i] = ('dma', s, dma_cnt[s])
            elif needs_inc[i]:
                cnt[e] += 1
                tl[i] = ('eng', e, cnt[e])
        sems = {}
        for e in self.eng:
            for ep in range(cnt[e] // SEM_EPOCH + 1):
                sems[(e, ep)] = sem_ctx(f"s_{e}_{ep}")
        dsems = [sem_ctx(f"s_dma_{s}") for s in range(N_DMA_SEMS)]
        seen = {e: {f: 0 for f in self.eng} for e in self.eng}
        seen_dma = {e: [0] * N_DMA_SEMS for e in self.eng}
        for i, (e, fn, rd, wr, dma) in enumerate(ops):
            eng = self.eng[e]
            need_eng = {}
            need_dma = {}
            for j in deps[i]:
                t = tl[j]
                if t[0] == 'eng':
                    _, f, c = t
                    if c > seen[e][f] and c > need_eng.get(f, 0):
                        need_eng[f] = c
                else:
                    _, s, v = t
                    if v > seen_dma[e][s] and v > need_dma.get(s, 0):
                        need_dma[s] = v
            for f, c in need_eng.items():
                ep = (c - 1) // SEM_EPOCH
                eng.wait_ge(sems[(f, ep)], c - ep * SEM_EPOCH)
                seen[e][f] = c
            for s, v in need_dma.items():
                eng.wait_ge(dsems[s], v)
                seen_dma[e][s] = v
            inst = fn[0](*fn[1], **fn[2])
            t = tl[i]
            if t is not None:
                if t[0] == 'dma':
                    inst.then_inc(dsems[t[1]], 16)
                else:
                    c = t[2]
                    inst.then_inc(sems[(e, (c - 1) // SEM_EPOCH)], 1)
        return dict(n_ops=n, counts=cnt)


class Arena:
    def __init__(self, ap_f32, ncols):
        self.ap = ap_f32
        self.n = ncols
        self.top = 0

    def mark(self):
        return self.top

    def release(self, m):
        self.top = m

    def alloc(self, shape, dt):
        nel = int(np.prod(shape))
        cols = (nel * (2 if dt == BF16 else 4) + 3) // 4
        cols = (cols + 7) // 8 * 8
        assert self.top + cols <= self.n, f"arena overflow {self.top}+{cols}>{self.n}"
        v = self.ap[:, self.top:self.top + cols]
        self.top += cols
        if dt == BF16:
            v = v.bitcast(BF16)
        v = v[:, 0:nel]
        if len(shape) == 2:
            v = v.rearrange("p (a b) -> p a b", a=shape[0])
        elif len(shape) == 3:
            v = v.rearrange("p (a b c) -> p a b c", a=shape[0], b=shape[1])
        elif len(shape) == 4:
            v = v.rearrange("p (a b c d) -> p a b c d", a=shape[0], b=shape[1], c=shape[2])
        return v


class StopBuild(Exception):
    pass


def build_program(NB, DEPTH, NCTX, NLAT, debug=False, stop=None):
    NTOK = NCTX + NLAT
    NSUB = NTOK // 128
    NSC = NCTX // 128
    NLS = NLAT // 128
    R = NB + 1
    nc = bass.Bass("TRN2", target_bir_lowering=False)
    dram = lambda name, shape, dt, kind: nc.dram_tensor(name, shape, dt, kind=kind).ap()
    skind = "ExternalOutput" if debug else "Internal"
    xin = dram("xin", [NB, D, NTOK], F32, "ExternalInput")
    cT = dram("cT", [128, 8, R], F32, "ExternalInput")
    ropec = dram("ropec", [128, NLS, 32], F32, "ExternalInput")
    ropes = dram("ropes", [128, NLS, 32], F32, "ExternalInput")
    cmask = dram("cmask", [128, 6, 128], F32, "ExternalInput")
    pfd = dram("pf", [DEPTH, 128, NPF], F32, "ExternalInput")
    pbd = dram("pb", [DEPTH, NPB], F32, "ExternalInput")
    rgwd = dram("rgw", [DEPTH, 128, 16, 128], F32, "ExternalInput")
    w_ada = dram("w_ada", [DEPTH, D, 6 * D], F32, "ExternalInput")
    w_in = dram("w_in", [DEPTH, D, INC], F32, "ExternalInput")
    w_gate = dram("w_gate", [DEPTH, 4, D, D], F32, "ExternalInput")
    w_br = dram("w_br", [DEPTH, 4, 512, D], F32, "ExternalInput")
    w_out = dram("w_out", [DEPTH, D, D], F32, "ExternalInput")
    w_up = dram("w_up", [DEPTH, D, 2 * DFF], F32, "ExternalInput")
    w_down = dram("w_down", [DEPTH, DFF, D], F32, "ExternalInput")
    outd = dram("out", [NB, D, NLAT], F32, "ExternalOutput")
    xs = [dram(f"xs{s}", [D, NTOK], F32, skind) for s in range(NB)]
    xm = dram("xm", [D, NTOK], F32, skind)
    u_tok = dram("u_tok", [NTOK, TOKC], BF16, skind)
    dt_tok = dram("dt_tok", [NTOK, 16], F32, skind)
    u_fm = dram("u_fm", [2048, NTOK], BF16, skind)
    brd = [dram(f"br{k}", [512, NTOK], BF16, skind) for k in range(4)]

    st = ExitStack()
    with st:
        ARENA_COLS = 52992
        arena_t = st.enter_context(nc.sbuf_tensor("arena", [128, ARENA_COLS], F32))
        psum_ts = [st.enter_context(nc.psum_tensor(f"psum{b}", [128, 512], F32)) for b in range(8)]
        A = Arena(arena_t, ARENA_COLS)
        PB = [psum_ts[b][:, :] for b in range(8)]
        PBH = [psum_ts[b][:, :].bitcast(BF16) for b in range(8)]
        P = Prog(nc)
        V, S_, G, T, SY = Rec(nc.vector), Rec(nc.scalar), Rec(nc.gpsimd), Rec(nc.tensor), Rec(nc.sync)

        def dve(fn, r, w):
            P.op('dve', fn, r, w)

        def act(fn, r, w):
            P.op('act', fn, r, w)

        def pool(fn, r, w):
            P.op('pool', fn, r, w)

        def pe(fn, r, w):
            P.op('pe', fn, r, w)

        def ld(out, in_, r, w):
            P.op('sp', lambda: SY.dma_start(out=out, in_=in_), r, w, dma=True)

        def stq(out, in_, r, w):
            P.op('act', lambda: S_.dma_start(out=out, in_=in_), r, w, dma=True)

        def mm(out, lhsT, rhs, start, stop, r, w, skip=False):
            pe(lambda: T.matmul(out, lhsT=lhsT, rhs=rhs, start=start, stop=stop, skip_group_check=skip), r, w)

        cm32 = A.alloc([6, 128], F32)
        ld(cm32, cmask, [], ['cm32'])
        LI, LS, UI, US, ONES, IDN = range(6)
        ident = A.alloc([128], BF16)
        ones_bf = A.alloc([128], BF16)
        dve(lambda: V.tensor_copy(out=ident, in_=cm32[:, IDN, :]), ['cm32'], ['ident'])
        dve(lambda: V.tensor_copy(out=ones_bf, in_=cm32[:, ONES, :]), ['cm32'], ['ones_bf'])
        rc = A.alloc([NLS, 32], F32)
        rs = A.alloc([NLS, 32], F32)
        ld(rc, ropec, [], ['rc'])
        ld(rs, ropes, [], ['rs'])
        scT = A.alloc([8, R], F32)
        ld(scT, cT, [], ['scT'])
        act(lambda: S_.activation(out=scT, in_=scT, func=AF.Silu), ['scT'], ['scT'])
        pf = A.alloc([NPF], F32)
        pbc = A.alloc([NPB], F32)
        modt = A.alloc([48, R], F32)
        A1 = A.alloc([8, R], F32)
        A2 = A.alloc([8, R], F32)
        aneg = A.alloc([16], F32)
        rgcp = A.alloc([8], F32)
        lamt = A.alloc([8], F32)
        gqk = A.alloc([10, 64], F32)
        gdk = A.alloc([16, 64], F32)
        dum = A.alloc([8], F32)
        dve(lambda: V.memset(dum, 0.0), [], [('dum', 'act'), ('dum', 'dve'), ('dum', 'pool')])
        P.dummy = {
            'pe': (lambda: T.matmul(PB[7][0:1, 0:2], lhsT=ones_bf[0:1, 0:1], rhs=ones_bf[0:1, 0:2], start=True, stop=True),
                   ['ones_bf'], [('ps', 7)]),
            'act': (lambda: S_.copy(out=dum[:, 0:1], in_=dum[:, 1:2]), [], [('dum', 'act')]),
            'dve': (lambda: V.tensor_copy(out=dum[:, 2:3], in_=dum[:, 3:4]), [], [('dum', 'dve')]),
            'pool': (lambda: G.tensor_copy(out=dum[:, 4:5], in_=dum[:, 5:6]), [], [('dum', 'pool')]),
        }
        PERSIST = A.mark()

        pbank = [0]

        def chk(name):
            if stop == name:
                raise StopBuild()

        def nb(lo=0, hi=8):
            b = lo + (pbank[0] % (hi - lo))
            pbank[0] += 1
            return b

        def tiles_of(include_ctx=True, w=512):
            t = [(NCTX * i // (-(-NCTX // w)), NCTX // (-(-NCTX // w))) for i in range(-(-NCTX // w))] if include_ctx else []
            t += [(NCTX + w * i, w) for i in range(NLAT // w)]
            return t

        def load_weight(dst_views, src_views, stage, tag):
            for i, (dv, sv) in enumerate(zip(dst_views, src_views)):
                sg = stage[i % len(stage)]
                sk = (tag + '_stg', i % len(stage))
                shp = dv.shape
                sgv = sg
                if len(shp) == 2:
                    sgv = sg[:, 0:shp[1]]
                else:
                    sgv = sg[:, 0:shp[1] * shp[2]].rearrange("p (a b) -> p a b", a=shp[1])
                ld(sgv, sv, [], [sk])
                pool(lambda dv=dv, sgv=sgv: G.tensor_copy(out=dv, in_=sgv), [sk], [(tag, i)])

        def norm_mod(xt, xk, W, sq, rstd, h, hk, Am, Bm, r, sfx):
            act(lambda: S_.activation(out=sq[:, :, 0:W], in_=xt[:, :, 0:W], func=AF.Square), [xk], ['sq' + sfx])
            b = nb(6, 8)
            for kc in range(8):
                mm(PB[b][:, 0:W], ones_bf, sq[:, kc, 0:W], kc == 0, kc == 7, ['sq' + sfx, 'ones_bf'], [('ps', b)])
            act(lambda: S_.activation(out=rstd[:, 0:W], in_=PB[b][:, 0:W], func=AF.Sqrt, scale=1.0 / D, bias=EPS),
                [('ps', b)], ['rstd' + sfx])
            dve(lambda: V.reciprocal(out=rstd[:, 0:W], in_=rstd[:, 0:W]), ['rstd' + sfx], ['rstd' + sfx])
            for kc in range(8):
                tk = ('tmpn', kc % 2)
                tv = tmpn[kc % 2]
                dve(lambda kc=kc, tv=tv: V.tensor_tensor(out=tv[:, 0:W], in0=xt[:, kc, 0:W], in1=rstd[:, 0:W], op=ALU.mult),
                    [xk, 'rstd' + sfx], [tk])
                act(lambda kc=kc, tv=tv: S_.activation(out=h[:, kc, 0:W], in_=tv[:, 0:W], func=AF.Identity,
                                                       scale=Am[:, kc, r:r + 1], bias=Bm[:, kc, r:r + 1]),
                    [tk, 'mods'], [hk])

        tmpn = [None, None]

        def emit_layer(l):
            last = (l == DEPTH - 1)
            chk('C0')
            A.release(PERSIST)
            ld(pf, pfd[l], [], ['pf'])
            ld(pbc, pbd[l:l + 1, :].partition_broadcast(128), [], ['pbc'])
            chk('S0p')
            wst = [A.alloc([6 * D], F32) for _ in range(2)]
            b0 = 0
            dve(lambda: V.memset(PB[b0][:, 0:48 * R], 0.0), [], [('ps', b0)])
            for kc in range(8):
                w = wst[kc % 2]
                wk = ('wst', kc % 2)
                ld(w, w_ada[l, kc * 128:(kc + 1) * 128, :], [], [wk])
                for j in range(48):
                    mm(PB[b0][:, j * R:(j + 1) * R], w[:, j * 128:(j + 1) * 128], scT[:, kc, :], False, kc == 7,
                       [wk, 'scT'], [('ps', b0)], skip=True)
                chk('S0k%d' % kc)
            chk('S0w')
            act(lambda: S_.copy(out=modt.rearrange("p a b -> p (a b)"), in_=PB[b0][:, 0:48 * R]), [('ps', b0)], ['mods'])
            chk('S0c')
            dve(lambda: V.tensor_tensor(out=modt, in0=modt,
                                        in1=pf[:, PF_BADA:PF_BADA + 48, None].to_broadcast([128, 48, R]), op=ALU.add),
                ['mods', 'pf'], ['mods'])
            chk('S0m')
            for (Ax, sc0, ng) in ((A1, 8, PF_N1), (A2, 32, PF_N2)):
                dve(lambda Ax=Ax, sc0=sc0: V.tensor_scalar(out=Ax, in0=modt[:, sc0:sc0 + 8, :], scalar1=1.0, scalar2=None,
                                                           op0=ALU.add), ['mods'], ['mods'])
                dve(lambda Ax=Ax, ng=ng: V.tensor_tensor(out=Ax, in0=Ax, in1=pf[:, ng:ng + 8, None].to_broadcast([128, 8, R]),
                                                         op=ALU.mult), ['mods', 'pf'], ['mods'])
            B1 = modt[:, 0:8, :]
            G1 = modt[:, 16:24, :]
            B2 = modt[:, 24:32, :]
            G2 = modt[:, 40:48, :]
            chk('S0n')
            act(lambda: S_.activation(out=aneg, in_=pbc[:, PB_ALOG:PB_ALOG + 16], func=AF.Exp), ['pbc'], ['aneg'])
            dve(lambda: V.tensor_scalar(out=aneg, in0=aneg, scalar1=-1.0, scalar2=None, op0=ALU.mult), ['aneg'], ['aneg'])
            act(lambda: S_.activation(out=rgcp, in_=pf[:, PF_RLM:PF_RLM + 8], func=AF.Exp, scale=-1.0), ['pf'], ['rgcp'])
            act(lambda: S_.activation(out=rgcp, in_=rgcp, func=AF.Ln, bias=1.0), ['rgcp'], ['rgcp'])
            dve(lambda: V.tensor_scalar(out=rgcp, in0=rgcp, scalar1=-8.0, scalar2=None, op0=ALU.mult), ['rgcp'], ['rgcp'])
            lt = A.alloc([128], F32)
            dve(lambda: V.tensor_tensor(out=lt[:, 0:64], in0=pbc[:, PB_LAM:PB_LAM + 64], in1=pbc[:, PB_LAM + 64:PB_LAM + 128],
                                        op=ALU.mult), ['pbc'], ['lt'])
            dve(lambda: V.tensor_tensor(out=lt[:, 64:128], in0=pbc[:, PB_LAM + 128:PB_LAM + 192],
                                        in1=pbc[:, PB_LAM + 192:PB_LAM + 256], op=ALU.mult), ['pbc', 'lt'], ['lt'])
            dve(lambda: V.tensor_reduce(out=lamt[:, 3:5], in_=lt.rearrange("p (a b) -> p a b", a=2), axis=AX.X, op=ALU.add),
                ['lt'], ['lamt'])
            act(lambda: S_.activation(out=lamt[:, 3:5], in_=lamt[:, 3:5], func=AF.Exp), ['lamt'], ['lamt'])
            dve(lambda: V.tensor_tensor(out=lamt[:, 0:1], in0=lamt[:, 4:5], in1=lamt[:, 3:4], op=ALU.subtract), ['lamt'], ['lamt'])
            dve(lambda: V.tensor_tensor(out=lamt[:, 0:1], in0=lamt[:, 0:1], in1=pbc[:, PB_LI:PB_LI + 1], op=ALU.subtract),
                ['lamt', 'pbc'], ['lamt'])
            dve(lambda: V.tensor_scalar(out=lamt[:, 1:2], in0=pbc[:, PB_LI:PB_LI + 1], scalar1=-1.0, scalar2=1.0,
                                        op0=ALU.mult, op1=ALU.add), ['lamt', 'pbc'], ['lamt'])
            dve(lambda: V.tensor_tensor(out=lamt[:, 2:3], in0=lamt[:, 1:2], in1=pf[:, PF_SUB:PF_SUB + 1], op=ALU.mult),
                ['lamt', 'pf'], ['lamt'])
            chk('S0l')
            dve(lambda: V.tensor_copy(out=gqk[:, 0:8, :], in_=pbc[:, None, PB_GQ:PB_GQ + 64].to_broadcast([128, 8, 64])),
                ['pbc'], ['gqk'])
            dve(lambda: V.tensor_copy(out=gqk[:, 8:10, :], in_=pbc[:, None, PB_GK:PB_GK + 64].to_broadcast([128, 2, 64])),
                ['pbc', 'gqk'], ['gqk'])
            dve(lambda: V.tensor_copy(out=gdk[:, 0:8, :], in_=pbc[:, None, PB_DQ:PB_DQ + 64].to_broadcast([128, 8, 64])),
                ['pbc'], ['gdk'])
            dve(lambda: V.tensor_copy(out=gdk[:, 8:16, :], in_=pbc[:, None, PB_DK:PB_DK + 64].to_broadcast([128, 8, 64])),
                ['pbc', 'gdk'], ['gdk'])
            P.barrier()
            chk('S0' + (kind if 'S0' in ('P3', 'P4') else ''))

            for s in range(NB):
                xsrc = xin[s] if l == 0 else xs[s]

                A.release(PERSIST)
                wb = A.alloc([8, INC], BF16)
                M1 = A.mark()
                stg = [A.alloc([INC], F32) for _ in range(2)]
                load_weight([wb[:, kc, :] for kc in range(8)], [w_in[l, kc * 128:(kc + 1) * 128, :] for kc in range(8)],
                            stg, 'wb')
                P.barrier()
                chk('P1w' + (kind if 'P1w' in ('P3', 'P4') else ''))
                A.release(M1)
                WB = [('wb', kc) for kc in range(8)]
                xt2 = [A.alloc([8, 512], F32) for _ in range(2)]
                sq = A.alloc([8, 512], BF16)
                rstd = A.alloc([512], F32)
                tmpn[0] = A.alloc([512], F32)
                tmpn[1] = A.alloc([512], F32)
                h2 = [A.alloc([8, 512], BF16) for _ in range(2)]
                ofm = [A.alloc([4, 512], BF16) for _ in range(2)]
                otok = [A.alloc([TOKC], BF16) for _ in range(2)]
                odt = [A.alloc([16], F32) for _ in range(2)]
                tokblocks = [(0, 0, 512)] + [(512 + 512 * i, 1536 + 512 * i, 512) for i in range(4)] + [(2560, 3584, 272)]
                fmcols = [512 + 128 * j for j in range(8)] + [3856 + 128 * j for j in range(8)]
                ev = [0]
                for ti, (t0, W) in enumerate(tiles_of()):
                    r = NB if t0 < NCTX else s
                    xt = xt2[ti % 2]
                    xk = ('xt', ti % 2)
                    h = h2[ti % 2]
                    hk = ('h', ti % 2)
                    ld(xt[:, :, 0:W], xsrc[:, t0:t0 + W].rearrange("(c p) t -> p c t", p=128), [('xs', s)], [xk])
                    chk('P1a')
                    norm_mod(xt, xk, W, sq, rstd, h, hk, A1, B1, r, '')
                    chk('P1b')
                    for j4 in range(4):
                        o = ofm[j4 % 2]
                        ok = ('ofm', j4 % 2)
                        for jj in range(4):
                            j = j4 * 4 + jj
                            c0 = fmcols[j]
                            b = nb(0, 6)
                            for kc in range(8):
                                mm(PB[b][:, 0:W], wb[:, kc, c0:c0 + 128], h[:, kc, 0:W], kc == 0, kc == 7,
                                   [WB[kc], hk], [('ps', b)])
                            ev[0] += 1
                            if ev[0] % 2:
                                act(lambda o=o, jj=jj, b=b: S_.copy(out=o[:, jj, 0:W], in_=PB[b][:, 0:W]), [('ps', b)], [ok])
                            else:
                                dve(lambda o=o, jj=jj, b=b: V.tensor_copy(out=o[:, jj, 0:W], in_=PB[b][:, 0:W]), [('ps', b)], [ok])
                        stq(u_fm[j4 * 512:(j4 + 1) * 512, t0:t0 + W].rearrange("(c p) t -> p c t", p=128), o[:, :, 0:W],
                            [ok], [('u_fm', j4)])
                    chk('P1d')
                    for si in range(W // 128):
                        tg = t0 + si * 128
                        ot = otok[si % 2]
                        otk = ('otok', si % 2)
                        od = odt[si % 2]
                        for (oc0, wc0, cw) in tokblocks:
                            b = nb(0, 6)
                            for kc in range(8):
                                mm(PB[b][:, 0:cw], h[:, kc, si * 128:(si + 1) * 128], wb[:, kc, wc0:wc0 + cw], kc == 0, kc == 7,
                                   [WB[kc], hk], [('ps', b)])
                            ev[0] += 1
                            if ev[0] % 2:
                                act(lambda ot=ot, b=b, oc0=oc0, cw=cw: S_.copy(out=ot[:, oc0:oc0 + cw], in_=PB[b][:, 0:cw]),
                                    [('ps', b)], [otk])
                            else:
                                dve(lambda ot=ot, b=b, oc0=oc0, cw=cw: V.tensor_copy(out=ot[:, oc0:oc0 + cw], in_=PB[b][:, 0:cw]),
                                    [('ps', b)], [otk])
                            if oc0 == 512:
                                dve(lambda od=od, b=b: V.tensor_copy(out=od, in_=PB[b][:, 0:16]), [('ps', b)], [otk])
                        stq(u_tok[tg:tg + 128, :], ot, [otk], [('u_tok', tg // 128)])
                        stq(dt_tok[tg:tg + 128, :], od, [otk], [('dt_tok', tg // 128)])
                P.barrier()
                chk('P1' + (kind if 'P1' in ('P3', 'P4') else ''))

                A.release(PERSIST)
                BCfm = A.alloc([4, NTOK], BF16)
                xs_tok = A.alloc([NSUB, 512], BF16)
                B_tok = A.alloc([NSUB, 256], BF16)
                dtr = A.alloc([NSUB, 16], F32)
                dtv = A.alloc([NSUB, 16], F32)
                av = A.alloc([NSUB, 16], F32)
                ev_ = A.alloc([NSUB, 16], F32)
                wg = A.alloc([NSUB, 16], F32)
                eA = A.alloc([NSUB, 16], F32)
                gS = A.alloc([512], F32)
                dtb = A.alloc([16], F32)
                M2 = A.mark()
                rawp = [A.alloc([NTOK + 6], BF16) for _ in range(2)]
                acc = A.alloc([NTOK], F32)
                xcv = [A.alloc([NTOK], BF16) for _ in range(2)]
                for i in range(2):
                    pool(lambda i=i: G.memset(rawp[i], 0.0), [], [('rawp', i)])
                SEG = [(0, NCTX, 1), (NCTX, NLAT, 4 + NCTX)]
                for j in range(8):
                    rp = rawp[j % 2]
                    rk = ('rawp', j % 2)
                    for (g0, gl, c0) in SEG:
                        ld(rp[:, c0:c0 + gl], u_fm[j * 128:(j + 1) * 128, g0:g0 + gl], [('u_fm', j // 4)], [rk])
                    for (g0, gl, c0) in SEG:
                        for k in range(4):
                            src = rp[:, c0 + k - 1:c0 + k - 1 + gl]
                            wk_ = pf[:, PF_SCW + j * 4 + k:PF_SCW + j * 4 + k + 1]
                            if k == 0:
                                dve(lambda src=src, wk_=wk_, g0=g0, gl=gl, j=j: V.tensor_scalar(
                                    out=acc[:, g0:g0 + gl], in0=src, scalar1=wk_, scalar2=pf[:, PF_SCB + j:PF_SCB + j + 1],
                                    op0=ALU.mult, op1=ALU.add), [rk, 'pf'], ['acc'])
                            else:
                                dve(lambda src=src, wk_=wk_, g0=g0, gl=gl: V.scalar_tensor_tensor(
                                    out=acc[:, g0:g0 + gl], in0=src, scalar=wk_, in1=acc[:, g0:g0 + gl],
                                    op0=ALU.mult, op1=ALU.add), [rk, 'pf', 'acc'], ['acc'])
                    if j < 6:
                        xc = xcv[j % 2]
                        xck = ('xcv', j % 2)
                    else:
                        xc = BCfm[:, j - 4, :]
                        xck = ('BCfm', j - 4)
                    act(lambda xc=xc: S_.activation(out=xc, in_=acc, func=AF.Silu), ['acc'], [xck])
                    if j in (4, 5):
                        pool(lambda xc=xc, j=j: G.tensor_copy(out=BCfm[:, j - 4, :], in_=xc), [xck], [('BCfm', j - 4)])
                    if j < 6:
                        for s0 in range(0, NSUB, 8):
                            ns = min(8, NSUB - s0)
                            b = nb(0, 6)
                            for q in range(ns):
                                pe(lambda q=q, s0=s0, b=b, xc=xc: T.transpose(out=PBH[b][:, q * 128:(q + 1) * 128],
                                                                             in_=xc[:, (s0 + q) * 128:(s0 + q + 1) * 128],
                                                                             identity=ident), [xck, 'ident'], [('ps', b)])
                            if j < 4:
                                dst = xs_tok[:, s0:s0 + ns, j * 128:(j + 1) * 128]
                                dk_ = 'xs_tok'
                            else:
                                dst = B_tok[:, s0:s0 + ns, (j - 4) * 128:(j - 3) * 128]
                                dk_ = 'B_tok'
                            dve(lambda dst=dst, b=b, ns=ns: V.tensor_copy(
                                out=dst, in_=PBH[b][:, 0:ns * 128].rearrange("p (a b) -> p a b", a=ns)), [('ps', b)], [dk_])
                ld(dtr, dt_tok.rearrange("(s p) j -> p s j", p=128), [('dt_tok', i) for i in range(NSUB)], ['dtr'])
                dve(lambda: V.tensor_copy(out=dtb, in_=pbc[:, PB_DTB:PB_DTB + 16]), ['pbc'], ['dtb'])
                dve(lambda: V.tensor_copy(out=gS, in_=pbc[:, PB_SNG:PB_SNG + 512]), ['pbc'], ['gS'])
                dve(lambda: V.tensor_tensor(out=dtr, in0=dtr, in1=dtb[:, None, :].to_broadcast([128, NSUB, 16]), op=ALU.add),
                    ['dtr', 'dtb'], ['dtr'])
                act(lambda: S_.activation(out=dtv, in_=dtr, func=AF.Exp), ['dtr'], ['dtv'])
                act(lambda: S_.activation(out=dtv, in_=dtv, func=AF.Ln, bias=1.0), ['dtv'], ['dtv'])
                dve(lambda: V.tensor_tensor(out=av, in0=dtv, in1=aneg[:, None, :].to_broadcast([128, NSUB, 16]), op=ALU.mult),
                    ['dtv', 'aneg'], ['av'])
                HALF = (NSUB + 1) // 2
                for c0 in range(0, NSUB, HALF):
                    ncn = min(HALF, NSUB - c0)
                    b1, b2, b3 = 0, 1, 2
                    for c in range(c0, c0 + ncn):
                        o = (c - c0) * 16
                        mm(PB[b1][:, o:o + 8], cm32[:, LI, :], av[:, c, 0:8], True, True, ['cm32', 'av'], [('ps', b1)])
                        mm(PB[b1][:, o + 8:o + 16], cm32[:, UI, :], av[:, c, 8:16], True, True, ['cm32', 'av'], [('ps', b1)])
                        mm(PB[b2][:, o:o + 8], cm32[:, US, :], av[:, c, 0:8], True, True, ['cm32', 'av'], [('ps', b2)])
                        mm(PB[b2][:, o + 8:o + 16], cm32[:, LS, :], av[:, c, 8:16], True, True, ['cm32', 'av'], [('ps', b2)])
                        mm(PB[b3][:, o:o + 16], cm32[:, ONES, :], av[:, c, :], True, True, ['cm32', 'av'], [('ps', b3)])
                    for (bb, dst, dk_) in ((b1, ev_, 'ev'), (b2, wg, 'wg'), (b3, eA, 'eA')):
                        act(lambda bb=bb, dst=dst, c0=c0, ncn=ncn: S_.activation(
                            out=dst[:, c0:c0 + ncn, :], in_=PB[bb][:, 0:ncn * 16].rearrange("p (a b) -> p a b", b=16),
                            func=AF.Exp), [('ps', bb)], [dk_])
                dve(lambda: V.tensor_tensor(out=wg, in0=wg, in1=dtv, op=ALU.mult), ['wg', 'dtv'], ['wg'])
                P.barrier()
                chk('P2a' + (kind if 'P2a' in ('P3', 'P4') else ''))
                A.release(M2)
                Sb_all = A.alloc([NSUB, 512], BF16)
                Sst = [A.alloc([512], F32) for _ in range(2)]
                Sf_bf = A.alloc([512], BF16)
                xw = [A.alloc([512], BF16) for _ in range(2)]
                CBm = A.alloc([2, 2, 128], F32)
                aM = A.alloc([16, 128], F32)
                E_sb = A.alloc([16, 128], F32)
                Wt = A.alloc([16, 128], BF16)
                zt = [A.alloc([512], BF16) for _ in range(2)]
                t1 = A.alloc([512], F32)
                t2 = A.alloc([512], F32)
                t3 = A.alloc([512], F32)
                yg = A.alloc([512], F32)
                ssq = A.alloc([4], F32)
                yn = A.alloc([512], BF16)
                ost = [A.alloc([4, 128], BF16) for _ in range(2)]
                junk = A.alloc([256], F32)
                for d in range(2):
                    dve(lambda d=d: V.memset(Sst[d], 0.0), [], [('Sst', d)])
                bw_order = list(range(NSC - 1, -1, -1)) + list(range(NSUB - 1, NSC - 1, -1))

                def state_update(c, d, xwk):
                    x_ = xw[xwk % 2]
                    k_ = ('xw', xwk % 2)
                    dve(lambda: V.tensor_tensor(out=x_.rearrange("p (h e) -> p h e", h=8),
                                                in0=xs_tok[:, c, :].rearrange("p (h e) -> p h e", h=8),
                                                in1=wg[:, c, d * 8:d * 8 + 8, None].to_broadcast([128, 8, 64]), op=ALU.mult),
                        ['xs_tok', 'wg'], [k_])
                    b = nb(4, 6)
                    for g in range(2):
                        mm(PB[b][:, g * 256:(g + 1) * 256], B_tok[:, c, g * 128:(g + 1) * 128], x_[:, g * 256:(g + 1) * 256],
                           True, True, ['B_tok', k_], [('ps', b)])
                    dve(lambda: V.tensor_tensor(out=Sst[d].rearrange("p (h e) -> p h e", h=8),
                                                in0=Sst[d].rearrange("p (h e) -> p h e", h=8),
                                                in1=eA[:, c, d * 8:d * 8 + 8, None].to_broadcast([128, 8, 64]), op=ALU.mult),
                        [('Sst', d), 'eA'], [('Sst', d)])
                    dve(lambda: V.tensor_tensor(out=Sst[d], in0=Sst[d], in1=PB[b], op=ALU.add), [('Sst', d), ('ps', b)],
                        [('Sst', d)])

                for i, c in enumerate(bw_order):
                    act(lambda c=c: S_.copy(out=Sb_all[:, c, :], in_=Sst[1]), [('Sst', 1)], [('Sb_all', c)])
                    if i < len(bw_order) - 1:
                        state_update(c, 1, i)
                for c in range(NSUB):
                    cs_ = slice(c * 128, (c + 1) * 128)
                    z_ = zt[c % 2]
                    zk = ('zt', c % 2)
                    ld(z_, u_tok[c * 128:(c + 1) * 128, TZ:TZ + 512], [('u_tok', c)], [zk])
                    act(lambda: S_.copy(out=Sf_bf, in_=Sst[0]), [('Sst', 0)], ['Sf_bf'])
                    bcb = 6
                    for g in range(2):
                        mm(PB[bcb][:, g * 128:(g + 1) * 128], BCfm[:, g, cs_], BCfm[:, 2 + g, cs_], True, True,
                           [('BCfm', g), ('BCfm', 2 + g)], [('ps', bcb)])
                    for d, mk in ((0, LI), (1, UI)):
                        dve(lambda d=d, mk=mk: V.tensor_tensor(
                            out=CBm[:, d, :, :], in0=PB[bcb][:, 0:256].rearrange("p (g l) -> p g l", g=2),
                            in1=cm32[:, mk, None, :].to_broadcast([128, 2, 128]), op=ALU.mult),
                            [('ps', bcb), 'cm32'], ['CBm'])
                    for d, mk in ((0, US), (1, LS)):
                        pool(lambda d=d, mk=mk, c=c: G.tensor_tensor(
                            out=aM[:, d * 8:d * 8 + 8, :], in0=cm32[:, mk, None, :].to_broadcast([128, 8, 128]),
                            in1=av[:, c, d * 8:d * 8 + 8, None].to_broadcast([128, 8, 128]), op=ALU.mult),
                            ['cm32', 'av'], [('aM', d)])
                    for q4 in range(4):
                        b = q4
                        for jj in range(4):
                            j = q4 * 4 + jj
                            d = j // 8
                            mm(PB[b][:, jj * 128:(jj + 1) * 128], aM[:, j, :], cm32[:, LI if d == 0 else UI, :], True, True,
                               [('aM', d), 'cm32'], [('ps', b)])
                        act(lambda q4=q4, b=b: S_.activation(out=E_sb[:, q4 * 4:q4 * 4 + 4, :],
                                                              in_=PB[b].rearrange("p (a l) -> p a l", a=4), func=AF.Exp),
                            [('ps', b)], [('E_sb', q4)])
                    dve(lambda: V.tensor_tensor(
                        out=E_sb.rearrange("p (a h) l -> p a h l", h=4), in0=E_sb.rearrange("p (a h) l -> p a h l", h=4),
                        in1=CBm.rearrange("p d g l -> p (d g) l")[:, :, None, :].to_broadcast([128, 4, 4, 128]), op=ALU.mult),
                        [('E_sb', q) for q in range(4)] + ['CBm'], [('E_sb', q) for q in range(4)])
                    dve(lambda c=c: V.tensor_tensor(out=Wt, in0=E_sb, in1=dtv[:, c, :, None].to_broadcast([128, 16, 128]),
                                                    op=ALU.mult), [('E_sb', q) for q in range(4)] + ['dtv'], ['Wt'])
                    by = 7
                    for hh in range(8):
                        mm(PB[by][:, hh * 64:(hh + 1) * 64], Wt[:, hh, :], xs_tok[:, c, hh * 64:(hh + 1) * 64], True, False,
                           ['Wt', 'xs_tok'], [('ps', by)])
                        mm(PB[by][:, hh * 64:(hh + 1) * 64], Wt[:, 8 + hh, :], xs_tok[:, c, hh * 64:(hh + 1) * 64], False, True,
                           ['Wt', 'xs_tok'], [('ps', by)])
                    bof, bob = 4, 5
                    for g in range(2):
                        mm(PB[bof][:, g * 256:(g + 1) * 256], BCfm[:, 2 + g, cs_], Sf_bf[:, g * 256:(g + 1) * 256], True, True,
                           [('BCfm', 2 + g), 'Sf_bf'], [('ps', bof)])
                        mm(PB[bob][:, g * 256:(g + 1) * 256], BCfm[:, 2 + g, cs_], Sb_all[:, c, g * 256:(g + 1) * 256], True, True,
                           [('BCfm', 2 + g), ('Sb_all', c)], [('ps', bob)])
                    dve(lambda c=c: V.tensor_tensor(out=t1.rearrange("p (h e) -> p h e", h=8),
                                                    in0=PB[bof].rearrange("p (h e) -> p h e", h=8),
                                                    in1=ev_[:, c, 0:8, None].to_broadcast([128, 8, 64]), op=ALU.mult),
                        [('ps', bof), 'ev'], ['t1'])
                    dve(lambda c=c: V.tensor_tensor(out=t2.rearrange("p (h e) -> p h e", h=8),
                                                    in0=PB[bob].rearrange("p (h e) -> p h e", h=8),
                                                    in1=ev_[:, c, 8:16, None].to_broadcast([128, 8, 64]), op=ALU.mult),
                        [('ps', bob), 'ev'], ['t2'])
                    pool(lambda c=c: G.tensor_tensor(out=t3.rearrange("p (h e) -> p h e", h=8),
                                                     in0=xs_tok[:, c, :].rearrange("p (h e) -> p h e", h=8),
                                                     in1=pbc[:, PB_D:PB_D + 8, None].to_broadcast([128, 8, 64]), op=ALU.mult),
                         ['xs_tok', 'pbc'], ['t3'])
                    pool(lambda: G.tensor_tensor(out=t1, in0=t1, in1=t2, op=ALU.add), ['t1', 't2'], ['t1'])
                    pool(lambda: G.tensor_tensor(out=t1, in0=t1, in1=t3, op=ALU.add), ['t1', 't3'], ['t1'])
                    dve(lambda: V.tensor_tensor(out=t1, in0=t1, in1=PB[by], op=ALU.add), ['t1', ('ps', by)], ['t1'])
                    if c < NSUB - 1:
                        state_update(c, 0, c)
                    act(lambda z_=z_: S_.activation(out=t2, in_=z_, func=AF.Silu), [zk, 't2'], ['t2'])
                    dve(lambda: V.tensor_tensor(out=yg, in0=t1, in1=t2, op=ALU.mult), ['t1', 't2'], ['yg'])
                    for g in range(2):
                        act(lambda g=g: S_.activation(out=junk, in_=yg[:, g * 256:(g + 1) * 256], func=AF.Square,
                                                      accum_out=ssq[:, g:g + 1]), ['yg', 'junk'], ['ssq', 'junk'])
                    act(lambda: S_.activation(out=ssq[:, 2:4], in_=ssq[:, 0:2], func=AF.Sqrt, scale=1.0 / 256, bias=EPS),
                        ['ssq'], ['ssq'])
                    dve(lambda: V.reciprocal(out=ssq[:, 2:4], in_=ssq[:, 2:4]), ['ssq'], ['ssq'])
                    for g in range(2):
                        dve(lambda g=g: V.scalar_tensor_tensor(out=yn[:, g * 256:(g + 1) * 256], in0=yg[:, g * 256:(g + 1) * 256],
                                                               scalar=ssq[:, 2 + g:3 + g], in1=gS[:, g * 256:(g + 1) * 256],
                                                               op0=ALU.mult, op1=ALU.mult), ['yg', 'ssq', 'gS'], ['yn'])
                    bt = 6
                    for k in range(4):
                        pe(lambda k=k: T.transpose(out=PBH[bt][:, 512 + k * 128:512 + (k + 1) * 128],
                                                   in_=yn[:, k * 128:(k + 1) * 128], identity=ident),
                           ['yn', 'ident'], [('ps', bt)])
                    o_ = ost[c % 2]
                    okk = ('ost', c % 2)
                    act(lambda o_=o_: S_.copy(out=o_, in_=PBH[bt][:, 512:1024].rearrange("p (k t) -> p k t", k=4)),
                        [('ps', bt)], [okk])
                    stq(brd[0][:, cs_].rearrange("(k p) t -> p k t", p=128), o_, [okk], [('br0', c)])
                P.barrier()
                chk('P2' + (kind if 'P2' in ('P3', 'P4') else ''))

                for kind in ('gqa', 'diff'):
                    A.release(PERSIST)
                    NH = 10 if kind == 'gqa' else 16
                    NQ = 8
                    NT = 6 if kind == 'gqa' else 8
                    c_in = TGQ if kind == 'gqa' else TDQ
                    w_in_cols = 640 if kind == 'gqa' else 1024
                    gq_, gk_n = (gqk, 'gqk') if kind == 'gqa' else (gdk, 'gdk')
                    qkT = A.alloc([NT, NTOK], BF16)
                    if kind == 'gqa':
                        vaug = A.alloc([NSUB, 2, 128], BF16)
                        pool(lambda: G.memset(vaug, 1.0), [], ['vaug'])
                        vtmp = [A.alloc([128], BF16) for _ in range(2)]
                    else:
                        vd = A.alloc([NSUB, 512], BF16)
                        ld(vd, u_tok[:, TDV:TDV + 512].rearrange("(s p) c -> p s c", p=128),
                           [('u_tok', i) for i in range(NSUB)], ['vd'])
                    qin = [A.alloc([NH, 64], BF16) for _ in range(2)]
                    sqf = A.alloc([NH, 64], F32)
                    qn = A.alloc([NH, 64], F32)
                    ssn = A.alloc([2, NH], F32)
                    ta = A.alloc([NH, 32], F32)
                    tb = A.alloc([NH, 32], F32)
                    tc_ = A.alloc([NH, 32], F32)
                    td = A.alloc([NH, 32], F32)
                    qr = [A.alloc([NH + 2, 64], BF16) for _ in range(2)]
                    for c in range(NSUB):
                        qi = qin[c % 2]
                        qik = ('qin', c % 2)
                        q_ = qr[c % 2]
                        qrk = ('qr', c % 2)
                        ld(qi, u_tok[c * 128:(c + 1) * 128, c_in:c_in + w_in_cols].rearrange("p (h e) -> p h e", e=64),
                           [('u_tok', c)], [qik])
                        if kind == 'gqa':
                            vt = vtmp[c % 2]
                            vk = ('vtmp', c % 2)
                            ld(vt, u_tok[c * 128:(c + 1) * 128, TGV:TGV + 128], [('u_tok', c)], [vk])
                            pool(lambda vt=vt, c=c: G.tensor_copy(out=vaug[:, c, :, 0:64], in_=vt.rearrange("p (g e) -> p g e", g=2)),
                                 [vk, 'vaug'], ['vaug'])
                        act(lambda qi=qi: S_.activation(out=sqf, in_=qi, func=AF.Square), [qik], ['sqf'])
                        dve(lambda: V.tensor_reduce(out=ssn[:, 0, :], in_=sqf, axis=AX.X, op=ALU.add), ['sqf'], ['ssn'])
                        act(lambda: S_.activation(out=ssn[:, 1, :], in_=ssn[:, 0, :], func=AF.Sqrt, scale=1.0 / 64, bias=EPS),
                            ['ssn'], ['ssn'])
                        dve(lambda: V.reciprocal(out=ssn[:, 1, :], in_=ssn[:, 1, :]), ['ssn'], ['ssn'])
                        dve(lambda qi=qi: V.tensor_tensor(out=qn, in0=qi, in1=ssn[:, 1, :, None].to_broadcast([128, NH, 64]),
                                                          op=ALU.mult), [qik, 'ssn'], ['qn'])
                        if c >= NSC:
                            cl = c - NSC
                            pool(lambda: G.tensor_tensor(out=qn, in0=qn, in1=gq_, op=ALU.mult), ['qn', gk_n], ['qn'])
                            csb = rc[:, cl, None, :].to_broadcast([128, NH, 32])
                            snb = rs[:, cl, None, :].to_broadcast([128, NH, 32])
                            dve(lambda csb=csb: V.tensor_tensor(out=ta, in0=qn[:, :, 0:32], in1=csb, op=ALU.mult), ['qn', 'rc'], ['ta'])
                            pool(lambda snb=snb: G.tensor_tensor(out=tb, in0=qn[:, :, 32:64], in1=snb, op=ALU.mult), ['qn', 'rs'], ['tb'])
                            pool(lambda snb=snb: G.tensor_tensor(out=tc_, in0=qn[:, :, 0:32], in1=snb, op=ALU.mult), ['qn', 'rs'], ['tc'])
                            dve(lambda csb=csb: V.tensor_tensor(out=td, in0=qn[:, :, 32:64], in1=csb, op=ALU.mult), ['qn', 'rc'], ['td'])
                            dve(lambda q_=q_: V.tensor_tensor(out=q_[:, 0:NH, 0:32], in0=ta, in1=tb, op=ALU.subtract),
                                ['ta', 'tb'], [qrk])
                            pool(lambda q_=q_: G.tensor_tensor(out=q_[:, 0:NH, 32:64], in0=tc_, in1=td, op=ALU.add),
                                 ['tc', 'td', qrk], [qrk])
                        else:
                            pool(lambda q_=q_: G.tensor_tensor(out=q_[:, 0:NH, :], in0=qn, in1=gq_, op=ALU.mult), ['qn', gk_n], [qrk])
                        bt = nb(0, 6)
                        if kind == 'gqa':
                            pool(lambda q_=q_: G.tensor_copy(out=q_[:, 10:12, :], in_=q_[:, 9:10, :].to_broadcast([128, 2, 64])),
                                 [qrk], [qrk])
                            pool(lambda q_=q_: G.tensor_copy(out=q_[:, 9:10, :], in_=q_[:, 8:9, :]), [qrk], [qrk])
                        for k in range(NT):
                            pe(lambda k=k, q_=q_, bt=bt: T.transpose(out=PBH[bt][:, k * 128:(k + 1) * 128],
                                                                     in_=q_[:, 2 * k:2 * k + 2, :], identity=ident),
                               [qrk, 'ident'], [('ps', bt)])
                        act(lambda bt=bt, c=c: S_.copy(out=qkT[:, :, c * 128:(c + 1) * 128],
                                                       in_=PBH[bt][:, 0:NT * 128].rearrange("p (k t) -> p k t", k=NT)),
                            [('ps', bt)], [('qkT', c)])
                    QKT = [('qkT', c) for c in range(NSUB)]
                    P.barrier()
                    chk('P3' + (kind if 'P3' in ('P3', 'P4') else ''))
                    MA = A.mark()
                    qblocks = ([] if last else [(0, NCTX, NSC)]) + [(NCTX + 512 * i, 512, NSUB) for i in range(NLAT // 512)]
                    pT = [A.alloc([512], BF16) for _ in range(8)]
                    oT = [A.alloc([4, 512], BF16) for _ in range(2)]
                    rsb = [A.alloc([512], F32) for _ in range(2)]
                    if kind == 'diff':
                        o1 = A.alloc([512], F32)
                        o2 = A.alloc([512], F32)
                        sqd = A.alloc([512], BF16)
                    ptc = [0]
                    for qi_, (q0, QW, nkt) in enumerate(qblocks):
                        o_ = oT[qi_ % 2]
                        ok_ = ('oT', qi_ % 2)
                        qs = slice(q0, q0 + QW)
                        if kind == 'gqa':
                            for hh in range(8):
                                hf, j, g = hh % 2, hh // 2, hh // 4
                                pp = slice(hf * 64, hf * 64 + 64)
                                bo = 4 + hh % 2
                                pq = []
                                for kt in range(nkt + 2):
                                    if kt < nkt:
                                        bs = kt % 3
                                        ks = slice(kt * 128, (kt + 1) * 128)
                                        mm(PB[bs][:, 0:QW], qkT[pp, 4 + g, ks], qkT[pp, j, qs], True, True, QKT, [('ps', bs)])
                                        p_ = pT[ptc[0] % 8]
                                        pk = ('pT', ptc[0] % 8)
                                        ptc[0] += 1
                                        act(lambda p_=p_, bs=bs: S_.activation(out=p_[:, 0:QW], in_=PB[bs][:, 0:QW], func=AF.Exp,
                                                                               scale=0.125), [('ps', bs)], [pk])
                                        pq.append((kt, p_, pk))
                                    if kt >= 2:
                                        kt2, p2, pk2 = pq.pop(0)
                                        mm(PB[bo][:, 0:QW], vaug[:, kt2, g, :], p2[:, 0:QW], kt2 == 0, kt2 == nkt - 1,
                                           ['vaug', pk2], [('ps', bo)])
                                r_ = rsb[hh % 2]
                                rk_ = ('rsb', hh % 2)
                                dve(lambda r_=r_, bo=bo: V.reciprocal(out=r_[0:64, 0:QW], in_=PB[bo][64:128, 0:QW]), [('ps', bo)], [rk_])
                                dve(lambda r_=r_, bo=bo, o_=o_, pp=pp, j=j: V.tensor_tensor(
                                    out=o_[pp, j, 0:QW], in0=PB[bo][0:64, 0:QW], in1=r_[0:64, 0:QW], op=ALU.mult),
                                    [('ps', bo), rk_, ok_], [ok_])
                            stq(brd[1][:, qs].rearrange("(k p) t -> p k t", p=128), o_[:, :, 0:QW], [ok_],
                                [('br1', i) for i in range(q0 // 128, (q0 + QW) // 128)])
                        else:
                            for hh in range(4):
                                bo = [4, 5]
                                bsu = [6, 7]
                                pq = [[], []]
                                for kt in range(nkt + 2):
                                    for cc in range(2):
                                        pp = slice(cc * 64, cc * 64 + 64)
                                        if kt >= 2:
                                            kt2, p2, pk2 = pq[cc].pop(0)
                                            mm(PB[bo[cc]][:, 0:QW], vd[:, kt2, hh * 128:(hh + 1) * 128], p2[:, 0:QW],
                                               kt2 == 0, kt2 == nkt - 1, ['vd', pk2], [('ps', bo[cc])])
                                            mm(PB[bsu[cc]][:, 0:QW], ones_bf, p2[:, 0:QW], kt2 == 0, kt2 == nkt - 1,
                                               ['ones_bf', pk2], [('ps', bsu[cc])])
                                        if kt < nkt:
                                            bs = cc * 2 + kt % 2
                                            ks = slice(kt * 128, (kt + 1) * 128)
                                            mm(PB[bs][:, 0:QW], qkT[pp, 4 + hh, ks], qkT[pp, hh, qs], True, True, QKT, [('ps', bs)])
                                            p_ = pT[ptc[0] % 8]
                                            pk = ('pT', ptc[0] % 8)
                                            ptc[0] += 1
                                            act(lambda p_=p_, bs=bs: S_.activation(out=p_[:, 0:QW], in_=PB[bs][:, 0:QW],
                                                                                   func=AF.Exp, scale=0.125), [('ps', bs)], [pk])
                                            pq[cc].append((kt, p_, pk))
                                for cc, ox in ((0, o1), (1, o2)):
                                    r_ = rsb[cc]
                                    rk_ = ('rsb', cc)
                                    dve(lambda r_=r_, cc=cc: V.reciprocal(out=r_[:, 0:QW], in_=PB[bsu[cc]][:, 0:QW]),
                                        [('ps', bsu[cc])], [rk_])
                                    dve(lambda r_=r_, cc=cc, ox=ox: V.tensor_tensor(out=ox[:, 0:QW], in0=PB[bo[cc]][:, 0:QW],
                                                                                   in1=r_[:, 0:QW], op=ALU.mult),
                                        [('ps', bo[cc]), rk_], [('o12', cc)])
                                dve(lambda: V.scalar_tensor_tensor(out=o1[:, 0:QW], in0=o2[:, 0:QW], scalar=lamt[:, 0:1],
                                                                   in1=o1[:, 0:QW], op0=ALU.mult, op1=ALU.add),
                                    [('o12', 0), ('o12', 1), 'lamt'], [('o12', 0)])
                                act(lambda: S_.activation(out=sqd[:, 0:QW], in_=o1[:, 0:QW], func=AF.Square), [('o12', 0)], ['sqd'])
                                b3 = 3
                                mm(PB[b3][:, 0:QW], ones_bf, sqd[:, 0:QW], True, True, ['ones_bf', 'sqd'], [('ps', b3)])
                                act(lambda: S_.activation(out=o2[:, 0:QW], in_=PB[b3][:, 0:QW], func=AF.Sqrt, scale=1.0 / 128,
                                                          bias=EPS), [('ps', b3), ('o12', 1)], [('o12', 1)])
                                dve(lambda: V.reciprocal(out=o2[:, 0:QW], in_=o2[:, 0:QW]), [('o12', 1)], [('o12', 1)])
                                dve(lambda hh=hh, o_=o_: V.scalar_tensor_tensor(out=o_[:, hh, 0:QW], in0=o1[:, 0:QW],
                                                                                scalar=lamt[:, 2:3], in1=o2[:, 0:QW],
                                                                                op0=ALU.mult, op1=ALU.mult),
                                    [('o12', 0), ('o12', 1), 'lamt', ok_], [ok_])
                            stq(brd[2][:, qs].rearrange("(k p) t -> p k t", p=128), o_[:, :, 0:QW], [ok_],
                                [('br2', i) for i in range(q0 // 128, (q0 + QW) // 128)])
                    P.barrier()
                    chk('P4' + (kind if 'P4' in ('P3', 'P4') else ''))

                A.release(PERSIST)
                rgw32 = A.alloc([16, 128], F32)
                rgwb = A.alloc([16, 128], BF16)
                ld(rgw32, rgwd[l], [], ['rgw32'])
                pool(lambda: G.tensor_copy(out=rgwb, in_=rgw32), ['rgw32'], ['rgwb'])
                rawx = A.alloc([NTOK + 6], BF16)
                pool(lambda: G.memset(rawx, 0.0), [], ['rawx'])
                xr = A.alloc([NTOK], F32)
                xrb = A.alloc([NTOK], BF16)
                a_all = A.alloc([NTOK], F32)
                b_all = A.alloc([NTOK], F32)
                hsum = A.alloc([NTOK], F32)
                hb = A.alloc([NTOK], F32)
                rgg = A.alloc([NTOK], BF16)
                rgo = A.alloc([NTOK], BF16)
                tg1 = A.alloc([512], F32)
                tg2 = A.alloc([512], F32)
                tg3 = A.alloc([512], F32)
                for j in range(4):
                    for (g0, gl, c0) in SEG:
                        ld(rawx[:, c0:c0 + gl], u_fm[(12 + j) * 128:(13 + j) * 128, g0:g0 + gl], [('u_fm', 3)], ['rawx'])
                    ld(rgg, u_fm[(8 + j) * 128:(9 + j) * 128, :], [('u_fm', 2)], ['rgg'])
                    for (g0, gl, c0) in SEG:
                        for k in range(4):
                            src = rawx[:, c0 + k - 1:c0 + k - 1 + gl]
                            wk_ = pf[:, PF_RCW + j * 4 + k:PF_RCW + j * 4 + k + 1]
                            if k == 0:
                                dve(lambda src=src, wk_=wk_, g0=g0, gl=gl, j=j: V.tensor_scalar(
                                    out=xr[:, g0:g0 + gl], in0=src, scalar1=wk_, scalar2=pf[:, PF_RCB + j:PF_RCB + j + 1],
                                    op0=ALU.mult, op1=ALU.add), ['rawx', 'pf'], ['xr'])
                            else:
                                dve(lambda src=src, wk_=wk_, g0=g0, gl=gl: V.scalar_tensor_tensor(
                                    out=xr[:, g0:g0 + gl], in0=src, scalar=wk_, in1=xr[:, g0:g0 + gl],
                                    op0=ALU.mult, op1=ALU.add), ['rawx', 'pf', 'xr'], ['xr'])
                    act(lambda: S_.copy(out=xrb, in_=xr), ['xr'], ['xrb'])
                    for d in range(2):
                        for (t0, W) in tiles_of():
                            ts_ = slice(t0, t0 + W)
                            ba_, bx_ = nb(0, 4), nb(4, 8)
                            mm(PB[ba_][:, 0:W], rgwb[:, (0 * 2 + d) * 4 + j, :], xrb[:, ts_], True, True, ['rgwb', 'xrb'], [('ps', ba_)])
                            mm(PB[bx_][:, 0:W], rgwb[:, (1 * 2 + d) * 4 + j, :], xrb[:, ts_], True, True, ['rgwb', 'xrb'], [('ps', bx_)])
                            cba = pf[:, PF_RBA + d * 4 + j:PF_RBA + d * 4 + j + 1]
                            cbx = pf[:, PF_RBX + d * 4 + j:PF_RBX + d * 4 + j + 1]
                            ccp = rgcp[:, d * 4 + j:d * 4 + j + 1]
                            act(lambda W=W, ba_=ba_, cba=cba: S_.activation(out=tg1[:, 0:W], in_=PB[ba_][:, 0:W], func=AF.Sigmoid,
                                                                            bias=cba), [('ps', ba_), 'pf'], ['tg1'])
                            act(lambda W=W, ts_=ts_, ccp=ccp: S_.activation(out=a_all[:, ts_], in_=tg1[:, 0:W], func=AF.Exp,
                                                                            scale=ccp), ['tg1', 'rgcp'], ['a_all'])
                            act(lambda W=W, bx_=bx_, cbx=cbx: S_.activation(out=tg2[:, 0:W], in_=PB[bx_][:, 0:W], func=AF.Sigmoid,
                                                                            bias=cbx), [('ps', bx_), 'pf'], ['tg2'])
                            dve(lambda W=W, ts_=ts_: V.tensor_tensor(out=tg2[:, 0:W], in0=tg2[:, 0:W], in1=xr[:, ts_], op=ALU.mult),
                                ['tg2', 'xr'], ['tg2'])
                            pool(lambda W=W, ts_=ts_: G.tensor_tensor(out=tg3[:, 0:W], in0=a_all[:, ts_], in1=a_all[:, ts_],
                                                                      op=ALU.mult), ['a_all', 'tg3'], ['tg3'])
                            act(lambda W=W: S_.activation(out=tg3[:, 0:W], in_=tg3[:, 0:W], func=AF.Sqrt, scale=-1.0, bias=1.0),
                                ['tg3'], ['tg3'])
                            dve(lambda W=W, ts_=ts_: V.tensor_tensor(out=b_all[:, ts_], in0=tg2[:, 0:W], in1=tg3[:, 0:W], op=ALU.mult),
                                ['tg2', 'tg3'], ['b_all'])
                        if d == 0:
                            dve(lambda: V.tensor_tensor_scan(out=hsum, data0=a_all, data1=b_all, initial=0.0, op0=ALU.mult,
                                                             op1=ALU.add), ['a_all', 'b_all'], ['hsum'])
                        else:
                            dve(lambda: V.tensor_tensor_scan(out=hb[:, 0:NCTX][:, ::-1], data0=a_all[:, 0:NCTX][:, ::-1],
                                                             data1=b_all[:, 0:NCTX][:, ::-1], initial=0.0, op0=ALU.mult,
                                                             op1=ALU.add), ['a_all', 'b_all'], ['hb'])
                            dve(lambda: V.tensor_tensor_scan(out=hb[:, NCTX:NTOK][:, ::-1], data0=a_all[:, NCTX:NTOK][:, ::-1],
                                                             data1=b_all[:, NCTX:NTOK][:, ::-1], initial=hb[:, 0:1],
                                                             op0=ALU.mult, op1=ALU.add), ['a_all', 'b_all', 'hb'], ['hb'])
                            pool(lambda: G.tensor_tensor(out=hsum, in0=hsum, in1=hb, op=ALU.add), ['hsum', 'hb'], ['hsum'])
                    pool(lambda: G.tensor_tensor(out=a_all, in0=rgg, in1=rgg, op=ALU.mult), ['rgg', 'a_all'], ['a_all'])
                    dve(lambda: V.tensor_scalar(out=a_all, in0=a_all, scalar1=0.044715, scalar2=1.0, op0=ALU.mult, op1=ALU.add),
                        ['a_all'], ['a_all'])
                    dve(lambda: V.tensor_tensor(out=a_all, in0=a_all, in1=rgg, op=ALU.mult), ['a_all', 'rgg'], ['a_all'])
                    act(lambda: S_.activation(out=b_all, in_=a_all, func=AF.Sigmoid, scale=1.5957691216057308),
                        ['a_all', 'b_all'], ['b_all'])
                    pool(lambda: G.tensor_tensor(out=b_all, in0=b_all, in1=rgg, op=ALU.mult), ['b_all', 'rgg'], ['b_all'])
                    dve(lambda: V.tensor_tensor(out=rgo, in0=b_all, in1=hsum, op=ALU.mult), ['b_all', 'hsum'], ['rgo'])
                    stq(brd[3][j * 128:(j + 1) * 128, :], rgo, ['rgo'], [('br3', j)])
                P.barrier()
                chk('P6' + (kind if 'P6' in ('P3', 'P4') else ''))

                A.release(PERSIST)
                wgt_ = A.alloc([4, 8, D], BF16)
                wbr_ = A.alloc([4, 4, D], BF16)
                wo_ = A.alloc([8, D], BF16)
                M7 = A.mark()
                stg7 = [A.alloc([4 * D], F32) for _ in range(2)]
                dsts, srcs = [], []
                for k in range(4):
                    for hf in range(2):
                        dsts.append(wgt_[:, k, hf * 4:(hf + 1) * 4, :])
                        srcs.append(w_gate[l, k, hf * 512:(hf + 1) * 512, :].rearrange("(c p) n -> p c n", p=128))
                for k in range(4):
                    dsts.append(wbr_[:, k, :, :])
                    srcs.append(w_br[l, k].rearrange("(c p) n -> p c n", p=128))
                for hf in range(2):
                    dsts.append(wo_[:, hf * 4:(hf + 1) * 4, :])
                    srcs.append(w_out[l, hf * 512:(hf + 1) * 512, :].rearrange("(c p) n -> p c n", p=128))
                load_weight(dsts, srcs, stg7, 'w7')
                W7 = [('w7', i) for i in range(len(dsts))]
                P.barrier()
                chk('P7w' + (kind if 'P7w' in ('P3', 'P4') else ''))
                A.release(M7)
                xt7 = [A.alloc([8, 256], F32) for _ in range(2)]
                sq = A.alloc([8, 256], BF16)
                rstd = A.alloc([256], F32)
                tmpn[0] = A.alloc([256], F32)
                tmpn[1] = A.alloc([256], F32)
                h7 = A.alloc([8, 256], BF16)
                ob = [A.alloc([4, 256], BF16) for _ in range(2)]
                macc = A.alloc([8, 256], F32)
                mbf = A.alloc([8, 256], BF16)
                sg2 = [A.alloc([256], F32) for _ in range(2)]
                tm2 = [A.alloc([256], F32) for _ in range(2)]
                cnt7 = [0]
                for ti, (t0, W) in enumerate(tiles_of(not last, 256)):
                    r = NB if t0 < NCTX else s
                    xt = xt7[ti % 2]
                    xk = ('xt', ti % 2)
                    ts_ = slice(t0, t0 + W)
                    ld(xt[:, :, 0:W], xsrc[:, ts_].rearrange("(c p) t -> p c t", p=128), [('xs', s)], [xk])
                    norm_mod(xt, xk, W, sq, rstd, h7, 'h7', A1, B1, r, '')
                    for k in range(4):
                        o_ = ob[k % 2]
                        obk = ('ob', k % 2)
                        ld(o_[:, :, 0:W], brd[k][:, ts_].rearrange("(c p) t -> p c t", p=128),
                           [(f'br{k}', i) for i in range(NSUB if k != 3 else 4)], [obk])
                        for oc in range(8):
                            ocs = slice(oc * 128, (oc + 1) * 128)
                            bg_, bb_ = nb(0, 4), nb(4, 8)
                            for kc in range(8):
                                mm(PB[bg_][:, 0:W], wgt_[:, k, kc, ocs], h7[:, kc, 0:W], kc == 0, kc == 7, W7 + ['h7'], [('ps', bg_)])
                            for kc in range(4):
                                mm(PB[bb_][:, 0:W], wbr_[:, k, kc, ocs], o_[:, kc, 0:W], kc == 0, kc == 3, W7 + [obk], [('ps', bb_)])
                            cnt7[0] += 1
                            sg = sg2[cnt7[0] % 2]
                            sgk = ('sg', cnt7[0] % 2)
                            act(lambda sg=sg, bg_=bg_, k=k, oc=oc: S_.activation(
                                out=sg[:, 0:W], in_=PB[bg_][:, 0:W], func=AF.Sigmoid,
                                bias=pf[:, PF_BG + k * 8 + oc:PF_BG + k * 8 + oc + 1]), [('ps', bg_), 'pf'], [sgk])
                            if k == 0:
                                dve(lambda sg=sg, bb_=bb_, oc=oc: V.tensor_tensor(out=macc[:, oc, 0:W], in0=PB[bb_][:, 0:W],
                                                                                  in1=sg[:, 0:W], op=ALU.mult),
                                    [('ps', bb_), sgk], [('macc', oc)])
                            else:
                                tm = tm2[cnt7[0] % 2]
                                tmk = ('tm', cnt7[0] % 2)
                                dve(lambda sg=sg, bb_=bb_, tm=tm: V.tensor_tensor(out=tm[:, 0:W], in0=PB[bb_][:, 0:W],
                                                                                  in1=sg[:, 0:W], op=ALU.mult),
                                    [('ps', bb_), sgk], [tmk])
                                pool(lambda tm=tm, oc=oc: G.tensor_tensor(out=macc[:, oc, 0:W], in0=macc[:, oc, 0:W],
                                                                          in1=tm[:, 0:W], op=ALU.add), [tmk, ('macc', oc)],
                                     [('macc', oc)])
                    act(lambda: S_.copy(out=mbf[:, :, 0:W], in_=macc[:, :, 0:W]), [('macc', oc) for oc in range(8)], ['mbf'])
                    for oc in range(8):
                        ocs = slice(oc * 128, (oc + 1) * 128)
                        b = nb(0, 8)
                        for kc in range(8):
                            mm(PB[b][:, 0:W], wo_[:, kc, ocs], mbf[:, kc, 0:W], kc == 0, kc == 7, W7 + ['mbf'], [('ps', b)])
                        dve(lambda oc=oc, b=b, xt=xt: V.scalar_tensor_tensor(out=xt[:, oc, 0:W], in0=PB[b][:, 0:W],
                                                                             scalar=G1[:, oc, r:r + 1], in1=xt[:, oc, 0:W],
                                                                             op0=ALU.mult, op1=ALU.add),
                            [('ps', b), 'mods', xk], [xk])
                    stq(xm[:, ts_].rearrange("(c p) t -> p c t", p=128), xt[:, :, 0:W], [xk], ['xm'])
                P.barrier()
                chk('P7' + (kind if 'P7' in ('P3', 'P4') else ''))

                A.release(PERSIST)
                wup = A.alloc([8, 2 * DFF], BF16)
                wdn = A.alloc([22, D], BF16)
                M8 = A.mark()
                stg8 = [A.alloc([4 * D], F32) for _ in range(2)]
                dsts, srcs = [], []
                for kc in range(8):
                    for q in range(2):
                        dsts.append(wup[:, kc, q * DFF:(q + 1) * DFF])
                        srcs.append(w_up[l, kc * 128:(kc + 1) * 128, q * DFF:(q + 1) * DFF])
                for q in range(11):
                    dsts.append(wdn[:, 2 * q:2 * q + 2, :])
                    srcs.append(w_down[l, q * 256:(q + 1) * 256, :].rearrange("(c p) n -> p c n", p=128))
                load_weight(dsts, srcs, stg8, 'w8')
                W8 = [('w8', i) for i in range(len(dsts))]
                P.barrier()
                chk('P8w' + (kind if 'P8w' in ('P3', 'P4') else ''))
                A.release(M8)
                xt8 = A.alloc([8, 256], F32)
                sq = A.alloc([8, 256], BF16)
                rstd = A.alloc([256], F32)
                tmpn[0] = A.alloc([256], F32)
                tmpn[1] = A.alloc([256], F32)
                h8 = A.alloc([8, 256], BF16)
                actb = A.alloc([22, 256], BF16)
                cg2 = [A.alloc([256], F32) for _ in range(2)]
                cv2 = [A.alloc([256], F32) for _ in range(2)]
                dve(lambda: V.memset(xt8, 1.0), [], ['xt8'])
                ftiles = []
                for (sg0, sg1) in ([] if last else [(0, NCTX)]) + [(NCTX, NTOK)]:
                    ln = sg1 - sg0
                    nlt = max(1, -(-ln // 254))
                    base = ln // nlt
                    o0 = sg0
                    for i in range(nlt):
                        wo = base + (1 if i < ln - base * nlt else 0)
                        ftiles.append((o0, wo, sg0, sg1))
                        o0 += wo
                dst_x = outd[s] if last else xs[s]
                for (o0, Wo, sg0, sg1) in ftiles:
                    Ww = Wo + 2
                    lo = max(o0 - 1, sg0)
                    hi = min(o0 + Wo + 1, sg1)
                    cl = lo - (o0 - 1)
                    ld(xt8[:, :, cl:cl + hi - lo], xm[:, lo:hi].rearrange("(c p) t -> p c t", p=128), ['xm'], ['xt8'])
                    r = NB if o0 < NCTX else s
                    norm_mod(xt8, 'xt8', Ww, sq, rstd, h8, 'h8', A2, B2, r, '')
                    if cl > 0:
                        pool(lambda: G.memset(h8[:, :, 0:1], 0.0), ['h8'], ['h8'])
                    if hi < o0 + Wo + 1:
                        pool(lambda Ww=Ww: G.memset(h8[:, :, Ww - 1:Ww], 0.0), ['h8'], ['h8'])
                    for i in range(22):
                        res = []
                        for half, buf in ((0, cg2), (1, cv2)):
                            ch = half * 22 + i
                            c0 = ch * 128
                            b = nb(0, 8)
                            for kc in range(8):
                                mm(PB[b][:, 0:Ww], wup[:, kc, c0:c0 + 128], h8[:, kc, 0:Ww], kc == 0, kc == 7, W8 + ['h8'], [('ps', b)])
                            cb_ = buf[i % 2]
                            cbk = ('c%d' % half, i % 2)
                            fw = lambda k, ch=ch: pf[:, PF_FCW + ch * 3 + k:PF_FCW + ch * 3 + k + 1]
                            act(lambda cb_=cb_, b=b, ch=ch, fw=fw: S_.activation(
                                out=cb_[:, 0:Wo], in_=PB[b][:, 1:1 + Wo], func=AF.Identity, scale=fw(1),
                                bias=pf[:, PF_FCB + ch:PF_FCB + ch + 1]), [('ps', b), 'pf'], [cbk])
                            dve(lambda cb_=cb_, b=b, fw=fw: V.scalar_tensor_tensor(out=cb_[:, 0:Wo], in0=PB[b][:, 0:Wo], scalar=fw(0),
                                                                                   in1=cb_[:, 0:Wo], op0=ALU.mult, op1=ALU.add),
                                [('ps', b), 'pf', cbk], [cbk])
                            dve(lambda cb_=cb_, b=b, fw=fw: V.scalar_tensor_tensor(out=cb_[:, 0:Wo], in0=PB[b][:, 2:2 + Wo],
                                                                                   scalar=fw(2), in1=cb_[:, 0:Wo], op0=ALU.mult,
                                                                                   op1=ALU.add), [('ps', b), 'pf', cbk], [cbk])
                            res.append((cb_, cbk))
                        (cgb, cgk), (cvb, cvk) = res
                        act(lambda cgb=cgb: S_.activation(out=cgb[:, 0:Wo], in_=cgb[:, 0:Wo], func=AF.Silu), [cgk], [cgk])
                        pool(lambda cgb=cgb, cvb=cvb, i=i: G.tensor_tensor(out=actb[:, i, 0:Wo], in0=cgb[:, 0:Wo], in1=cvb[:, 0:Wo],
                                                                           op=ALU.mult), [cgk, cvk], [('actb', i)])
                    AK = [('actb', i) for i in range(22)]
                    for oc in range(8):
                        ocs = slice(oc * 128, (oc + 1) * 128)
                        b = nb(0, 8)
                        for kc in range(22):
                            mm(PB[b][:, 0:Wo], wdn[:, kc, ocs], actb[:, kc, 0:Wo], kc == 0, kc == 21, W8 + AK, [('ps', b)])
                        dve(lambda oc=oc, b=b: V.scalar_tensor_tensor(out=xt8[:, oc, 1:1 + Wo], in0=PB[b][:, 0:Wo],
                                                                      scalar=G2[:, oc, r:r + 1], in1=xt8[:, oc, 1:1 + Wo],
                                                                      op0=ALU.mult, op1=ALU.add), [('ps', b), 'mods', 'xt8'], ['xt8'])
                    od0 = o0 - NCTX if last else o0
                    stq(dst_x[:, od0:od0 + Wo].rearrange("(c p) t -> p c t", p=128), xt8[:, :, 1:1 + Wo], ['xt8'],
                        [('xs', s), 'out'])
                P.barrier()
                chk('P8' + (kind if 'P8' in ('P3', 'P4') else ''))
        try:
            for l_ in range(DEPTH):
                emit_layer(l_)
        except StopBuild:
            P.barrier()
        P.op('sp', lambda: SY.nop(), reads=['out'])
        info = P.emit(lambda name: st.enter_context(nc.semaphore(name)))
    return nc, info


def _consts(NLAT):
    m = np.arange(128)[:, None]
    l_ = np.arange(128)[None, :]
    cm = np.stack([(m <= l_), (m < l_), (m >= l_), (m > l_), np.ones((128, 128), bool), (m == l_)], 1).astype(np.float32)
    t = np.arange(NLAT)
    row = (t // 64).astype(np.float32)
    col = (t % 64).astype(np.float32)
    inv = (10000.0 ** (-np.arange(16, dtype=np.float32) / 16)).astype(np.float32)
    ang = np.concatenate([row[:, None] * inv, col[:, None] * inv], -1).astype(np.float32)
    cs = np.cos(ang).astype(np.float32).reshape(NLAT // 128, 128, 32).transpose(1, 0, 2)
    sn = np.sin(ang).astype(np.float32).reshape(NLAT // 128, 128, 32).transpose(1, 0, 2)
    return np.ascontiguousarray(cm), np.ascontiguousarray(cs), np.ascontiguousarray(sn)


def _fm(v, n):
    return np.ascontiguousarray(np.asarray(v, np.float32).reshape(n, 128).T)


def _pack_params(inp, DEPTH):
    pf = np.zeros((DEPTH, 128, NPF), np.float32)
    pb = np.zeros((DEPTH, NPB), np.float32)
    rgw = np.zeros((DEPTH, 128, 16, 128), np.float32)
    for l in range(DEPTH):
        pf[l, :, PF_BADA:PF_BADA + 48] = _fm(inp['b_ada'][l], 48)
        pf[l, :, PF_N1:PF_N1 + 8] = _fm(inp['norm1_g'][l], 8)
        pf[l, :, PF_N2:PF_N2 + 8] = _fm(inp['norm2_g'][l], 8)
        scw = np.asarray(inp['ssd_conv_w'][l])
        pf[l, :, PF_SCW:PF_SCW + 32] = scw.reshape(4, 8, 128).transpose(2, 1, 0).reshape(128, 32)
        pf[l, :, PF_SCB:PF_SCB + 8] = _fm(inp['ssd_conv_b'][l], 8)
        pf[l, :, PF_SUB] = np.asarray(inp['diff_subln_g'][l])
        rcw = np.asarray(inp['rg_conv_w'][l])
        pf[l, :, PF_RCW:PF_RCW + 16] = rcw.reshape(4, 4, 128).transpose(2, 1, 0).reshape(128, 16)
        pf[l, :, PF_RCB:PF_RCB + 4] = _fm(inp['rg_conv_b'][l], 4)
        for nm, off in (('rg_ba', PF_RBA), ('rg_bx', PF_RBX), ('rg_lambda', PF_RLM)):
            v = np.asarray(inp[nm][l])
            pf[l, :, off:off + 8] = v.reshape(2, 4, 128).transpose(2, 0, 1).reshape(128, 8)
        bg = np.asarray(inp['b_gate'][l])
        pf[l, :, PF_BG:PF_BG + 32] = bg.reshape(4, 8, 128).transpose(2, 0, 1).reshape(128, 32)
        fcw = np.asarray(inp['ffn_conv_w'][l])
        pf[l, :, PF_FCW:PF_FCW + 132] = fcw.reshape(3, 44, 128).transpose(2, 1, 0).reshape(128, 132)
        pf[l, :, PF_FCB:PF_FCB + 44] = _fm(inp['ffn_conv_b'][l], 44)
        pb[l, PB_DTB:PB_DTB + 16] = np.asarray(inp['ssd_dt_bias'][l]).reshape(16)
        pb[l, PB_ALOG:PB_ALOG + 16] = np.asarray(inp['ssd_a_log'][l]).reshape(16)
        pb[l, PB_D:PB_D + 8] = np.asarray(inp['ssd_d'][l])
        pb[l, PB_SNG:PB_SNG + 512] = np.asarray(inp['ssd_norm_g'][l])
        pb[l, PB_GQ:PB_GQ + 64] = np.asarray(inp['gqa_qnorm_g'][l])
        pb[l, PB_GK:PB_GK + 64] = np.asarray(inp['gqa_knorm_g'][l])
        pb[l, PB_DQ:PB_DQ + 64] = np.asarray(inp['diff_qnorm_g'][l])
        pb[l, PB_DK:PB_DK + 64] = np.asarray(inp['diff_knorm_g'][l])
        pb[l, PB_LAM:PB_LAM + 256] = np.asarray(inp['diff_lambda'][l]).reshape(256)
        pb[l, PB_LI] = 0.8 - 0.6 * math.exp(-0.3 * l)
        for ax, nm in enumerate(('rg_wa', 'rg_wx')):
            w = np.asarray(inp[nm][l])
            for d in range(2):
                for j in range(4):
                    for q in range(2):
                        rgw[l, q * 64:(q + 1) * 64, (ax * 2 + d) * 4 + j, q * 64:(q + 1) * 64] = w[d, 2 * j + q]
    return pf, pb, rgw


_CACHE = {}
STOP = None


def run(inputs, NB, DEPTH, NCTX, NLAT, n_cores, debug=False):
    x = np.asarray(inputs['x'], np.float32)
    ctx = np.asarray(inputs['ctx'], np.float32)
    c = np.asarray(inputs['c'], np.float32)
    c_ctx = np.asarray(inputs['c_ctx'], np.float32)
    key = (NB, DEPTH, NCTX, NLAT, debug)
    if key not in _CACHE:
        _CACHE[key] = build_program(NB, DEPTH, NCTX, NLAT, debug, stop=STOP)
    nc, info = _CACHE[key]
    cm, cs, sn = _consts(NLAT)
    pf, pb, rgw = _pack_params(inputs, DEPTH)
    wnames = ['w_ada', 'w_in', 'w_gate', 'w_br', 'w_out', 'w_up', 'w_down']
    shared = {k: np.ascontiguousarray(np.asarray(inputs[k], np.float32)) for k in wnames}
    shared.update(dict(ropec=cs, ropes=sn, cmask=cm, pf=pf, pb=pb, rgw=rgw))
    in_maps = []
    for core in range(n_cores):
        bs = range(core * NB, (core + 1) * NB)
        xin = np.stack([np.concatenate([ctx[b].T, x[b].T], axis=1) for b in bs], 0)
        cc = np.stack([c[b] for b in bs] + [c_ctx], 0)
        cTm = np.ascontiguousarray(cc.reshape(NB + 1, 8, 128).transpose(2, 1, 0))
        m = dict(shared)
        m['xin'] = np.ascontiguousarray(xin)
        m['cT'] = cTm
        in_maps.append(m)
    res = run_bass_kernel_spmd(nc, in_maps, core_ids=list(range(n_cores)))
    outs = []
    for core in range(n_cores):
        o = res.results[core]['out']
        for i in range(NB):
            outs.append(np.ascontiguousarray(o[i].T))
    return np.stack(outs, 0).astype(np.float32), res


def kernel(**inputs):
    out, _ = run(inputs, NB=2, DEPTH=2, NCTX=256, NLAT=4096, n_cores=8)
    return out
```

```python
import math
from contextlib import ExitStack
import numpy as np
import concourse.bass as bass
import concourse.mybir as mybir
from concourse.bass_utils import run_bass_kernel_spmd

F32 = mybir.dt.float32
BF16 = mybir.dt.bfloat16
AF = mybir.ActivationFunctionType
ALU = mybir.AluOpType
AX = mybir.AxisListType

SEM_EPOCH = 30000
N_DMA_SEMS = 56
EPS = 1e-6
D = 1024
INC = 4880
DFF = 2816
TZ, TDT, TGQ, TGK, TGV, TDQ, TDK, TDV, TOKC = 0, 512, 528, 1040, 1168, 1296, 1808, 2320, 2832
PF_BADA, PF_N1, PF_N2, PF_SCW, PF_SCB, PF_SUB, PF_RCW, PF_RCB, PF_RBA, PF_RBX, PF_RLM, PF_BG, PF_FCW, PF_FCB, NPF = \
    0, 48, 56, 64, 96, 104, 105, 121, 125, 133, 141, 149, 181, 313, 357
PB_DTB, PB_ALOG, PB_D, PB_SNG, PB_GQ, PB_GK, PB_DQ, PB_DK, PB_LAM, PB_LI, NPB = 0, 16, 32, 40, 552, 616, 680, 744, 808, 1064, 1065


class Rec:
    def __init__(self, target):
        self._t = target

    def __getattr__(self, name):
        f = getattr(self._t, name)
        return lambda *a, **k: (f, a, k)


class Prog:
    def __init__(self, nc):
        self.nc = nc
        self.eng = {'pe': nc.tensor, 'act': nc.scalar, 'dve': nc.vector, 'pool': nc.gpsimd, 'sp': nc.sync}
        self.ops = []
        self.pending_dma_w = set()
        self.nbar = 0
        self.dummy = None

    def op(self, engine, fn, reads=(), writes=(), dma=False):
        rec = fn()
        assert isinstance(rec, tuple) and len(rec) == 3
        self.ops.append((engine, rec, tuple(reads), tuple(writes), dma))
        if dma:
            self.pending_dma_w.update(writes)

    def barrier(self):
        b = self.nbar
        self.nbar += 1
        nc = self.nc
        pend = list(self.pending_dma_w)
        self.pending_dma_w = set()
        engs = ['pe', 'act', 'dve', 'pool', 'sp']
        for e in engs:
            rd = pend if e == 'sp' else []
            if e == 'sp' or self.dummy is None:
                self.op(e, (lambda e=e: (self.eng[e].nop, (), {})), reads=rd, writes=[('bar', b, e)])
            else:
                fn, r2, w2 = self.dummy[e]
                self.op(e, fn, reads=r2, writes=[('bar', b, e)] + w2)
        for e in engs:
            self.op(e, (lambda e=e: (self.eng[e].nop, (), {})), reads=[('bar', b, f) for f in engs if f != e],
                    writes=[('bar2', b, e)])

    def emit(self, sem_ctx):
        ops = self.ops
        n = len(ops)
        last_w = {}
        rd_eng = {}
        rd_dma = {}
        deps = [None] * n
        needs_inc = [False] * n
        for i, (e, fn, rd, wr, dma) in enumerate(ops):
            d = set()
            for k in rd:
                w = last_w.get(k)
                if w is not None:
                    d.add(w)
            for k in wr:
                w = last_w.get(k)
                if w is not None:
                    d.add(w)
                re_ = rd_eng.get(k)
                if re_:
                    d.update(re_.values())
                rdm = rd_dma.get(k)
                if rdm:
                    d.update(rdm)
            d.discard(i)
            dd = []
            for j in d:
                ej, _, _, _, dmaj = ops[j]
                if ej == e and (not dmaj) and (not dma) and e == 'pe':
                    continue
                dd.append(j)
                needs_inc[j] = True
            deps[i] = dd
            for k in rd:
                if dma:
                    rd_dma.setdefault(k, []).append(i)
                else:
                    rd_eng.setdefault(k, {})[e] = i
            for k in wr:
                last_w[k] = i
                rd_eng[k] = {}
                rd_dma[k] = []
        cnt = {e: 0 for e in self.eng}
        tl = [None] * n
        dma_cnt = [0] * N_DMA_SEMS
        dma_rr = 0
        for i, (e, fn, rd, wr, dma) in enumerate(ops):
            if dma:
                s = dma_rr % N_DMA_SEMS
                dma_rr += 1
                dma_cnt[s] += 16
                tl[i] = ('dma', s, dma_cnt[s])
            elif needs_inc[i]:
                cnt[e] += 1
                tl[i] = ('eng', e, cnt[e])
        sems = {}
        for e in self.eng:
            for ep in range(cnt[e] // SEM_EPOCH + 1):
                sems[(e, ep)] = sem_ctx(f"s_{e}_{ep}")
        dsems = [sem_ctx(f"s_dma_{s}") for s in range(N_DMA_SEMS)]
        seen = {e: {f: 0 for f in self.eng} for e in self.eng}
        seen_dma = {e: [0] * N_DMA_SEMS for e in self.eng}
        for i, (e, fn, rd, wr, dma) in enumerate(ops):
            eng = self.eng[e]
            need_eng = {}
            need_dma = {}
            for j in deps[i]:
                t = tl[j]
                if t[0] == 'eng':
                    _, f, c = t
                    if c > seen[e][f] and c > need_eng.get(f, 0):
                        need_eng[f] = c
                else:
                    _, s, v = t
                    if v > seen_dma[e][s] and v > need_dma.get(s, 0):
                        need_dma[s] = v
            for f, c in need_eng.items():
                ep = (c - 1) // SEM_EPOCH
                eng.wait_ge(sems[(f, ep)], c - ep * SEM_EPOCH)
                seen[e][f] = c
            for s, v in need_dma.items():
                eng.wait_ge(dsems[s], v)
                seen_dma[e][s] = v
            inst = fn[0](*fn[1], **fn[2])
            t = tl[i]
            if t is not None:
                if t[0] == 'dma':
                    inst.then_inc(dsems[t[1]], 16)
                else:
                    c = t[2]
                    inst.then_inc(sems[(e, (c - 1) // SEM_EPOCH)], 1)
        return dict(n_ops=n, counts=cnt)


class Arena:
    def __init__(self, ap_f32, ncols):
        self.ap = ap_f32
        self.n = ncols
        self.top = 0

    def mark(self):
        return self.top

    def release(self, m):
        self.top = m

    def alloc(self, shape, dt):
        nel = int(np.prod(shape))
        cols = (nel * (2 if dt == BF16 else 4) + 3) // 4
        cols = (cols + 7) // 8 * 8
        assert self.top + cols <= self.n, f"arena overflow {self.top}+{cols}>{self.n}"
        v = self.ap[:, self.top:self.top + cols]
        self.top += cols
        if dt == BF16:
            v = v.bitcast(BF16)
        v = v[:, 0:nel]
        if len(shape) == 2:
            v = v.rearrange("p (a b) -> p a b", a=shape[0])
        elif len(shape) == 3:
            v = v.rearrange("p (a b c) -> p a b c", a=shape[0], b=shape[1])
        elif len(shape) == 4:
            v = v.rearrange("p (a b c d) -> p a b c d", a=shape[0], b=shape[1], c=shape[2])
        return v


class StopBuild(Exception):
    pass


def build_program(NB, DEPTH, NCTX, NLAT, debug=False, stop=None):
    NTOK = NCTX + NLAT
    NSUB = NTOK // 128
    NSC = NCTX // 128
    NLS = NLAT // 128
    R = NB + 1
    nc = bass.Bass("TRN2", target_bir_lowering=False)
    dram = lambda name, shape, dt, kind: nc.dram_tensor(name, shape, dt, kind=kind).ap()
    skind = "ExternalOutput" if debug else "Internal"
    xin = dram("xin", [NB, D, NTOK], F32, "ExternalInput")
    cT = dram("cT", [128, 8, R], F32, "ExternalInput")
    ropec = dram("ropec", [128, NLS, 32], F32, "ExternalInput")
    ropes = dram("ropes", [128, NLS, 32], F32, "ExternalInput")
    cmask = dram("cmask", [128, 6, 128], F32, "ExternalInput")
    pfd = dram("pf", [DEPTH, 128, NPF], F32, "ExternalInput")
    pbd = dram("pb", [DEPTH, NPB], F32, "ExternalInput")
    rgwd = dram("rgw", [DEPTH, 128, 16, 128], F32, "ExternalInput")
    w_ada = dram("w_ada", [DEPTH, D, 6 * D], F32, "ExternalInput")
    w_in = dram("w_in", [DEPTH, D, INC], F32, "ExternalInput")
    w_gate = dram("w_gate", [DEPTH, 4, D, D], F32, "ExternalInput")
    w_br = dram("w_br", [DEPTH, 4, 512, D], F32, "ExternalInput")
    w_out = dram("w_out", [DEPTH, D, D], F32, "ExternalInput")
    w_up = dram("w_up", [DEPTH, D, 2 * DFF], F32, "ExternalInput")
    w_down = dram("w_down", [DEPTH, DFF, D], F32, "ExternalInput")
    outd = dram("out", [NB, D, NLAT], F32, "ExternalOutput")
    xs = [dram(f"xs{s}", [D, NTOK], F32, skind) for s in range(NB)]
    xm = dram("xm", [D, NTOK], F32, skind)
    u_tok = dram("u_tok", [NTOK, TOKC], BF16, skind)
    dt_tok = dram("dt_tok", [NTOK, 16], F32, skind)
    u_fm = dram("u_fm", [2048, NTOK], BF16, skind)
    brd = [dram(f"br{k}", [512, NTOK], BF16, skind) for k in range(4)]

    st = ExitStack()
    with st:
        ARENA_COLS = 52992
        arena_t = st.enter_context(nc.sbuf_tensor("arena", [128, ARENA_COLS], F32))
        psum_ts = [st.enter_context(nc.psum_tensor(f"psum{b}", [128, 512], F32)) for b in range(8)]
        A = Arena(arena_t, ARENA_COLS)
        PB = [psum_ts[b][:, :] for b in range(8)]
        PBH = [psum_ts[b][:, :].bitcast(BF16) for b in range(8)]
        P = Prog(nc)
        V, S_, G, T, SY = Rec(nc.vector), Rec(nc.scalar), Rec(nc.gpsimd), Rec(nc.tensor), Rec(nc.sync)

        def dve(fn, r, w):
            P.op('dve', fn, r, w)

        def act(fn, r, w):
            P.op('act', fn, r, w)

        def pool(fn, r, w):
            P.op('pool', fn, r, w)

        def pe(fn, r, w):
            P.op('pe', fn, r, w)

        def ld(out, in_, r, w):
            P.op('sp', lambda: SY.dma_start(out=out, in_=in_), r, w, dma=True)

        def stq(out, in_, r, w):
            P.op('act', lambda: S_.dma_start(out=out, in_=in_), r, w, dma=True)

        def mm(out, lhsT, rhs, start, stop, r, w, skip=False):
            pe(lambda: T.matmul(out, lhsT=lhsT, rhs=rhs, start=start, stop=stop, skip_group_check=skip), r, w)

        cm32 = A.alloc([6, 128], F32)
        ld(cm32, cmask, [], ['cm32'])
        LI, LS, UI, US, ONES, IDN = range(6)
        ident = A.alloc([128], BF16)
        ones_bf = A.alloc([128], BF16)
        dve(lambda: V.tensor_copy(out=ident, in_=cm32[:, IDN, :]), ['cm32'], ['ident'])
        dve(lambda: V.tensor_copy(out=ones_bf, in_=cm32[:, ONES, :]), ['cm32'], ['ones_bf'])
        rc = A.alloc([NLS, 32], F32)
        rs = A.alloc([NLS, 32], F32)
        ld(rc, ropec, [], ['rc'])
        ld(rs, ropes, [], ['rs'])
        scT = A.alloc([8, R], F32)
        ld(scT, cT, [], ['scT'])
        act(lambda: S_.activation(out=scT, in_=scT, func=AF.Silu), ['scT'], ['scT'])
        pf = A.alloc([NPF], F32)
        pbc = A.alloc([NPB], F32)
        modt = A.alloc([48, R], F32)
        A1 = A.alloc([8, R], F32)
        A2 = A.alloc([8, R], F32)
        aneg = A.alloc([16], F32)
        rgcp = A.alloc([8], F32)
        lamt = A.alloc([8], F32)
        gqk = A.alloc([10, 64], F32)
        gdk = A.alloc([16, 64], F32)
        dum = A.alloc([8], F32)
        dve(lambda: V.memset(dum, 0.0), [], [('dum', 'act'), ('dum', 'dve'), ('dum', 'pool')])
        P.dummy = {
            'pe': (lambda: T.matmul(PB[7][0:1, 0:2], lhsT=ones_bf[0:1, 0:1], rhs=ones_bf[0:1, 0:2], start=True, stop=True),
                   ['ones_bf'], [('ps', 7)]),
            'act': (lambda: S_.copy(out=dum[:, 0:1], in_=dum[:, 1:2]), [], [('dum', 'act')]),
            'dve': (lambda: V.tensor_copy(out=dum[:, 2:3], in_=dum[:, 3:4]), [], [('dum', 'dve')]),
            'pool': (lambda: G.tensor_copy(out=dum[:, 4:5], in_=dum[:, 5:6]), [], [('dum', 'pool')]),
        }
        PERSIST = A.mark()

        pbank = [0]

        def chk(name):
            if stop == name:
                raise StopBuild()

        def nb(lo=0, hi=8):
            b = lo + (pbank[0] % (hi - lo))
            pbank[0] += 1
            return b

        def tiles_of(include_ctx=True, w=512):
            t = [(NCTX * i // (-(-NCTX // w)), NCTX // (-(-NCTX // w))) for i in range(-(-NCTX // w))] if include_ctx else []
            t += [(NCTX + w * i, w) for i in range(NLAT // w)]
            return t

        def load_weight(dst_views, src_views, stage, tag):
            for i, (dv, sv) in enumerate(zip(dst_views, src_views)):
                sg = stage[i % len(stage)]
                sk = (tag + '_stg', i % len(stage))
                shp = dv.shape
                sgv = sg
                if len(shp) == 2:
                    sgv = sg[:, 0:shp[1]]
                else:
                    sgv = sg[:, 0:shp[1] * shp[2]].rearrange("p (a b) -> p a b", a=shp[1])
                ld(sgv, sv, [], [sk])
                pool(lambda dv=dv, sgv=sgv: G.tensor_copy(out=dv, in_=sgv), [sk], [(tag, i)])

        def norm_mod(xt, xk, W, sq, rstd, h, hk, Am, Bm, r, sfx):
            act(lambda: S_.activation(out=sq[:, :, 0:W], in_=xt[:, :, 0:W], func=AF.Square), [xk], ['sq' + sfx])
            b = nb(6, 8)
            for kc in range(8):
                mm(PB[b][:, 0:W], ones_bf, sq[:, kc, 0:W], kc == 0, kc == 7, ['sq' + sfx, 'ones_bf'], [('ps', b)])
            act(lambda: S_.activation(out=rstd[:, 0:W], in_=PB[b][:, 0:W], func=AF.Sqrt, scale=1.0 / D, bias=EPS),
                [('ps', b)], ['rstd' + sfx])
            dve(lambda: V.reciprocal(out=rstd[:, 0:W], in_=rstd[:, 0:W]), ['rstd' + sfx], ['rstd' + sfx])
            for kc in range(8):
                tk = ('tmpn', kc % 2)
                tv = tmpn[kc % 2]
                dve(lambda kc=kc, tv=tv: V.tensor_tensor(out=tv[:, 0:W], in0=xt[:, kc, 0:W], in1=rstd[:, 0:W], op=ALU.mult),
                    [xk, 'rstd' + sfx], [tk])
                act(lambda kc=kc, tv=tv: S_.activation(out=h[:, kc, 0:W], in_=tv[:, 0:W], func=AF.Identity,
                                                       scale=Am[:, kc, r:r + 1], bias=Bm[:, kc, r:r + 1]),
                    [tk, 'mods'], [hk])

        tmpn = [None, None]

        def emit_layer(l):
            last = (l == DEPTH - 1)
            chk('C0')
            A.release(PERSIST)
            ld(pf, pfd[l], [], ['pf'])
            ld(pbc, pbd[l:l + 1, :].partition_broadcast(128), [], ['pbc'])
            chk('S0p')
            wst = [A.alloc([6 * D], F32) for _ in range(2)]
            b0 = 0
            dve(lambda: V.memset(PB[b0][:, 0:48 * R], 0.0), [], [('ps', b0)])
            for kc in range(8):
                w = wst[kc % 2]
                wk = ('wst', kc % 2)
                ld(w, w_ada[l, kc * 128:(kc + 1) * 128, :], [], [wk])
                for j in range(48):
                    mm(PB[b0][:, j * R:(j + 1) * R], w[:, j * 128:(j + 1) * 128], scT[:, kc, :], False, kc == 7,
                       [wk, 'scT'], [('ps', b0)], skip=True)
                chk('S0k%d' % kc)
            chk('S0w')
            act(lambda: S_.copy(out=modt.rearrange("p a b -> p (a b)"), in_=PB[b0][:, 0:48 * R]), [('ps', b0)], ['mods'])
            chk('S0c')
            dve(lambda: V.tensor_tensor(out=modt, in0=modt,
                                        in1=pf[:, PF_BADA:PF_BADA + 48, None].to_broadcast([128, 48, R]), op=ALU.add),
                ['mods', 'pf'], ['mods'])
            chk('S0m')
            for (Ax, sc0, ng) in ((A1, 8, PF_N1), (A2, 32, PF_N2)):
                dve(lambda Ax=Ax, sc0=sc0: V.tensor_scalar(out=Ax, in0=modt[:, sc0:sc0 + 8, :], scalar1=1.0, scalar2=None,
                                                           op0=ALU.add), ['mods'], ['mods'])
                dve(lambda Ax=Ax, ng=ng: V.tensor_tensor(out=Ax, in0=Ax, in1=pf[:, ng:ng + 8, None].to_broadcast([128, 8, R]),
                                                         op=ALU.mult), ['mods', 'pf'], ['mods'])
            B1 = modt[:, 0:8, :]
            G1 = modt[:, 16:24, :]
            B2 = modt[:, 24:32, :]
            G2 = modt[:, 40:48, :]
            chk('S0n')
            act(lambda: S_.activation(out=aneg, in_=pbc[:, PB_ALOG:PB_ALOG + 16], func=AF.Exp), ['pbc'], ['aneg'])
            dve(lambda: V.tensor_scalar(out=aneg, in0=aneg, scalar1=-1.0, scalar2=None, op0=ALU.mult), ['aneg'], ['aneg'])
            act(lambda: S_.activation(out=rgcp, in_=pf[:, PF_RLM:PF_RLM + 8], func=AF.Exp, scale=-1.0), ['pf'], ['rgcp'])
            act(lambda: S_.activation(out=rgcp, in_=rgcp, func=AF.Ln, bias=1.0), ['rgcp'], ['rgcp'])
            dve(lambda: V.tensor_scalar(out=rgcp, in0=rgcp, scalar1=-8.0, scalar2=None, op0=ALU.mult), ['rgcp'], ['rgcp'])
            lt = A.alloc([128], F32)
            dve(lambda: V.tensor_tensor(out=lt[:, 0:64], in0=pbc[:, PB_LAM:PB_LAM + 64], in1=pbc[:, PB_LAM + 64:PB_LAM + 128],
                                        op=ALU.mult), ['pbc'], ['lt'])
            dve(lambda: V.tensor_tensor(out=lt[:, 64:128], in0=pbc[:, PB_LAM + 128:PB_LAM + 192],
                                        in1=pbc[:, PB_LAM + 192:PB_LAM + 256], op=ALU.mult), ['pbc', 'lt'], ['lt'])
            dve(lambda: V.tensor_reduce(out=lamt[:, 3:5], in_=lt.rearrange("p (a b) -> p a b", a=2), axis=AX.X, op=ALU.add),
                ['lt'], ['lamt'])
            act(lambda: S_.activation(out=lamt[:, 3:5], in_=lamt[:, 3:5], func=AF.Exp), ['lamt'], ['lamt'])
            dve(lambda: V.tensor_tensor(out=lamt[:, 0:1], in0=lamt[:, 4:5], in1=lamt[:, 3:4], op=ALU.subtract), ['lamt'], ['lamt'])
            dve(lambda: V.tensor_tensor(out=lamt[:, 0:1], in0=lamt[:, 0:1], in1=pbc[:, PB_LI:PB_LI + 1], op=ALU.subtract),
                ['lamt', 'pbc'], ['lamt'])
            dve(lambda: V.tensor_scalar(out=lamt[:, 1:2], in0=pbc[:, PB_LI:PB_LI + 1], scalar1=-1.0, scalar2=1.0,
                                        op0=ALU.mult, op1=ALU.add), ['lamt', 'pbc'], ['lamt'])
            dve(lambda: V.tensor_tensor(out=lamt[:, 2:3], in0=lamt[:, 1:2], in1=pf[:, PF_SUB:PF_SUB + 1], op=ALU.mult),
                ['lamt', 'pf'], ['lamt'])
            chk('S0l')
            dve(lambda: V.tensor_copy(out=gqk[:, 0:8, :], in_=pbc[:, None, PB_GQ:PB_GQ + 64].to_broadcast([128, 8, 64])),
                ['pbc'], ['gqk'])
            dve(lambda: V.tensor_copy(out=gqk[:, 8:10, :], in_=pbc[:, None, PB_GK:PB_GK + 64].to_broadcast([128, 2, 64])),
                ['pbc', 'gqk'], ['gqk'])
            dve(lambda: V.tensor_copy(out=gdk[:, 0:8, :], in_=pbc[:, None, PB_DQ:PB_DQ + 64].to_broadcast([128, 8, 64])),
                ['pbc'], ['gdk'])
            dve(lambda: V.tensor_copy(out=gdk[:, 8:16, :], in_=pbc[:, None, PB_DK:PB_DK + 64].to_broadcast([128, 8, 64])),
                ['pbc', 'gdk'], ['gdk'])
            P.barrier()
            chk('S0' + (kind if 'S0' in ('P3', 'P4') else ''))

            for s in range(NB):
                xsrc = xin[s] if l == 0 else xs[s]

                A.release(PERSIST)
                wb = A.alloc([8, INC], BF16)
                M1 = A.mark()
                stg = [A.alloc([INC], F32) for _ in range(2)]
                load_weight([wb[:, kc, :] for kc in range(8)], [w_in[l, kc * 128:(kc + 1) * 128, :] for kc in range(8)],
                            stg, 'wb')
                P.barrier()
                chk('P1w' + (kind if 'P1w' in ('P3', 'P4') else ''))
                A.release(M1)
                WB = [('wb', kc) for kc in range(8)]
                xt2 = [A.alloc([8, 512], F32) for _ in range(2)]
                sq = A.alloc([8, 512], BF16)
                rstd = A.alloc([512], F32)
                tmpn[0] = A.alloc([512], F32)
                tmpn[1] = A.alloc([512], F32)
                h2 = [A.alloc([8, 512], BF16) for _ in range(2)]
                ofm = [A.alloc([4, 512], BF16) for _ in range(2)]
                otok = [A.alloc([TOKC], BF16) for _ in range(2)]
                odt = [A.alloc([16], F32) for _ in range(2)]
                tokblocks = [(0, 0, 512)] + [(512 + 512 * i, 1536 + 512 * i, 512) for i in range(4)] + [(2560, 3584, 272)]
                fmcols = [512 + 128 * j for j in range(8)] + [3856 + 128 * j for j in range(8)]
                ev = [0]
                for ti, (t0, W) in enumerate(tiles_of()):
                    r = NB if t0 < NCTX else s
                    xt = xt2[ti % 2]
                    xk = ('xt', ti % 2)
                    h = h2[ti % 2]
                    hk = ('h', ti % 2)
                    ld(xt[:, :, 0:W], xsrc[:, t0:t0 + W].rearrange("(c p) t -> p c t", p=128), [('xs', s)], [xk])
                    chk('P1a')
                    norm_mod(xt, xk, W, sq, rstd, h, hk, A1, B1, r, '')
                    chk('P1b')
                    for j4 in range(4):
                        o = ofm[j4 % 2]
                        ok = ('ofm', j4 % 2)
                        for jj in range(4):
                            j = j4 * 4 + jj
                            c0 = fmcols[j]
                            b = nb(0, 6)
                            for kc in range(8):
                                mm(PB[b][:, 0:W], wb[:, kc, c0:c0 + 128], h[:, kc, 0:W], kc == 0, kc == 7,
                                   [WB[kc], hk], [('ps', b)])
                            ev[0] += 1
                            if ev[0] % 2:
                                act(lambda o=o, jj=jj, b=b: S_.copy(out=o[:, jj, 0:W], in_=PB[b][:, 0:W]), [('ps', b)], [ok])
                            else:
                                dve(lambda o=o, jj=jj, b=b: V.tensor_copy(out=o[:, jj, 0:W], in_=PB[b][:, 0:W]), [('ps', b)], [ok])
                        stq(u_fm[j4 * 512:(j4 + 1) * 512, t0:t0 + W].rearrange("(c p) t -> p c t", p=128), o[:, :, 0:W],
                            [ok], [('u_fm', j4)])
                    chk('P1d')
                    for si in range(W // 128):
                        tg = t0 + si * 128
                        ot = otok[si % 2]
                        otk = ('otok', si % 2)
                        od = odt[si % 2]
                        for (oc0, wc0, cw) in tokblocks:
                            b = nb(0, 6)
                            for kc in range(8):
                                mm(PB[b][:, 0:cw], h[:, kc, si * 128:(si + 1) * 128], wb[:, kc, wc0:wc0 + cw], kc == 0, kc == 7,
                                   [WB[kc], hk], [('ps', b)])
                            ev[0] += 1
                            if ev[0] % 2:
                                act(lambda ot=ot, b=b, oc0=oc0, cw=cw: S_.copy(out=ot[:, oc0:oc0 + cw], in_=PB[b][:, 0:cw]),
                                    [('ps', b)], [otk])
                            else:
                                dve(lambda ot=ot, b=b, oc0=oc0, cw=cw: V.tensor_copy(out=ot[:, oc0:oc0 + cw], in_=PB[b][:, 0:cw]),
                                    [('ps', b)], [otk])
                            if oc0 == 512:
                                dve(lambda od=od, b=b: V.tensor_copy(out=od, in_=PB[b][:, 0:16]), [('ps', b)], [otk])
                        stq(u_tok[tg:tg + 128, :], ot, [otk], [('u_tok', tg // 128)])
                        stq(dt_tok[tg:tg + 128, :], od, [otk], [('dt_tok', tg // 128)])
                P.barrier()
                chk('P1' + (kind if 'P1' in ('P3', 'P4') else ''))

                A.release(PERSIST)
                BCfm = A.alloc([4, NTOK], BF16)
                xs_tok = A.alloc([NSUB, 512], BF16)
                B_tok = A.alloc([NSUB, 256], BF16)
                dtr = A.alloc([NSUB, 16], F32)
                dtv = A.alloc([NSUB, 16], F32)
                av = A.alloc([NSUB, 16], F32)
                ev_ = A.alloc([NSUB, 16], F32)
                wg = A.alloc([NSUB, 16], F32)
                eA = A.alloc([NSUB, 16], F32)
                gS = A.alloc([512], F32)
                dtb = A.alloc([16], F32)
                M2 = A.mark()
                rawp = [A.alloc([NTOK + 6], BF16) for _ in range(2)]
                acc = A.alloc([NTOK], F32)
                xcv = [A.alloc([NTOK], BF16) for _ in range(2)]
                for i in range(2):
                    pool(lambda i=i: G.memset(rawp[i], 0.0), [], [('rawp', i)])
                SEG = [(0, NCTX, 1), (NCTX, NLAT, 4 + NCTX)]
                for j in range(8):
                    rp = rawp[j % 2]
                    rk = ('rawp', j % 2)
                    for (g0, gl, c0) in SEG:
                        ld(rp[:, c0:c0 + gl], u_fm[j * 128:(j + 1) * 128, g0:g0 + gl], [('u_fm', j // 4)], [rk])
                    for (g0, gl, c0) in SEG:
                        for k in range(4):
                            src = rp[:, c0 + k - 1:c0 + k - 1 + gl]
                            wk_ = pf[:, PF_SCW + j * 4 + k:PF_SCW + j * 4 + k + 1]
                            if k == 0:
                                dve(lambda src=src, wk_=wk_, g0=g0, gl=gl, j=j: V.tensor_scalar(
                                    out=acc[:, g0:g0 + gl], in0=src, scalar1=wk_, scalar2=pf[:, PF_SCB + j:PF_SCB + j + 1],
                                    op0=ALU.mult, op1=ALU.add), [rk, 'pf'], ['acc'])
                            else:
                                dve(lambda src=src, wk_=wk_, g0=g0, gl=gl: V.scalar_tensor_tensor(
                                    out=acc[:, g0:g0 + gl], in0=src, scalar=wk_, in1=acc[:, g0:g0 + gl],
                                    op0=ALU.mult, op1=ALU.add), [rk, 'pf', 'acc'], ['acc'])
                    if j < 6:
                        xc = xcv[j % 2]
                        xck = ('xcv', j % 2)
                    else:
                        xc = BCfm[:, j - 4, :]
                        xck = ('BCfm', j - 4)
                    act(lambda xc=xc: S_.activation(out=xc, in_=acc, func=AF.Silu), ['acc'], [xck])
                    if j in (4, 5):
                        pool(lambda xc=xc, j=j: G.tensor_copy(out=BCfm[:, j - 4, :], in_=xc), [xck], [('BCfm', j - 4)])
                    if j < 6:
                        for s0 in range(0, NSUB, 8):
                            ns = min(8, NSUB - s0)
                            b = nb(0, 6)
                            for q in range(ns):
                                pe(lambda q=q, s0=s0, b=b, xc=xc: T.transpose(out=PBH[b][:, q * 128:(q + 1) * 128],
                                                                             in_=xc[:, (s0 + q) * 128:(s0 + q + 1) * 128],
                                                                             identity=ident), [xck, 'ident'], [('ps', b)])
                            if j < 4:
                                dst = xs_tok[:, s0:s0 + ns, j * 128:(j + 1) * 128]
                                dk_ = 'xs_tok'
                            else:
                                dst = B_tok[:, s0:s0 + ns, (j - 4) * 128:(j - 3) * 128]
                                dk_ = 'B_tok'
                            dve(lambda dst=dst, b=b, ns=ns: V.tensor_copy(
                                out=dst, in_=PBH[b][:, 0:ns * 128].rearrange("p (a b) -> p a b", a=ns)), [('ps', b)], [dk_])
                ld(dtr, dt_tok.rearrange("(s p) j -> p s j", p=128), [('dt_tok', i) for i in range(NSUB)], ['dtr'])
                dve(lambda: V.tensor_copy(out=dtb, in_=pbc[:, PB_DTB:PB_DTB + 16]), ['pbc'], ['dtb'])
                dve(lambda: V.tensor_copy(out=gS, in_=pbc[:, PB_SNG:PB_SNG + 512]), ['pbc'], ['gS'])
                dve(lambda: V.tensor_tensor(out=dtr, in0=dtr, in1=dtb[:, None, :].to_broadcast([128, NSUB, 16]), op=ALU.add),
                    ['dtr', 'dtb'], ['dtr'])
                act(lambda: S_.activation(out=dtv, in_=dtr, func=AF.Exp), ['dtr'], ['dtv'])
                act(lambda: S_.activation(out=dtv, in_=dtv, func=AF.Ln, bias=1.0), ['dtv'], ['dtv'])
                dve(lambda: V.tensor_tensor(out=av, in0=dtv, in1=aneg[:, None, :].to_broadcast([128, NSUB, 16]), op=ALU.mult),
                    ['dtv', 'aneg'], ['av'])
                HALF = (NSUB + 1) // 2
                for c0 in range(0, NSUB, HALF):
                    ncn = min(HALF, NSUB - c0)
                    b1, b2, b3 = 0, 1, 2
                    for c in range(c0, c0 + ncn):
                        o = (c - c0) * 16
                        mm(PB[b1][:, o:o + 8], cm32[:, LI, :], av[:, c, 0:8], True, True, ['cm32', 'av'], [('ps', b1)])
                        mm(PB[b1][:, o + 8:o + 16], cm32[:, UI, :], av[:, c, 8:16], True, True, ['cm32', 'av'], [('ps', b1)])
                        mm(PB[b2][:, o:o + 8], cm32[:, US, :], av[:, c, 0:8], True, True, ['cm32', 'av'], [('ps', b2)])
                        mm(PB[b2][:, o + 8:o + 16], cm32[:, LS, :], av[:, c, 8:16], True, True, ['cm32', 'av'], [('ps', b2)])
                        mm(PB[b3][:, o:o + 16], cm32[:, ONES, :], av[:, c, :], True, True, ['cm32', 'av'], [('ps', b3)])
                    for (bb, dst, dk_) in ((b1, ev_, 'ev'), (b2, wg, 'wg'), (b3, eA, 'eA')):
                        act(lambda bb=bb, dst=dst, c0=c0, ncn=ncn: S_.activation(
                            out=dst[:, c0:c0 + ncn, :], in_=PB[bb][:, 0:ncn * 16].rearrange("p (a b) -> p a b", b=16),
                            func=AF.Exp), [('ps', bb)], [dk_])
                dve(lambda: V.tensor_tensor(out=wg, in0=wg, in1=dtv, op=ALU.mult), ['wg', 'dtv'], ['wg'])
                P.barrier()
                chk('P2a' + (kind if 'P2a' in ('P3', 'P4') else ''))
                A.release(M2)
                Sb_all = A.alloc([NSUB, 512], BF16)
                Sst = [A.alloc([512], F32) for _ in range(2)]
                Sf_bf = A.alloc([512], BF16)
                xw = [A.alloc([512], BF16) for _ in range(2)]
                CBm = A.alloc([2, 2, 128], F32)
                aM = A.alloc([16, 128], F32)
                E_sb = A.alloc([16, 128], F32)
                Wt = A.alloc([16, 128], BF16)
                zt = [A.alloc([512], BF16) for _ in range(2)]
                t1 = A.alloc([512], F32)
                t2 = A.alloc([512], F32)
                t3 = A.alloc([512], F32)
                yg = A.alloc([512], F32)
                ssq = A.alloc([4], F32)
                yn = A.alloc([512], BF16)
                ost = [A.alloc([4, 128], BF16) for _ in range(2)]
                junk = A.alloc([256], F32)
                for d in range(2):
                    dve(lambda d=d: V.memset(Sst[d], 0.0), [], [('Sst', d)])
                bw_order = list(range(NSC - 1, -1, -1)) + list(range(NSUB - 1, NSC - 1, -1))

                def state_update(c, d, xwk):
                    x_ = xw[xwk % 2]
                    k_ = ('xw', xwk % 2)
                    dve(lambda: V.tensor_tensor(out=x_.rearrange("p (h e) -> p h e", h=8),
                                                in0=xs_tok[:, c, :].rearrange("p (h e) -> p h e", h=8),
                                                in1=wg[:, c, d * 8:d * 8 + 8, None].to_broadcast([128, 8, 64]), op=ALU.mult),
                        ['xs_tok', 'wg'], [k_])
                    b = nb(4, 6)
                    for g in range(2):
                        mm(PB[b][:, g * 256:(g + 1) * 256], B_tok[:, c, g * 128:(g + 1) * 128], x_[:, g * 256:(g + 1) * 256],
                           True, True, ['B_tok', k_], [('ps', b)])
                    dve(lambda: V.tensor_tensor(out=Sst[d].rearrange("p (h e) -> p h e", h=8),
                                                in0=Sst[d].rearrange("p (h e) -> p h e", h=8),
                                                in1=eA[:, c, d * 8:d * 8 + 8, None].to_broadcast([128, 8, 64]), op=ALU.mult),
                        [('Sst', d), 'eA'], [('Sst', d)])
                    dve(lambda: V.tensor_tensor(out=Sst[d], in0=Sst[d], in1=PB[b], op=ALU.add), [('Sst', d), ('ps', b)],
                        [('Sst', d)])

                for i, c in enumerate(bw_order):
                    act(lambda c=c: S_.copy(out=Sb_all[:, c, :], in_=Sst[1]), [('Sst', 1)], [('Sb_all', c)])
                    if i < len(bw_order) - 1:
                        state_update(c, 1, i)
                for c in range(NSUB):
                    cs_ = slice(c * 128, (c + 1) * 128)
                    z_ = zt[c % 2]
                    zk = ('zt', c % 2)
                    ld(z_, u_tok[c * 128:(c + 1) * 128, TZ:TZ + 512], [('u_tok', c)], [zk])
                    act(lambda: S_.copy(out=Sf_bf, in_=Sst[0]), [('Sst', 0)], ['Sf_bf'])
                    bcb = 6
                    for g in range(2):
                        mm(PB[bcb][:, g * 128:(g + 1) * 128], BCfm[:, g, cs_], BCfm[:, 2 + g, cs_], True, True,
                           [('BCfm', g), ('BCfm', 2 + g)], [('ps', bcb)])
                    for d, mk in ((0, LI), (1, UI)):
                        dve(lambda d=d, mk=mk: V.tensor_tensor(
                            out=CBm[:, d, :, :], in0=PB[bcb][:, 0:256].rearrange("p (g l) -> p g l", g=2),
                            in1=cm32[:, mk, None, :].to_broadcast([128, 2, 128]), op=ALU.mult),
                            [('ps', bcb), 'cm32'], ['CBm'])
                    for d, mk in ((0, US), (1, LS)):
                        pool(lambda d=d, mk=mk, c=c: G.tensor_tensor(
                            out=aM[:, d * 8:d * 8 + 8, :], in0=cm32[:, mk, None, :].to_broadcast([128, 8, 128]),
                            in1=av[:, c, d * 8:d * 8 + 8, None].to_broadcast([128, 8, 128]), op=ALU.mult),
                            ['cm32', 'av'], [('aM', d)])
                    for q4 in range(4):
                        b = q4
                        for jj in range(4):
                            j = q4 * 4 + jj
                            d = j // 8
                            mm(PB[b][:, jj * 128:(jj + 1) * 128], aM[:, j, :], cm32[:, LI if d == 0 else UI, :], True, True,
                               [('aM', d), 'cm32'], [('ps', b)])
                        act(lambda q4=q4, b=b: S_.activation(out=E_sb[:, q4 * 4:q4 * 4 + 4, :],
                                                              in_=PB[b].rearrange("p (a l) -> p a l", a=4), func=AF.Exp),
                            [('ps', b)], [('E_sb', q4)])
                    dve(lambda: V.tensor_tensor(
                        out=E_sb.rearrange("p (a h) l -> p a h l", h=4), in0=E_sb.rearrange("p (a h) l -> p a h l", h=4),
                        in1=CBm.rearrange("p d g l -> p (d g) l")[:, :, None, :].to_broadcast([128, 4, 4, 128]), op=ALU.mult),
                        [('E_sb', q) for q in range(4)] + ['CBm'], [('E_sb', q) for q in range(4)])
                    dve(lambda c=c: V.tensor_tensor(out=Wt, in0=E_sb, in1=dtv[:, c, :, None].to_broadcast([128, 16, 128]),
                                                    op=ALU.mult), [('E_sb', q) for q in range(4)] + ['dtv'], ['Wt'])
                    by = 7
                    for hh in range(8):
                        mm(PB[by][:, hh * 64:(hh + 1) * 64], Wt[:, hh, :], xs_tok[:, c, hh * 64:(hh + 1) * 64], True, False,
                           ['Wt', 'xs_tok'], [('ps', by)])
                        mm(PB[by][:, hh * 64:(hh + 1) * 64], Wt[:, 8 + hh, :], xs_tok[:, c, hh * 64:(hh + 1) * 64], False, True,
                           ['Wt', 'xs_tok'], [('ps', by)])
                    bof, bob = 4, 5
                    for g in range(2):
                        mm(PB[bof][:, g * 256:(g + 1) * 256], BCfm[:, 2 + g, cs_], Sf_bf[:, g * 256:(g + 1) * 256], True, True,
                           [('BCfm', 2 + g), 'Sf_bf'], [('ps', bof)])
                        mm(PB[bob][:, g * 256:(g + 1) * 256], BCfm[:, 2 + g, cs_], Sb_all[:, c, g * 256:(g + 1) * 256], True, True,
                           [('BCfm', 2 + g), ('Sb_all', c)], [('ps', bob)])
                    dve(lambda c=c: V.tensor_tensor(out=t1.rearrange("p (h e) -> p h e", h=8),
                                                    in0=PB[bof].rearrange("p (h e) -> p h e", h=8),
                                                    in1=ev_[:, c, 0:8, None].to_broadcast([128, 8, 64]), op=ALU.mult),
                        [('ps', bof), 'ev'], ['t1'])
                    dve(lambda c=c: V.tensor_tensor(out=t2.rearrange("p (h e) -> p h e", h=8),
                                                    in0=PB[bob].rearrange("p (h e) -> p h e", h=8),
                                                    in1=ev_[:, c, 8:16, None].to_broadcast([128, 8, 64]), op=ALU.mult),
                        [('ps', bob), 'ev'], ['t2'])
                    pool(lambda c=c: G.tensor_tensor(out=t3.rearrange("p (h e) -> p h e", h=8),
                                                     in0=xs_tok[:, c, :].rearrange("p (h e) -> p h e", h=8),
                                                     in1=pbc[:, PB_D:PB_D + 8, None].to_broadcast([128, 8, 64]), op=ALU.mult),
                         ['xs_tok', 'pbc'], ['t3'])
                    pool(lambda: G.tensor_tensor(out=t1, in0=t1, in1=t2, op=ALU.add), ['t1', 't2'], ['t1'])
                    pool(lambda: G.tensor_tensor(out=t1, in0=t1, in1=t3, op=ALU.add), ['t1', 't3'], ['t1'])
                    dve(lambda: V.tensor_tensor(out=t1, in0=t1, in1=PB[by], op=ALU.add), ['t1', ('ps', by)], ['t1'])
                    if c < NSUB - 1:
                        state_update(c, 0, c)
                    act(lambda z_=z_: S_.activation(out=t2, in_=z_, func=AF.Silu), [zk, 't2'], ['t2'])
                    dve(lambda: V.tensor_tensor(out=yg, in0=t1, in1=t2, op=ALU.mult), ['t1', 't2'], ['yg'])
                    for g in range(2):
                        act(lambda g=g: S_.activation(out=junk, in_=yg[:, g * 256:(g + 1) * 256], func=AF.Square,
                                                      accum_out=ssq[:, g:g + 1]), ['yg', 'junk'], ['ssq', 'junk'])
                    act(lambda: S_.activation(out=ssq[:, 2:4], in_=ssq[:, 0:2], func=AF.Sqrt, scale=1.0 / 256, bias=EPS),
                        ['ssq'], ['ssq'])
                    dve(lambda: V.reciprocal(out=ssq[:, 2:4], in_=ssq[:, 2:4]), ['ssq'], ['ssq'])
                    for g in range(2):
                        dve(lambda g=g: V.scalar_tensor_tensor(out=yn[:, g * 256:(g + 1) * 256], in0=yg[:, g * 256:(g + 1) * 256],
                                                               scalar=ssq[:, 2 + g:3 + g], in1=gS[:, g * 256:(g + 1) * 256],
                                                               op0=ALU.mult, op1=ALU.mult), ['yg', 'ssq', 'gS'], ['yn'])
                    bt = 6
                    for k in range(4):
                        pe(lambda k=k: T.transpose(out=PBH[bt][:, 512 + k * 128:512 + (k + 1) * 128],
                                                   in_=yn[:, k * 128:(k + 1) * 128], identity=ident),
                           ['yn', 'ident'], [('ps', bt)])
                    o_ = ost[c % 2]
                    okk = ('ost', c % 2)
                    act(lambda o_=o_: S_.copy(out=o_, in_=PBH[bt][:, 512:1024].rearrange("p (k t) -> p k t", k=4)),
                        [('ps', bt)], [okk])
                    stq(brd[0][:, cs_].rearrange("(k p) t -> p k t", p=128), o_, [okk], [('br0', c)])
                P.barrier()
                chk('P2' + (kind if 'P2' in ('P3', 'P4') else ''))

                for kind in ('gqa', 'diff'):
                    A.release(PERSIST)
                    NH = 10 if kind == 'gqa' else 16
                    NQ = 8
                    NT = 6 if kind == 'gqa' else 8
                    c_in = TGQ if kind == 'gqa' else TDQ
                    w_in_cols = 640 if kind == 'gqa' else 1024
                    gq_, gk_n = (gqk, 'gqk') if kind == 'gqa' else (gdk, 'gdk')
                    qkT = A.alloc([NT, NTOK], BF16)
                    if kind == 'gqa':
                        vaug = A.alloc([NSUB, 2, 128], BF16)
                        pool(lambda: G.memset(vaug, 1.0), [], ['vaug'])
                        vtmp = [A.alloc([128], BF16) for _ in range(2)]
                    else:
                        vd = A.alloc([NSUB, 512], BF16)
                        ld(vd, u_tok[:, TDV:TDV + 512].rearrange("(s p) c -> p s c", p=128),
                           [('u_tok', i) for i in range(NSUB)], ['vd'])
                    qin = [A.alloc([NH, 64], BF16) for _ in range(2)]
                    sqf = A.alloc([NH, 64], F32)
                    qn = A.alloc([NH, 64], F32)
                    ssn = A.alloc([2, NH], F32)
                    ta = A.alloc([NH, 32], F32)
                    tb = A.alloc([NH, 32], F32)
                    tc_ = A.alloc([NH, 32], F32)
                    td = A.alloc([NH, 32], F32)
                    qr = [A.alloc([NH + 2, 64], BF16) for _ in range(2)]
                    for c in range(NSUB):
                        qi = qin[c % 2]
                        qik = ('qin', c % 2)
                        q_ = qr[c % 2]
                        qrk = ('qr', c % 2)
                        ld(qi, u_tok[c * 128:(c + 1) * 128, c_in:c_in + w_in_cols].rearrange("p (h e) -> p h e", e=64),
                           [('u_tok', c)], [qik])
                        if kind == 'gqa':
                            vt = vtmp[c % 2]
                            vk = ('vtmp', c % 2)
                            ld(vt, u_tok[c * 128:(c + 1) * 128, TGV:TGV + 128], [('u_tok', c)], [vk])
                            pool(lambda vt=vt, c=c: G.tensor_copy(out=vaug[:, c, :, 0:64], in_=vt.rearrange("p (g e) -> p g e", g=2)),
                                 [vk, 'vaug'], ['vaug'])
                        act(lambda qi=qi: S_.activation(out=sqf, in_=qi, func=AF.Square), [qik], ['sqf'])
                        dve(lambda: V.tensor_reduce(out=ssn[:, 0, :], in_=sqf, axis=AX.X, op=ALU.add), ['sqf'], ['ssn'])
                        act(lambda: S_.activation(out=ssn[:, 1, :], in_=ssn[:, 0, :], func=AF.Sqrt, scale=1.0 / 64, bias=EPS),
                            ['ssn'], ['ssn'])
                        dve(lambda: V.reciprocal(out=ssn[:, 1, :], in_=ssn[:, 1, :]), ['ssn'], ['ssn'])
                        dve(lambda qi=qi: V.tensor_tensor(out=qn, in0=qi, in1=ssn[:, 1, :, None].to_broadcast([128, NH, 64]),
                                                          op=ALU.mult), [qik, 'ssn'], ['qn'])
                        if c >= NSC:
                            cl = c - NSC
                            pool(lambda: G.tensor_tensor(out=qn, in0=qn, in1=gq_, op=ALU.mult), ['qn', gk_n], ['qn'])
                            csb = rc[:, cl, None, :].to_broadcast([128, NH, 32])
                            snb = rs[:, cl, None, :].to_broadcast([128, NH, 32])
                            dve(lambda csb=csb: V.tensor_tensor(out=ta, in0=qn[:, :, 0:32], in1=csb, op=ALU.mult), ['qn', 'rc'], ['ta'])
                            pool(lambda snb=snb: G.tensor_tensor(out=tb, in0=qn[:, :, 32:64], in1=snb, op=ALU.mult), ['qn', 'rs'], ['tb'])
                            pool(lambda snb=snb: G.tensor_tensor(out=tc_, in0=qn[:, :, 0:32], in1=snb, op=ALU.mult), ['qn', 'rs'], ['tc'])
                            dve(lambda csb=csb: V.tensor_tensor(out=td, in0=qn[:, :, 32:64], in1=csb, op=ALU.mult), ['qn', 'rc'], ['td'])
                            dve(lambda q_=q_: V.tensor_tensor(out=q_[:, 0:NH, 0:32], in0=ta, in1=tb, op=ALU.subtract),
                                ['ta', 'tb'], [qrk])
                            pool(lambda q_=q_: G.tensor_tensor(out=q_[:, 0:NH, 32:64], in0=tc_, in1=td, op=ALU.add),
                                 ['tc', 'td', qrk], [qrk])
                        else:
                            pool(lambda q_=q_: G.tensor_tensor(out=q_[:, 0:NH, :], in0=qn, in1=gq_, op=ALU.mult), ['qn', gk_n], [qrk])
                        bt = nb(0, 6)
                        if kind == 'gqa':
                            pool(lambda q_=q_: G.tensor_copy(out=q_[:, 10:12, :], in_=q_[:, 9:10, :].to_broadcast([128, 2, 64])),
                                 [qrk], [qrk])
                            pool(lambda q_=q_: G.tensor_copy(out=q_[:, 9:10, :], in_=q_[:, 8:9, :]), [qrk], [qrk])
                        for k in range(NT):
                            pe(lambda k=k, q_=q_, bt=bt: T.transpose(out=PBH[bt][:, k * 128:(k + 1) * 128],
                                                                     in_=q_[:, 2 * k:2 * k + 2, :], identity=ident),
                               [qrk, 'ident'], [('ps', bt)])
                        act(lambda bt=bt, c=c: S_.copy(out=qkT[:, :, c * 128:(c + 1) * 128],
                                                       in_=PBH[bt][:, 0:NT * 128].rearrange("p (k t) -> p k t", k=NT)),
                            [('ps', bt)], [('qkT', c)])
                    QKT = [('qkT', c) for c in range(NSUB)]
                    P.barrier()
                    chk('P3' + (kind if 'P3' in ('P3', 'P4') else ''))
                    MA = A.mark()
                    qblocks = ([] if last else [(0, NCTX, NSC)]) + [(NCTX + 512 * i, 512, NSUB) for i in range(NLAT // 512)]
                    pT = [A.alloc([512], BF16) for _ in range(8)]
                    oT = [A.alloc([4, 512], BF16) for _ in range(2)]
                    rsb = [A.alloc([512], F32) for _ in range(2)]
                    if kind == 'diff':
                        o1 = A.alloc([512], F32)
                        o2 = A.alloc([512], F32)
                        sqd = A.alloc([512], BF16)
                    ptc = [0]
                    for qi_, (q0, QW, nkt) in enumerate(qblocks):
                        o_ = oT[qi_ % 2]
                        ok_ = ('oT', qi_ % 2)
                        qs = slice(q0, q0 + QW)
                        if kind == 'gqa':
                            for hh in range(8):
                                hf, j, g = hh % 2, hh // 2, hh // 4
                                pp = slice(hf * 64, hf * 64 + 64)
                                bo = 4 + hh % 2
                                pq = []
                                for kt in range(nkt + 2):
                                    if kt < nkt:
                                        bs = kt % 3
                                        ks = slice(kt * 128, (kt + 1) * 128)
                                        mm(PB[bs][:, 0:QW], qkT[pp, 4 + g, ks], qkT[pp, j, qs], True, True, QKT, [('ps', bs)])
                                        p_ = pT[ptc[0] % 8]
                                        pk = ('pT', ptc[0] % 8)
                                        ptc[0] += 1
                                        act(lambda p_=p_, bs=bs: S_.activation(out=p_[:, 0:QW], in_=PB[bs][:, 0:QW], func=AF.Exp,
                                                                               scale=0.125), [('ps', bs)], [pk])
                                        pq.append((kt, p_, pk))
                                    if kt >= 2:
                                        kt2, p2, pk2 = pq.pop(0)
                                        mm(PB[bo][:, 0:QW], vaug[:, kt2, g, :], p2[:, 0:QW], kt2 == 0, kt2 == nkt - 1,
                                           ['vaug', pk2], [('ps', bo)])
                                r_ = rsb[hh % 2]
                                rk_ = ('rsb', hh % 2)
                                dve(lambda r_=r_, bo=bo: V.reciprocal(out=r_[0:64, 0:QW], in_=PB[bo][64:128, 0:QW]), [('ps', bo)], [rk_])
                                dve(lambda r_=r_, bo=bo, o_=o_, pp=pp, j=j: V.tensor_tensor(
                                    out=o_[pp, j, 0:QW], in0=PB[bo][0:64, 0:QW], in1=r_[0:64, 0:QW], op=ALU.mult),
                                    [('ps', bo), rk_, ok_], [ok_])
                            stq(brd[1][:, qs].rearrange("(k p) t -> p k t", p=128), o_[:, :, 0:QW], [ok_],
                                [('br1', i) for i in range(q0 // 128, (q0 + QW) // 128)])
                        else:
                            for hh in range(4):
                                bo = [4, 5]
                                bsu = [6, 7]
                                pq = [[], []]
                                for kt in range(nkt + 2):
                                    if kt >= 2:
                                        for cc in range(2):
                                            kt2, p2, pk2 = pq[cc].pop(0)
                                            mm(PB[bo[cc]][:, 0:QW], vd[:, kt2, hh * 128:(hh + 1) * 128], p2[:, 0:QW],
                                               kt2 == 0, kt2 == nkt - 1, ['vd', pk2], [('ps', bo[cc])])
                                            mm(PB[bsu[cc]][:, 0:QW], ones_bf, p2[:, 0:QW], kt2 == 0, kt2 == nkt - 1,
                                               ['ones_bf', pk2], [('ps', bsu[cc])])
                                    if kt < nkt:
                                        ks = slice(kt * 128, (kt + 1) * 128)
                                        for cc in range(2):
                                            pp = slice(cc * 64, cc * 64 + 64)
                                            bs = cc * 2 + kt % 2
                                            mm(PB[bs][:, 0:QW], qkT[pp, 4 + hh, ks], qkT[pp, hh, qs], True, True, QKT, [('ps', bs)])
                                        for cc in range(2):
                                            bs = cc * 2 + kt % 2
                                            p_ = pT[ptc[0] % 8]
                                            pk = ('pT', ptc[0] % 8)
                                            ptc[0] += 1
                                            act(lambda p_=p_, bs=bs: S_.activation(out=p_[:, 0:QW], in_=PB[bs][:, 0:QW],
                                                                                   func=AF.Exp, scale=0.125), [('ps', bs)], [pk])
                                            pq[cc].append((kt, p_, pk))
                                for cc, ox in ((0, o1), (1, o2)):
                                    r_ = rsb[cc]
                                    rk_ = ('rsb', cc)
                                    dve(lambda r_=r_, cc=cc: V.reciprocal(out=r_[:, 0:QW], in_=PB[bsu[cc]][:, 0:QW]),
                                        [('ps', bsu[cc])], [rk_])
                                    dve(lambda r_=r_, cc=cc, ox=ox: V.tensor_tensor(out=ox[:, 0:QW], in0=PB[bo[cc]][:, 0:QW],
                                                                                   in1=r_[:, 0:QW], op=ALU.mult),
                                        [('ps', bo[cc]), rk_], [('o12', cc)])
                                dve(lambda: V.scalar_tensor_tensor(out=o1[:, 0:QW], in0=o2[:, 0:QW], scalar=lamt[:, 0:1],
                                                                   in1=o1[:, 0:QW], op0=ALU.mult, op1=ALU.add),
                                    [('o12', 0), ('o12', 1), 'lamt'], [('o12', 0)])
                                act(lambda: S_.activation(out=sqd[:, 0:QW], in_=o1[:, 0:QW], func=AF.Square), [('o12', 0)], ['sqd'])
                                b3 = 3
                                mm(PB[b3][:, 0:QW], ones_bf, sqd[:, 0:QW], True, True, ['ones_bf', 'sqd'], [('ps', b3)])
                                act(lambda: S_.activation(out=o2[:, 0:QW], in_=PB[b3][:, 0:QW], func=AF.Sqrt, scale=1.0 / 128,
                                                          bias=EPS), [('ps', b3), ('o12', 1)], [('o12', 1)])
                                dve(lambda: V.reciprocal(out=o2[:, 0:QW], in_=o2[:, 0:QW]), [('o12', 1)], [('o12', 1)])
                                dve(lambda hh=hh, o_=o_: V.scalar_tensor_tensor(out=o_[:, hh, 0:QW], in0=o1[:, 0:QW],
                                                                                scalar=lamt[:, 2:3], in1=o2[:, 0:QW],
                                                                                op0=ALU.mult, op1=ALU.mult),
                                    [('o12', 0), ('o12', 1), 'lamt', ok_], [ok_])
                            stq(brd[2][:, qs].rearrange("(k p) t -> p k t", p=128), o_[:, :, 0:QW], [ok_],
                                [('br2', i) for i in range(q0 // 128, (q0 + QW) // 128)])
                    P.barrier()
                    chk('P4' + (kind if 'P4' in ('P3', 'P4') else ''))

                A.release(PERSIST)
                rgw32 = A.alloc([16, 128], F32)
                rgwb = A.alloc([16, 128], BF16)
                ld(rgw32, rgwd[l], [], ['rgw32'])
                pool(lambda: G.tensor_copy(out=rgwb, in_=rgw32), ['rgw32'], ['rgwb'])
                rawx = A.alloc([NTOK + 6], BF16)
                pool(lambda: G.memset(rawx, 0.0), [], ['rawx'])
                xr = A.alloc([NTOK], F32)
                xrb = A.alloc([NTOK], BF16)
                a_all = A.alloc([NTOK], F32)
                b_all = A.alloc([NTOK], F32)
                hsum = A.alloc([NTOK], F32)
                hb = A.alloc([NTOK], F32)
                rgg = A.alloc([NTOK], BF16)
                rgo = A.alloc([NTOK], BF16)
                tg1 = A.alloc([512], F32)
                tg2 = A.alloc([512], F32)
                tg3 = A.alloc([512], F32)
                for j in range(4):
                    for (g0, gl, c0) in SEG:
                        ld(rawx[:, c0:c0 + gl], u_fm[(12 + j) * 128:(13 + j) * 128, g0:g0 + gl], [('u_fm', 3)], ['rawx'])
                    ld(rgg, u_fm[(8 + j) * 128:(9 + j) * 128, :], [('u_fm', 2)], ['rgg'])
                    for (g0, gl, c0) in SEG:
                        for k in range(4):
                            src = rawx[:, c0 + k - 1:c0 + k - 1 + gl]
                            wk_ = pf[:, PF_RCW + j * 4 + k:PF_RCW + j * 4 + k + 1]
                            if k == 0:
                                dve(lambda src=src, wk_=wk_, g0=g0, gl=gl, j=j: V.tensor_scalar(
                                    out=xr[:, g0:g0 + gl], in0=src, scalar1=wk_, scalar2=pf[:, PF_RCB + j:PF_RCB + j + 1],
                                    op0=ALU.mult, op1=ALU.add), ['rawx', 'pf'], ['xr'])
                            else:
                                dve(lambda src=src, wk_=wk_, g0=g0, gl=gl: V.scalar_tensor_tensor(
                                    out=xr[:, g0:g0 + gl], in0=src, scalar=wk_, in1=xr[:, g0:g0 + gl],
                                    op0=ALU.mult, op1=ALU.add), ['rawx', 'pf', 'xr'], ['xr'])
                    act(lambda: S_.copy(out=xrb, in_=xr), ['xr'], ['xrb'])
                    for d in range(2):
                        for (t0, W) in tiles_of():
                            ts_ = slice(t0, t0 + W)
                            ba_, bx_ = nb(0, 4), nb(4, 8)
                            mm(PB[ba_][:, 0:W], rgwb[:, (0 * 2 + d) * 4 + j, :], xrb[:, ts_], True, True, ['rgwb', 'xrb'], [('ps', ba_)])
                            mm(PB[bx_][:, 0:W], rgwb[:, (1 * 2 + d) * 4 + j, :], xrb[:, ts_], True, True, ['rgwb', 'xrb'], [('ps', bx_)])
                            cba = pf[:, PF_RBA + d * 4 + j:PF_RBA + d * 4 + j + 1]
                            cbx = pf[:, PF_RBX + d * 4 + j:PF_RBX + d * 4 + j + 1]
                            ccp = rgcp[:, d * 4 + j:d * 4 + j + 1]
                            act(lambda W=W, ba_=ba_, cba=cba: S_.activation(out=tg1[:, 0:W], in_=PB[ba_][:, 0:W], func=AF.Sigmoid,
                                                                            bias=cba), [('ps', ba_), 'pf'], ['tg1'])
                            act(lambda W=W, ts_=ts_, ccp=ccp: S_.activation(out=a_all[:, ts_], in_=tg1[:, 0:W], func=AF.Exp,
                                                                            scale=ccp), ['tg1', 'rgcp'], ['a_all'])
                            act(lambda W=W, bx_=bx_, cbx=cbx: S_.activation(out=tg2[:, 0:W], in_=PB[bx_][:, 0:W], func=AF.Sigmoid,
                                                                            bias=cbx), [('ps', bx_), 'pf'], ['tg2'])
                            dve(lambda W=W, ts_=ts_: V.tensor_tensor(out=tg2[:, 0:W], in0=tg2[:, 0:W], in1=xr[:, ts_], op=ALU.mult),
                                ['tg2', 'xr'], ['tg2'])
                            pool(lambda W=W, ts_=ts_: G.tensor_tensor(out=tg3[:, 0:W], in0=a_all[:, ts_], in1=a_all[:, ts_],
                                                                      op=ALU.mult), ['a_all', 'tg3'], ['tg3'])
                            act(lambda W=W: S_.activation(out=tg3[:, 0:W], in_=tg3[:, 0:W], func=AF.Sqrt, scale=-1.0, bias=1.0),
                                ['tg3'], ['tg3'])
                            dve(lambda W=W, ts_=ts_: V.tensor_tensor(out=b_all[:, ts_], in0=tg2[:, 0:W], in1=tg3[:, 0:W], op=ALU.mult),
                                ['tg2', 'tg3'], ['b_all'])
                        if d == 0:
                            dve(lambda: V.tensor_tensor_scan(out=hsum, data0=a_all, data1=b_all, initial=0.0, op0=ALU.mult,
                                                             op1=ALU.add), ['a_all', 'b_all'], ['hsum'])
                        else:
                            dve(lambda: V.tensor_tensor_scan(out=hb[:, 0:NCTX][:, ::-1], data0=a_all[:, 0:NCTX][:, ::-1],
                                                             data1=b_all[:, 0:NCTX][:, ::-1], initial=0.0, op0=ALU.mult,
                                                             op1=ALU.add), ['a_all', 'b_all'], ['hb'])
                            dve(lambda: V.tensor_tensor_scan(out=hb[:, NCTX:NTOK][:, ::-1], data0=a_all[:, NCTX:NTOK][:, ::-1],
                                                             data1=b_all[:, NCTX:NTOK][:, ::-1], initial=hb[:, 0:1],
                                                             op0=ALU.mult, op1=ALU.add), ['a_all', 'b_all', 'hb'], ['hb'])
                            pool(lambda: G.tensor_tensor(out=hsum, in0=hsum, in1=hb, op=ALU.add), ['hsum', 'hb'], ['hsum'])
                    pool(lambda: G.tensor_tensor(out=a_all, in0=rgg, in1=rgg, op=ALU.mult), ['rgg', 'a_all'], ['a_all'])
                    dve(lambda: V.tensor_scalar(out=a_all, in0=a_all, scalar1=0.044715, scalar2=1.0, op0=ALU.mult, op1=ALU.add),
                        ['a_all'], ['a_all'])
                    dve(lambda: V.tensor_tensor(out=a_all, in0=a_all, in1=rgg, op=ALU.mult), ['a_all', 'rgg'], ['a_all'])
                    act(lambda: S_.activation(out=b_all, in_=a_all, func=AF.Sigmoid, scale=1.5957691216057308),
                        ['a_all', 'b_all'], ['b_all'])
                    pool(lambda: G.tensor_tensor(out=b_all, in0=b_all, in1=rgg, op=ALU.mult), ['b_all', 'rgg'], ['b_all'])
                    dve(lambda: V.tensor_tensor(out=rgo, in0=b_all, in1=hsum, op=ALU.mult), ['b_all', 'hsum'], ['rgo'])
                    stq(brd[3][j * 128:(j + 1) * 128, :], rgo, ['rgo'], [('br3', j)])
                P.barrier()
                chk('P6' + (kind if 'P6' in ('P3', 'P4') else ''))

                A.release(PERSIST)
                wgt_ = A.alloc([4, 8, D], BF16)
                wbr_ = A.alloc([4, 4, D], BF16)
                wo_ = A.alloc([8, D], BF16)
                M7 = A.mark()
                stg7 = [A.alloc([4 * D], F32) for _ in range(2)]
                dsts, srcs = [], []
                for k in range(4):
                    for hf in range(2):
                        dsts.append(wgt_[:, k, hf * 4:(hf + 1) * 4, :])
                        srcs.append(w_gate[l, k, hf * 512:(hf + 1) * 512, :].rearrange("(c p) n -> p c n", p=128))
                for k in range(4):
                    dsts.append(wbr_[:, k, :, :])
                    srcs.append(w_br[l, k].rearrange("(c p) n -> p c n", p=128))
                for hf in range(2):
                    dsts.append(wo_[:, hf * 4:(hf + 1) * 4, :])
                    srcs.append(w_out[l, hf * 512:(hf + 1) * 512, :].rearrange("(c p) n -> p c n", p=128))
                load_weight(dsts, srcs, stg7, 'w7')
                W7 = [('w7', i) for i in range(len(dsts))]
                P.barrier()
                chk('P7w' + (kind if 'P7w' in ('P3', 'P4') else ''))
                A.release(M7)
                xt7 = [A.alloc([8, 256], F32) for _ in range(2)]
                sq = A.alloc([8, 256], BF16)
                rstd = A.alloc([256], F32)
                tmpn[0] = A.alloc([256], F32)
                tmpn[1] = A.alloc([256], F32)
                h7 = A.alloc([8, 256], BF16)
                ob = [A.alloc([4, 256], BF16) for _ in range(2)]
                macc = A.alloc([8, 256], F32)
                mbf = A.alloc([8, 256], BF16)
                sg2 = [A.alloc([256], F32) for _ in range(2)]
                tm2 = [A.alloc([256], F32) for _ in range(2)]
                cnt7 = [0]
                for ti, (t0, W) in enumerate(tiles_of(not last, 256)):
                    r = NB if t0 < NCTX else s
                    xt = xt7[ti % 2]
                    xk = ('xt', ti % 2)
                    ts_ = slice(t0, t0 + W)
                    ld(xt[:, :, 0:W], xsrc[:, ts_].rearrange("(c p) t -> p c t", p=128), [('xs', s)], [xk])
                    norm_mod(xt, xk, W, sq, rstd, h7, 'h7', A1, B1, r, '')
                    for k in range(4):
                        o_ = ob[k % 2]
                        obk = ('ob', k % 2)
                        ld(o_[:, :, 0:W], brd[k][:, ts_].rearrange("(c p) t -> p c t", p=128),
                           [(f'br{k}', i) for i in range(NSUB if k != 3 else 4)], [obk])
                        for oc in range(8):
                            ocs = slice(oc * 128, (oc + 1) * 128)
                            bg_, bb_ = nb(0, 4), nb(4, 8)
                            for kc in range(8):
                                mm(PB[bg_][:, 0:W], wgt_[:, k, kc, ocs], h7[:, kc, 0:W], kc == 0, kc == 7, W7 + ['h7'], [('ps', bg_)])
                            for kc in range(4):
                                mm(PB[bb_][:, 0:W], wbr_[:, k, kc, ocs], o_[:, kc, 0:W], kc == 0, kc == 3, W7 + [obk], [('ps', bb_)])
                            cnt7[0] += 1
                            sg = sg2[cnt7[0] % 2]
                            sgk = ('sg', cnt7[0] % 2)
                            act(lambda sg=sg, bg_=bg_, k=k, oc=oc: S_.activation(
                                out=sg[:, 0:W], in_=PB[bg_][:, 0:W], func=AF.Sigmoid,
                                bias=pf[:, PF_BG + k * 8 + oc:PF_BG + k * 8 + oc + 1]), [('ps', bg_), 'pf'], [sgk])
                            if k == 0:
                                dve(lambda sg=sg, bb_=bb_, oc=oc: V.tensor_tensor(out=macc[:, oc, 0:W], in0=PB[bb_][:, 0:W],
                                                                                  in1=sg[:, 0:W], op=ALU.mult),
                                    [('ps', bb_), sgk], [('macc', oc)])
                            else:
                                tm = tm2[cnt7[0] % 2]
                                tmk = ('tm', cnt7[0] % 2)
                                dve(lambda sg=sg, bb_=bb_, tm=tm: V.tensor_tensor(out=tm[:, 0:W], in0=PB[bb_][:, 0:W],
                                                                                  in1=sg[:, 0:W], op=ALU.mult),
                                    [('ps', bb_), sgk], [tmk])
                                pool(lambda tm=tm, oc=oc: G.tensor_tensor(out=macc[:, oc, 0:W], in0=macc[:, oc, 0:W],
                                                                          in1=tm[:, 0:W], op=ALU.add), [tmk, ('macc', oc)],
                                     [('macc', oc)])
                    act(lambda: S_.copy(out=mbf[:, :, 0:W], in_=macc[:, :, 0:W]), [('macc', oc) for oc in range(8)], ['mbf'])
                    for oc in range(8):
                        ocs = slice(oc * 128, (oc + 1) * 128)
                        b = nb(0, 8)
                        for kc in range(8):
                            mm(PB[b][:, 0:W], wo_[:, kc, ocs], mbf[:, kc, 0:W], kc == 0, kc == 7, W7 + ['mbf'], [('ps', b)])
                        dve(lambda oc=oc, b=b, xt=xt: V.scalar_tensor_tensor(out=xt[:, oc, 0:W], in0=PB[b][:, 0:W],
                                                                             scalar=G1[:, oc, r:r + 1], in1=xt[:, oc, 0:W],
                                                                             op0=ALU.mult, op1=ALU.add),
                            [('ps', b), 'mods', xk], [xk])
                    stq(xm[:, ts_].rearrange("(c p) t -> p c t", p=128), xt[:, :, 0:W], [xk], ['xm'])
                P.barrier()
                chk('P7' + (kind if 'P7' in ('P3', 'P4') else ''))

                A.release(PERSIST)
                wup = A.alloc([8, 2 * DFF], BF16)
                wdn = A.alloc([22, D], BF16)
                M8 = A.mark()
                stg8 = [A.alloc([4 * D], F32) for _ in range(2)]
                dsts, srcs = [], []
                for kc in range(8):
                    for q in range(2):
                        dsts.append(wup[:, kc, q * DFF:(q + 1) * DFF])
                        srcs.append(w_up[l, kc * 128:(kc + 1) * 128, q * DFF:(q + 1) * DFF])
                for q in range(11):
                    dsts.append(wdn[:, 2 * q:2 * q + 2, :])
                    srcs.append(w_down[l, q * 256:(q + 1) * 256, :].rearrange("(c p) n -> p c n", p=128))
                load_weight(dsts, srcs, stg8, 'w8')
                W8 = [('w8', i) for i in range(len(dsts))]
                P.barrier()
                chk('P8w' + (kind if 'P8w' in ('P3', 'P4') else ''))
                A.release(M8)
                xt8 = A.alloc([8, 256], F32)
                sq = A.alloc([8, 256], BF16)
                rstd = A.alloc([256], F32)
                tmpn[0] = A.alloc([256], F32)
                tmpn[1] = A.alloc([256], F32)
                h8 = A.alloc([8, 256], BF16)
                actb = A.alloc([22, 256], BF16)
                cg2 = [A.alloc([256], F32) for _ in range(2)]
                cv2 = [A.alloc([256], F32) for _ in range(2)]
                dve(lambda: V.memset(xt8, 1.0), [], ['xt8'])
                ftiles = []
                for (sg0, sg1) in ([] if last else [(0, NCTX)]) + [(NCTX, NTOK)]:
                    ln = sg1 - sg0
                    nlt = max(1, -(-ln // 254))
                    base = ln // nlt
                    o0 = sg0
                    for i in range(nlt):
                        wo = base + (1 if i < ln - base * nlt else 0)
                        ftiles.append((o0, wo, sg0, sg1))
                        o0 += wo
                dst_x = outd[s] if last else xs[s]
                for (o0, Wo, sg0, sg1) in ftiles:
                    Ww = Wo + 2
                    lo = max(o0 - 1, sg0)
                    hi = min(o0 + Wo + 1, sg1)
                    cl = lo - (o0 - 1)
                    ld(xt8[:, :, cl:cl + hi - lo], xm[:, lo:hi].rearrange("(c p) t -> p c t", p=128), ['xm'], ['xt8'])
                    r = NB if o0 < NCTX else s
                    norm_mod(xt8, 'xt8', Ww, sq, rstd, h8, 'h8', A2, B2, r, '')
                    if cl > 0:
                        pool(lambda: G.memset(h8[:, :, 0:1], 0.0), ['h8'], ['h8'])
                    if hi < o0 + Wo + 1:
                        pool(lambda Ww=Ww: G.memset(h8[:, :, Ww - 1:Ww], 0.0), ['h8'], ['h8'])
                    for i in range(22):
                        res = []
                        for half, buf in ((0, cg2), (1, cv2)):
                            ch = half * 22 + i
                            c0 = ch * 128
                            b = nb(0, 8)
                            for kc in range(8):
                                mm(PB[b][:, 0:Ww], wup[:, kc, c0:c0 + 128], h8[:, kc, 0:Ww], kc == 0, kc == 7, W8 + ['h8'], [('ps', b)])
                            cb_ = buf[i % 2]
                            cbk = ('c%d' % half, i % 2)
                            fw = lambda k, ch=ch: pf[:, PF_FCW + ch * 3 + k:PF_FCW + ch * 3 + k + 1]
                            act(lambda cb_=cb_, b=b, ch=ch, fw=fw: S_.activation(
                                out=cb_[:, 0:Wo], in_=PB[b][:, 1:1 + Wo], func=AF.Identity, scale=fw(1),
                                bias=pf[:, PF_FCB + ch:PF_FCB + ch + 1]), [('ps', b), 'pf'], [cbk])
                            dve(lambda cb_=cb_, b=b, fw=fw: V.scalar_tensor_tensor(out=cb_[:, 0:Wo], in0=PB[b][:, 0:Wo], scalar=fw(0),
                                                                                   in1=cb_[:, 0:Wo], op0=ALU.mult, op1=ALU.add),
                                [('ps', b), 'pf', cbk], [cbk])
                            dve(lambda cb_=cb_, b=b, fw=fw: V.scalar_tensor_tensor(out=cb_[:, 0:Wo], in0=PB[b][:, 2:2 + Wo],
                                                                                   scalar=fw(2), in1=cb_[:, 0:Wo], op0=ALU.mult,
                                                                                   op1=ALU.add), [('ps', b), 'pf', cbk], [cbk])
                            res.append((cb_, cbk))
                        (cgb, cgk), (cvb, cvk) = res
                        act(lambda cgb=cgb: S_.activation(out=cgb[:, 0:Wo], in_=cgb[:, 0:Wo], func=AF.Silu), [cgk], [cgk])
                        pool(lambda cgb=cgb, cvb=cvb, i=i: G.tensor_tensor(out=actb[:, i, 0:Wo], in0=cgb[:, 0:Wo], in1=cvb[:, 0:Wo],
                                                                           op=ALU.mult), [cgk, cvk], [('actb', i)])
                    AK = [('actb', i) for i in range(22)]
                    for oc in range(8):
                        ocs = slice(oc * 128, (oc + 1) * 128)
                        b = nb(0, 8)
                        for kc in range(22):
                            mm(PB[b][:, 0:Wo], wdn[:, kc, ocs], actb[:, kc, 0:Wo], kc == 0, kc == 21, W8 + AK, [('ps', b)])
                        dve(lambda oc=oc, b=b: V.scalar_tensor_tensor(out=xt8[:, oc, 1:1 + Wo], in0=PB[b][:, 0:Wo],
                                                                      scalar=G2[:, oc, r:r + 1], in1=xt8[:, oc, 1:1 + Wo],
                                                                      op0=ALU.mult, op1=ALU.add), [('ps', b), 'mods', 'xt8'], ['xt8'])
                    od0 = o0 - NCTX if last else o0
                    stq(dst_x[:, od0:od0 + Wo].rearrange("(c p) t -> p c t", p=128), xt8[:, :, 1:1 + Wo], ['xt8'],
                        [('xs', s), 'out'])
                P.barrier()
                chk('P8' + (kind if 'P8' in ('P3', 'P4') else ''))
        try:
            for l_ in range(DEPTH):
                emit_layer(l_)
        except StopBuild:
            P.barrier()
        P.op('sp', lambda: SY.nop(), reads=['out'])
        info = P.emit(lambda name: st.enter_context(nc.semaphore(name)))
    return nc, info


def _consts(NLAT):
    m = np.arange(128)[:, None]
    l_ = np.arange(128)[None, :]
    cm = np.stack([(m <= l_), (m < l_), (m >= l_), (m > l_), np.ones((128, 128), bool), (m == l_)], 1).astype(np.float32)
    t = np.arange(NLAT)
    row = (t // 64).astype(np.float32)
    col = (t % 64).astype(np.float32)
    inv = (10000.0 ** (-np.arange(16, dtype=np.float32) / 16)).astype(np.float32)
    ang = np.concatenate([row[:, None] * inv, col[:, None] * inv], -1).astype(np.float32)
    cs = np.cos(ang).astype(np.float32).reshape(NLAT // 128, 128, 32).transpose(1, 0, 2)
    sn = np.sin(ang).astype(np.float32).reshape(NLAT // 128, 128, 32).transpose(1, 0, 2)
    return np.ascontiguousarray(cm), np.ascontiguousarray(cs), np.ascontiguousarray(sn)


def _fm(v, n):
    return np.ascontiguousarray(np.asarray(v, np.float32).reshape(n, 128).T)


def _pack_params(inp, DEPTH):
    pf = np.zeros((DEPTH, 128, NPF), np.float32)
    pb = np.zeros((DEPTH, NPB), np.float32)
    rgw = np.zeros((DEPTH, 128, 16, 128), np.float32)
    for l in range(DEPTH):
        pf[l, :, PF_BADA:PF_BADA + 48] = _fm(inp['b_ada'][l], 48)
        pf[l, :, PF_N1:PF_N1 + 8] = _fm(inp['norm1_g'][l], 8)
        pf[l, :, PF_N2:PF_N2 + 8] = _fm(inp['norm2_g'][l], 8)
        scw = np.asarray(inp['ssd_conv_w'][l])
        pf[l, :, PF_SCW:PF_SCW + 32] = scw.reshape(4, 8, 128).transpose(2, 1, 0).reshape(128, 32)
        pf[l, :, PF_SCB:PF_SCB + 8] = _fm(inp['ssd_conv_b'][l], 8)
        pf[l, :, PF_SUB] = np.asarray(inp['diff_subln_g'][l])
        rcw = np.asarray(inp['rg_conv_w'][l])
        pf[l, :, PF_RCW:PF_RCW + 16] = rcw.reshape(4, 4, 128).transpose(2, 1, 0).reshape(128, 16)
        pf[l, :, PF_RCB:PF_RCB + 4] = _fm(inp['rg_conv_b'][l], 4)
        for nm, off in (('rg_ba', PF_RBA), ('rg_bx', PF_RBX), ('rg_lambda', PF_RLM)):
            v = np.asarray(inp[nm][l])
            pf[l, :, off:off + 8] = v.reshape(2, 4, 128).transpose(2, 0, 1).reshape(128, 8)
        bg = np.asarray(inp['b_gate'][l])
        pf[l, :, PF_BG:PF_BG + 32] = bg.reshape(4, 8, 128).transpose(2, 0, 1).reshape(128, 32)
        fcw = np.asarray(inp['ffn_conv_w'][l])
        pf[l, :, PF_FCW:PF_FCW + 132] = fcw.reshape(3, 44, 128).transpose(2, 1, 0).reshape(128, 132)
        pf[l, :, PF_FCB:PF_FCB + 44] = _fm(inp['ffn_conv_b'][l], 44)
        pb[l, PB_DTB:PB_DTB + 16] = np.asarray(inp['ssd_dt_bias'][l]).reshape(16)
        pb[l, PB_ALOG:PB_ALOG + 16] = np.asarray(inp['ssd_a_log'][l]).reshape(16)
        pb[l, PB_D:PB_D + 8] = np.asarray(inp['ssd_d'][l])
        pb[l, PB_SNG:PB_SNG + 512] = np.asarray(inp['ssd_norm_g'][l])
        pb[l, PB_GQ:PB_GQ + 64] = np.asarray(inp['gqa_qnorm_g'][l])
        pb[l, PB_GK:PB_GK + 64] = np.asarray(inp['gqa_knorm_g'][l])
        pb[l, PB_DQ:PB_DQ + 64] = np.asarray(inp['diff_qnorm_g'][l])
        pb[l, PB_DK:PB_DK + 64] = np.asarray(inp['diff_knorm_g'][l])
        pb[l, PB_LAM:PB_LAM + 256] = np.asarray(inp['diff_lambda'][l]).reshape(256)
        pb[l, PB_LI] = 0.8 - 0.6 * math.exp(-0.3 * l)
        for ax, nm in enumerate(('rg_wa', 'rg_wx')):
            w = np.asarray(inp[nm][l])
            for d in range(2):
                for j in range(4):
                    for q in range(2):
                        rgw[l, q * 64:(q + 1) * 64, (ax * 2 + d) * 4 + j, q * 64:(q + 1) * 64] = w[d, 2 * j + q]
    return pf, pb, rgw


_CACHE = {}
STOP = None


def run(inputs, NB, DEPTH, NCTX, NLAT, n_cores, debug=False):
    x = np.asarray(inputs['x'], np.float32)
    ctx = np.asarray(inputs['ctx'], np.float32)
    c = np.asarray(inputs['c'], np.float32)
    c_ctx = np.asarray(inputs['c_ctx'], np.float32)
    key = (NB, DEPTH, NCTX, NLAT, debug)
    if key not in _CACHE:
        _CACHE[key] = build_program(NB, DEPTH, NCTX, NLAT, debug, stop=STOP)
    nc, info = _CACHE[key]
    cm, cs, sn = _consts(NLAT)
    pf, pb, rgw = _pack_params(inputs, DEPTH)
    wnames = ['w_ada', 'w_in', 'w_gate', 'w_br', 'w_out', 'w_up', 'w_down']
    shared = {k: np.ascontiguousarray(np.asarray(inputs[k], np.float32)) for k in wnames}
    shared.update(dict(ropec=cs, ropes=sn, cmask=cm, pf=pf, pb=pb, rgw=rgw))
    in_maps = []
    for core in range(n_cores):
        bs = range(core * NB, (core + 1) * NB)
        xin = np.stack([np.concatenate([ctx[b].T, x[b].T], axis=1) for b in bs], 0)
        cc = np.stack([c[b] for b in bs] + [c_ctx], 0)
        cTm = np.ascontiguousarray(cc.reshape(NB + 1, 8, 128).transpose(2, 1, 0))
        m = dict(shared)
        m['xin'] = np.ascontiguousarray(xin)
        m['cT'] = cTm
        in_maps.append(m)
    res = run_bass_kernel_spmd(nc, in_maps, core_ids=list(range(n_cores)))
    outs = []
    for core in range(n_cores):
        o = res.results[core]['out']
        for i in range(NB):
            outs.append(np.ascontiguousarray(o[i].T))
    return np.stack(outs, 0).astype(np.float32), res


def kernel(**inputs):
    out, _ = run(inputs, NB=2, DEPTH=2, NCTX=256, NLAT=4096, n_cores=8)
    return out
```
